# Optimizing a Trainium2 kernel written in Bass

```python
import jax
import jax.numpy as jnp
from jax import lax
import numpy as np

D_MODEL = 1024
BATCH = 2
SEQ = 8192
DEPTH = 4
DEC_BATCH = 32
DEC_SEQ = 4
PAST_LEN = 8192
PAGE_SIZE = 128

D_LRU = D_MODEL // 4
N_LRU_BLOCKS = 4
LRU_BLOCK = D_LRU // N_LRU_BLOCKS
CONV_W = 4
LRU_C = 8.0
HGRN_DK = 64
HGRN_DV = 64
D_HGRN = D_MODEL // 4
N_HGRN = D_HGRN // HGRN_DK
HGRN_CHUNK = 64
ATT_DH = 64
D_ATT = D_MODEL // 2
N_ATT = D_ATT // ATT_DH
DILATED = ((128, 1), (512, 4), (2048, 16))
MAX_WINDOW = 2048
Q_BLOCK = 128
N_MEM = 256
N_XHEADS = 4
XHEAD_DIM = D_MODEL // N_XHEADS
D_FF = 2816
GROUP_COLS = (D_LRU, D_LRU, D_HGRN, D_HGRN, D_HGRN, D_HGRN, D_ATT, D_ATT, D_ATT)
D_IN = 2 * D_LRU + 4 * D_HGRN + 3 * D_ATT
EPS = 1e-6
F32 = jnp.float32

kernel_name = 'hymba_lru_hgrn2_dilated_swa_decoder_step'


def rmsnorm(x, g):
    xf = x.astype(F32)
    y = xf * lax.rsqrt(jnp.mean(xf * xf, axis=-1, keepdims=True) + EPS)
    return (y * g.astype(F32)).astype(x.dtype)


def swiglu(x, w_gate, w_up, w_down):
    return (jax.nn.silu(x @ w_gate) * (x @ w_up)) @ w_down


def causal_conv(x, buf, w, b):
    T = x.shape[1]
    xp = jnp.concatenate([buf.astype(x.dtype), x], axis=1)
    y = b.astype(x.dtype)
    for tap in range(CONV_W):
        y = y + w[tap].astype(x.dtype) * xp[:, tap:tap + T]
    return y, xp[:, -(CONV_W - 1):]


def rg_lru(x, h0, w_a, b_a, w_x, b_x, lam):
    B, T, _ = x.shape
    xf = x.astype(F32)
    xb = xf.reshape(B, T, N_LRU_BLOCKS, LRU_BLOCK)
    r = jax.nn.sigmoid(jnp.einsum('btnc,ncd->btnd', xb, w_a.astype(F32)).reshape(B, T, D_LRU) + b_a.astype(F32))
    i = jax.nn.sigmoid(jnp.einsum('btnc,ncd->btnd', xb, w_x.astype(F32)).reshape(B, T, D_LRU) + b_x.astype(F32))
    log_a = -LRU_C * r * jax.nn.softplus(-lam.astype(F32))
    a = jnp.exp(log_a)
    u = jnp.sqrt(-jnp.expm1(2.0 * log_a)) * (i * xf)

    def combine(lhs, rhs):
        a1, b1 = lhs
        a2, b2 = rhs
        return a1 * a2, a2 * b1 + b2

    a_cum, u_cum = lax.associative_scan(combine, (a, u), axis=1)
    h = a_cum * h0.astype(F32)[:, None] + u_cum
    return h.astype(x.dtype), h[:, -1].astype(h0.dtype)


def hgrn2(q, f_pre, v, g, s0, lb, norm_g):
    B, T, _ = q.shape
    H, DK, DV = N_HGRN, HGRN_DK, HGRN_DV
    lb = lb.astype(F32).reshape(H, DK)
    f = lb + (1.0 - lb) * jax.nn.sigmoid(f_pre.astype(F32).reshape(B, T, H, DK))
    log_f = jnp.log(f)
    k = 1.0 - f
    qf = q.astype(F32).reshape(B, T, H, DK)
    vf = v.astype(F32).reshape(B, T, H, DV)
    C = HGRN_CHUNK if T % HGRN_CHUNK == 0 else T
    N = T // C

    def to_chunks(z):
        return z.reshape(B, N, C, H, z.shape[-1]).transpose(1, 0, 3, 2, 4)

    causal = jnp.tril(jnp.ones((C, C), dtype=bool))

    def chunk_step(S, inp):
        qc, kc, vc, gc = inp
        cum = jnp.cumsum(gc, axis=2)
        rel = jnp.where(causal[None, None, :, :, None],
                        cum[:, :, :, None, :] - cum[:, :, None, :, :], -jnp.inf)
        scores = jnp.einsum('bhtd,bhsd,bhtsd->bhts', qc, kc, jnp.exp(rel))
        o = (jnp.einsum('bhts,bhsv->bhtv', scores, vc)
             + jnp.einsum('bhtd,bhdv->bhtv', qc * jnp.exp(cum), S))
        last = cum[:, :, -1]
        S = (jnp.exp(last)[..., None] * S
             + jnp.einsum('bhsd,bhsv->bhdv', kc * jnp.exp(last[:, :, None] - cum), vc))
        return S, o

    s_fin, o = lax.scan(chunk_step, s0.astype(F32),
                        (to_chunks(qf), to_chunks(k), to_chunks(vf), to_chunks(log_f)))
    o = o.transpose(1, 0, 3, 2, 4).reshape(B, T, H, DV)
    o = rmsnorm(o, norm_g).reshape(B, T, H * DV) * jax.nn.silu(g.astype(F32))
    return o.astype(q.dtype), s_fin.astype(s0.dtype)


def dilated_attention(q, k_all, v_all, q_idx):
    slopes = jnp.asarray(2.0 ** (-8.0 * np.arange(1, N_ATT + 1) / N_ATT), dtype=F32)
    scale = ATT_DH ** -0.5
    lses = []
    outs = []
    for window, dil in DILATED:
        n_keys = window // dil + 1
        offs = jnp.arange(n_keys, dtype=jnp.int32) * dil
        idx = q_idx[:, None] - offs[None, :]
        valid = idx >= 0
        idx = jnp.maximum(idx, 0)
        kg = k_all[:, idx]
        vg = v_all[:, idx]
        s = (jnp.einsum('bthd,btnhd->bthn', q, kg).astype(F32) * scale
             - slopes[:, None] * offs.astype(F32)[None, :])
        s = jnp.where(valid[None, :, None, :], s, -jnp.inf)
        m = jnp.max(s, axis=-1, keepdims=True)
        p = jnp.exp(s - m)
        den = jnp.sum(p, axis=-1)
        o = jnp.einsum('bthn,btnhd->bthd', p, vg.astype(F32)) / den[..., None]
        lses.append(m[..., 0] + jnp.log(den))
        outs.append(o)
    w = jax.nn.softmax(jnp.stack(lses, axis=-1), axis=-1)
    return jnp.einsum('bthg,gbthd->bthd', w, jnp.stack(outs, axis=0))


def dilated_attention_blocked(q, k, v):
    B, T, H, D = q.shape
    nb = T // Q_BLOCK
    qb = q.reshape(B, nb, Q_BLOCK, H, D).transpose(1, 0, 2, 3, 4)
    idx = jnp.arange(T, dtype=jnp.int32).reshape(nb, Q_BLOCK)
    ob = lax.map(lambda a: dilated_attention(a[0], k, v, a[1]), (qb, idx))
    return ob.transpose(1, 0, 2, 3, 4).reshape(B, T, H, D)


def cross_attend(h, mem_k, mem_v, wq, wo):
    B, T, _ = h.shape
    q = (h @ wq).reshape(B, T, N_XHEADS, XHEAD_DIM)
    s = jnp.einsum('bthd,bmhd->bhtm', q, mem_k).astype(F32) * (XHEAD_DIM ** -0.5)
    p = jax.nn.softmax(s, axis=-1)
    o = jnp.einsum('bhtm,bmhd->bthd', p.astype(h.dtype), mem_v)
    return o.reshape(B, T, D_MODEL) @ wo


def trunk_layer(x, lp, lru_h, lru_conv, hgrn_s, swa_k_buf, swa_v_buf, mem_k, mem_v):
    B, T, _ = x.shape
    x = x + 0.5 * swiglu(rmsnorm(x, lp['n_ffn1']), lp['ffn1_wg'], lp['ffn1_wu'], lp['ffn1_wd'])
    h = rmsnorm(x, lp['n_mix'])
    z = h @ lp['w_in']
    xa, ga, qb, fb, ib, gb, qc, kc, vc = jnp.split(z, np.cumsum(GROUP_COLS)[:-1].tolist(), axis=-1)
    xa, conv_new = causal_conv(xa, lru_conv, lp['lru_conv_w'], lp['lru_conv_b'])
    ya, h_last = rg_lru(xa, lru_h, lp['lru_wa'], lp['lru_ba'], lp['lru_wx'], lp['lru_bx'], lp['lru_lambda'])
    ya = rmsnorm(ya * jax.nn.gelu(ga), lp['gn_a'])
    yb, s_new = hgrn2(qb, fb, ib, gb, hgrn_s, lp['lb'], lp['hgrn_norm'])
    q = qc.reshape(B, T, N_ATT, ATT_DH)
    k = kc.reshape(B, T, N_ATT, ATT_DH)
    v = vc.reshape(B, T, N_ATT, ATT_DH)
    if swa_k_buf is None:
        yc = dilated_attention_blocked(q, k, v)
        keep = min(MAX_WINDOW, T)
        k_new = k[:, T - keep:]
        v_new = v[:, T - keep:]
    else:
        W = swa_k_buf.shape[1]
        k_all = jnp.concatenate([swa_k_buf.astype(k.dtype), k], axis=1)
        v_all = jnp.concatenate([swa_v_buf.astype(v.dtype), v], axis=1)
        yc = dilated_attention(q, k_all, v_all, W + jnp.arange(T, dtype=jnp.int32))
        k_new = k_all[:, T:]
        v_new = v_all[:, T:]
    yc = rmsnorm(yc.reshape(B, T, D_ATT).astype(x.dtype), lp['gn_c'])
    x = x + jnp.concatenate([ya, yb, yc], axis=-1) @ lp['w_out']
    x = x + cross_attend(rmsnorm(x, lp['n_cross']), mem_k, mem_v, lp['x_wq'], lp['x_wo'])
    x = x + 0.5 * swiglu(rmsnorm(x, lp['n_ffn2']), lp['ffn2_wg'], lp['ffn2_wu'], lp['ffn2_wd'])
    return x, h_last, conv_new, s_new, k_new, v_new


def setup_inputs(seed: int = 0) -> dict:
    key = jax.random.key(seed)
    keys = list(jax.random.split(key, 64))

    def nrm(shape, scale):
        return scale * jax.random.normal(keys.pop(), shape, jnp.float32)

    def gain(shape):
        return 1.0 + 0.02 * jax.random.normal(keys.pop(), shape, jnp.float32)

    w_buf = min(MAX_WINDOW, PAST_LEN)
    u = jax.random.uniform(keys.pop(), (DEPTH, D_LRU), jnp.float32, minval=0.9, maxval=0.999)
    sig = u ** (1.0 / LRU_C)
    lru_lambda = jnp.log(sig) - jnp.log1p(-sig)
    d_in_s = D_MODEL ** -0.5
    return {
        'x_prompt': nrm((BATCH, SEQ, D_MODEL), 1.0),
        'x_sample': nrm((DEC_BATCH, DEC_SEQ, D_MODEL), 1.0),
        'state_lru_h': nrm((DEPTH, DEC_BATCH, D_LRU), 0.5),
        'state_lru_conv': nrm((DEPTH, DEC_BATCH, CONV_W - 1, D_LRU), 1.0),
        'state_hgrn': nrm((DEPTH, DEC_BATCH, N_HGRN, HGRN_DK, HGRN_DV), 0.5),
        'cache_swa_k': nrm((DEPTH, DEC_BATCH, w_buf, N_ATT, ATT_DH), 1.0),
        'cache_swa_v': nrm((DEPTH, DEC_BATCH, w_buf, N_ATT, ATT_DH), 1.0),
        'cache_mem_k': nrm((DEPTH, DEC_BATCH, N_MEM, N_XHEADS, XHEAD_DIM), 1.0),
        'cache_mem_v': nrm((DEPTH, DEC_BATCH, N_MEM, N_XHEADS, XHEAD_DIM), 1.0),
        'mem_prompt': nrm((BATCH, N_MEM, D_MODEL), 1.0),
        'n_ffn1': gain((DEPTH, D_MODEL)),
        'ffn1_wg': nrm((DEPTH, D_MODEL, D_FF), d_in_s),
        'ffn1_wu': nrm((DEPTH, D_MODEL, D_FF), d_in_s),
        'ffn1_wd': nrm((DEPTH, D_FF, D_MODEL), D_FF ** -0.5),
        'n_mix': gain((DEPTH, D_MODEL)),
        'w_in': nrm((DEPTH, D_MODEL, D_IN), d_in_s),
        'lru_conv_w': nrm((DEPTH, CONV_W, D_LRU), CONV_W ** -0.5),
        'lru_conv_b': nrm((DEPTH, D_LRU), 0.01),
        'lru_wa': nrm((DEPTH, N_LRU_BLOCKS, LRU_BLOCK, LRU_BLOCK), LRU_BLOCK ** -0.5),
        'lru_ba': nrm((DEPTH, D_LRU), 0.01),
        'lru_wx': nrm((DEPTH, N_LRU_BLOCKS, LRU_BLOCK, LRU_BLOCK), LRU_BLOCK ** -0.5),
        'lru_bx': nrm((DEPTH, D_LRU), 0.01),
        'lru_lambda': lru_lambda,
        'hgrn_lb': nrm((DEPTH, D_HGRN), 0.1),
        'hgrn_norm': gain((DEPTH, HGRN_DV)),
        'gn_a': gain((DEPTH, D_LRU)),
        'gn_c': gain((DEPTH, D_ATT)),
        'w_out': nrm((DEPTH, D_MODEL, D_MODEL), d_in_s),
        'n_cross': gain((DEPTH, D_MODEL)),
        'x_wq': nrm((DEPTH, D_MODEL, D_MODEL), d_in_s),
        'x_wk': nrm((DEPTH, D_MODEL, D_MODEL), d_in_s),
        'x_wv': nrm((DEPTH, D_MODEL, D_MODEL), d_in_s),
        'x_wo': nrm((DEPTH, D_MODEL, D_MODEL), d_in_s),
        'n_ffn2': gain((DEPTH, D_MODEL)),
        'ffn2_wg': nrm((DEPTH, D_MODEL, D_FF), d_in_s),
        'ffn2_wu': nrm((DEPTH, D_MODEL, D_FF), d_in_s),
        'ffn2_wd': nrm((DEPTH, D_FF, D_MODEL), D_FF ** -0.5),
        'n_final': gain((D_MODEL,)),
    }


def reference(x_prompt, x_sample, state_lru_h, state_lru_conv, state_hgrn, cache_swa_k, cache_swa_v,
              cache_mem_k, cache_mem_v, mem_prompt, n_ffn1, ffn1_wg, ffn1_wu, ffn1_wd, n_mix, w_in,
              lru_conv_w, lru_conv_b, lru_wa, lru_ba, lru_wx, lru_bx, lru_lambda, hgrn_lb, hgrn_norm,
              gn_a, gn_c, w_out, n_cross, x_wq, x_wk, x_wv, x_wo, n_ffn2, ffn2_wg, ffn2_wu, ffn2_wd,
              n_final):
    lb_all = lax.cumsum(jax.nn.softmax(hgrn_lb.astype(F32), axis=0), axis=0)
    lb_all = lb_all - lb_all[0]
    B = x_prompt.shape[0]
    dt = x_prompt.dtype
    xp = x_prompt
    xs = x_sample
    p_h, p_c, p_s, p_k, p_v, p_mk, p_mv = [], [], [], [], [], [], []
    s_h, s_c, s_s, s_k, s_v = [], [], [], [], []
    for l in range(DEPTH):
        lp = {
            'n_ffn1': n_ffn1[l], 'ffn1_wg': ffn1_wg[l], 'ffn1_wu': ffn1_wu[l], 'ffn1_wd': ffn1_wd[l],
            'n_mix': n_mix[l], 'w_in': w_in[l],
            'lru_conv_w': lru_conv_w[l], 'lru_conv_b': lru_conv_b[l],
            'lru_wa': lru_wa[l], 'lru_ba': lru_ba[l], 'lru_wx': lru_wx[l], 'lru_bx': lru_bx[l],
            'lru_lambda': lru_lambda[l], 'lb': lb_all[l], 'hgrn_norm': hgrn_norm[l],
            'gn_a': gn_a[l], 'gn_c': gn_c[l], 'w_out': w_out[l],
            'n_cross': n_cross[l], 'x_wq': x_wq[l], 'x_wo': x_wo[l],
            'n_ffn2': n_ffn2[l], 'ffn2_wg': ffn2_wg[l], 'ffn2_wu': ffn2_wu[l], 'ffn2_wd': ffn2_wd[l],
        }
        mk = (mem_prompt @ x_wk[l]).reshape(B, N_MEM, N_XHEADS, XHEAD_DIM)
        mv = (mem_prompt @ x_wv[l]).reshape(B, N_MEM, N_XHEADS, XHEAD_DIM)
        xp, ph, pc, ps, pk, pv = trunk_layer(
            xp, lp, jnp.zeros((B, D_LRU), dt), jnp.zeros((B, CONV_W - 1, D_LRU), dt),
            jnp.zeros((B, N_HGRN, HGRN_DK, HGRN_DV), dt), None, None, mk, mv)
        xs, sh, sc, ss, sk, sv = trunk_layer(
            xs, lp, state_lru_h[l], state_lru_conv[l], state_hgrn[l],
            cache_swa_k[l], cache_swa_v[l], cache_mem_k[l], cache_mem_v[l])
        p_h.append(ph); p_c.append(pc); p_s.append(ps); p_k.append(pk); p_v.append(pv)
        p_mk.append(mk); p_mv.append(mv)
        s_h.append(sh); s_c.append(sc); s_s.append(ss); s_k.append(sk); s_v.append(sv)
    y_prompt = rmsnorm(xp, n_final)
    y_sample = rmsnorm(xs, n_final)
    return (y_prompt, y_sample,
            jnp.stack(p_h), jnp.stack(p_c), jnp.stack(p_s), jnp.stack(p_k), jnp.stack(p_v),
            jnp.stack(p_mk), jnp.stack(p_mv),
            jnp.stack(s_h), jnp.stack(s_c), jnp.stack(s_s), jnp.stack(s_k), jnp.stack(s_v))
```

```python
from contextlib import ExitStack
import numpy as np
import ml_dtypes
import concourse.bass as bass
import concourse.mybir as mybir
from concourse.bass_utils import run_bass_kernel_spmd

F32 = mybir.dt.float32
BF16 = mybir.dt.bfloat16
AF = mybir.ActivationFunctionType
ALU = mybir.AluOpType

ENGS = ["pe", "act", "dve", "pool", "sp"]
SEM_ROLL = 30000
NDMASEM = 6

D = 1024
DFF = 2816
DIN = 3072
NMEM = 256
DS = 4
W = 2048
DIL = ((128, 1), (512, 4), (2048, 16))
EPS = 1e-6
GSZ = 4
PL = 64


class Sched:
    def __init__(self, nc, stack):
        self.nc = nc
        self.stack = stack
        self.q = {e: [] for e in ENGS}
        self.cur_sem = {}
        self.cur_cnt = {}
        self.nsem = 0
        for e in ENGS:
            self._new_sem(e)
        self.dsem = {}
        self.dcnt = {}
        self.dnext = {}
        for e in ["sp", "pool", "act"]:
            self.dsem[e] = [self._alloc_sem("d%s%d" % (e, k)) for k in range(NDMASEM)]
            self.dcnt[e] = [0] * NDMASEM
            self.dnext[e] = 0
        self.seen = {e: {} for e in ENGS}
        self.res = {}
        self.n_wait = 0
        self.n_ins = 0

    def _alloc_sem(self, name):
        self.nsem += 1
        return self.stack.enter_context(self.nc.semaphore(name))

    def _new_sem(self, e):
        self.cur_sem[e] = self._alloc_sem("s%s%d" % (e, self.nsem))
        self.cur_cnt[e] = 0

    def _need(self, eng, ev):
        if ev is None:
            return
        sem, val = ev
        k = id(sem)
        if self.seen[eng].get(k, 0) >= val:
            return
        self.seen[eng][k] = val
        self.n_wait += 1
        self.q[eng].append(lambda E, sem=sem, val=val: E.wait_ge(sem, val))

    def _deps(self, eng, reads, writes, acc=False):
        for r in reads:
            st = self.res.get(r)
            if st is not None:
                self._need(eng, st[0])
        for w in writes:
            st = self.res.get(w)
            if st is not None:
                if not (acc and st[2] == eng):
                    self._need(eng, st[0])
                for ev in st[1]:
                    self._need(eng, ev)

    def _record(self, eng, ev, reads, writes):
        for r in reads:
            st = self.res.get(r)
            if st is None:
                st = [None, [], None]
                self.res[r] = st
            st[1].append(ev)
            if len(st[1]) > 10:
                d = {}
                for s, v in st[1]:
                    if id(s) not in d or d[id(s)][1] < v:
                        d[id(s)] = (s, v)
                st[1] = list(d.values())
        for w in writes:
            self.res[w] = [ev, [], eng]

    def op(self, eng, fn, reads=(), writes=(), acc=False):
        self._deps(eng, reads, writes, acc=acc)
        if self.cur_cnt[eng] >= SEM_ROLL:
            self._new_sem(eng)
        sem = self.cur_sem[eng]
        self.cur_cnt[eng] += 1
        val = self.cur_cnt[eng]
        self.n_ins += 1
        self.q[eng].append(lambda E, fn=fn, sem=sem: fn(E).then_inc(sem, 1))
        ev = (sem, val)
        self._record(eng, ev, reads, writes)
        return ev

    def dma(self, eng, out, in_, reads=(), writes=(), **kw):
        k = self.dnext[eng]
        self.dnext[eng] = (k + 1) % NDMASEM
        sem = self.dsem[eng][k]
        if self.dcnt[eng][k] > 0:
            self._need(eng, (sem, self.dcnt[eng][k]))
        self._deps(eng, reads, writes)
        self.dcnt[eng][k] += 16
        val = self.dcnt[eng][k]
        self.n_ins += 1
        self.q[eng].append(
            lambda E, out=out, in_=in_, sem=sem, kw=kw: E.dma_start(out=out, in_=in_, **kw).then_inc(sem, 16))
        ev = (sem, val)
        self._record(eng, ev, reads, writes)
        return ev

    def finish(self, final_events):
        for ev in final_events:
            self._need("sp", ev)
        nc = self.nc
        with nc.Block() as block:
            @block.sync
            def _(E):
                for f in self.q["sp"]:
                    f(E)

            @block.tensor
            def _(E):
                for f in self.q["pe"]:
                    f(E)

            @block.scalar
            def _(E):
                for f in self.q["act"]:
                    f(E)

            @block.vector
            def _(E):
                for f in self.q["dve"]:
                    f(E)

            @block.gpsimd
            def _(E):
                for f in self.q["pool"]:
                    f(E)


class Ring:
    def __init__(self, items):
        self.items = items
        self.i = 0

    def get(self):
        it = self.items[self.i]
        self.i = (self.i + 1) % len(self.items)
        return it


def wt_table():
    slopes = 2.0 ** (-8.0 * np.arange(1, 9) / 8.0)
    db = np.arange(-3, 20)[None, :, None]
    ik = np.arange(128)[:, None, None]
    iq = np.arange(128)[None, None, :]
    delta = 128 * db + iq - ik
    mult = np.zeros(delta.shape, np.float64)
    for win, dil in DIL:
        mult += ((delta >= 0) & (delta <= win) & (delta % dil == 0))
    dpos = np.maximum(delta, 0).astype(np.float64)
    tab = np.stack([mult * np.exp(-s * dpos) for s in slopes], 0)
    return tab.astype(np.float32)


def build(cfg):
    SEQ = cfg["SEQ"]
    TC = cfg["TC"]
    DEPTH = cfg["DEPTH"]
    NSB = cfg["NSB"]
    NPASS = SEQ // TC
    NBLK = TC // 128
    WB = W // 128
    NST = NSB * DS
    NTX = TC + NST
    KEEP = min(W, SEQ)
    NPT = DEPTH * PL + 8
    assert TC % 512 == 0 and SEQ % TC == 0

    nc = bass.Bass("TRN2", target_bir_lowering=False)

    def din(name, shape, dt=F32):
        return nc.dram_tensor(name, list(shape), dt, kind="ExternalInput").ap()

    def dout(name, shape, dt=F32):
        return nc.dram_tensor(name, list(shape), dt, kind="ExternalOutput").ap()

    def dint(name, shape, dt=F32):
        return nc.dram_tensor(name, list(shape), dt, kind=cfg.get("scr_kind", "Internal")).ap()

    xp = din("xp", [SEQ, D])
    xs = din("xs", [NST, D])
    st_h = din("st_h", [DEPTH, NSB, 128, 2])
    st_conv = din("st_conv", [DEPTH, NSB, 128, 2, 3])
    st_hg = din("st_hg", [DEPTH, NSB, 64, 4, 64])
    c_k = din("c_k", [DEPTH, NSB, W, 512])
    c_v = din("c_v", [DEPTH, NSB, W, 512])
    cm_k = din("cm_k", [DEPTH, NSB, NMEM, D])
    cm_v = din("cm_v", [DEPTH, NSB, NMEM, D])
    memp = din("memp", [NMEM, D])
    w = {}
    for nm, shp in [("ffn1_wg", [DEPTH, D, DFF]), ("ffn1_wu", [DEPTH, D, DFF]), ("ffn1_wd", [DEPTH, DFF, D]),
                    ("ffn2_wg", [DEPTH, D, DFF]), ("ffn2_wu", [DEPTH, D, DFF]), ("ffn2_wd", [DEPTH, DFF, D]),
                    ("w_in", [DEPTH, D, DIN]), ("w_out", [DEPTH, D, D]), ("x_wq", [DEPTH, D, D]),
                    ("x_wk", [DEPTH, D, D]), ("x_wv", [DEPTH, D, D]), ("x_wo", [DEPTH, D, D]),
                    ("lru_wbd", [DEPTH, 2, 2, 128, 128])]:
        w[nm] = din(nm, shp)
    ptab_d = din("ptab", [128, NPT])
    wt_d = din("wt_tab", [8, 128, 23 * 128])
    ident_d = din("ident", [128, 128])
    triu_d = din("triu", [64, 512])
    rmask_d = din("rmask", [64, NTX])

    yp = dout("yp", [SEQ, D])
    ys = dout("ys", [NST, D])
    o_h = dout("o_h", [DEPTH, 128, 2])
    o_conv = dout("o_conv", [DEPTH, 128, 2, 3])
    o_hg = dout("o_hg", [DEPTH, 64, 4, 64])
    o_k = dout("o_k", [DEPTH, KEEP, 512])
    o_v = dout("o_v", [DEPTH, KEEP, 512])
    o_mk = dout("o_mk", [DEPTH, NMEM, D])
    o_mv = dout("o_mv", [DEPTH, NMEM, D])
    s_h = dout("s_h", [DEPTH, NSB, 128, 2])
    s_conv = dout("s_conv", [DEPTH, NSB, 128, 2, 3])
    s_hg = dout("s_hg", [DEPTH, NSB, 64, 4, 64])
    s_k = dout("s_k", [DEPTH, NSB, W, 512])
    s_v = dout("s_v", [DEPTH, NSB, W, 512])

    kT_scr = dint("kT_scr", [DEPTH, 512, SEQ], BF16)
    v_scr = dint("v_scr", [DEPTH, 8, SEQ, 64], BF16)
    mkT_scr = dint("mkT_scr", [DEPTH, 128, 8 * NMEM], BF16)
    mv_scr = dint("mv_scr", [DEPTH, 128, 2 * D], BF16)

    st = ExitStack()
    S = Sched(nc, st)
    finals = []

    def sb(name, shape, dt=F32):
        return st.enter_context(nc.sbuf_tensor(name, list(shape), dt))

    def ps(name, shape, dt=F32):
        return st.enter_context(nc.psum_tensor(name, list(shape), dt))

    x = sb("x", [128, 8, NTX])
    h = sb("h", [128, 8, NTX], BF16)
    ybuf = sb("ybuf", [128, 8 * NTX], BF16)
    NWB = 4
    wbufs = [sb("wb%d" % i, [128, 4096], BF16) for i in range(NWB)]
    wring = Ring(list(range(NWB)))
    actb = [sb("actb%d" % i, [128, GSZ, 512], BF16) for i in range(2)]
    actring = Ring([0, 1])
    FS = [sb("fs%d" % i, [128, NTX + 8]) for i in range(10)]
    SW = max(NTX + 8, 64 * (TC // 64 + NSB))
    BS = [sb("bs%d" % i, [128, SW], BF16) for i in range(5)]
    t512 = [sb("t512_%d" % i, [128, 512]) for i in range(3)]
    tring = Ring([0, 1, 2])
    b512 = [sb("b512_%d" % i, [128, 512], BF16) for i in range(4)]
    bring = Ring([0, 1, 2, 3])
    vh = sb("vh", [128, WB + NBLK, 64], BF16)
    kT = sb("kT", [64, W + TC], BF16)
    wtb = sb("wtb", [128, 23 * 128], BF16)
    memT = wtb[:, 0:8 * NMEM].rearrange("p (c n) -> p c n", c=8)
    ptab = sb("ptab_sb", [128, NPT])
    dpar = sb("dpar", [128, DEPTH, 16])
    ident_f = sb("ident_f", [128, 128])
    ident_b = sb("ident_b", [128, 128], BF16)
    ones_b = sb("ones_b", [128, 128], BF16)
    triu = sb("triu_sb", [64, 512])
    rmask = sb("rmask_sb", [64, NTX])
    lru_h = sb("lru_h", [128, DEPTH, 2])
    lru_tail = sb("lru_tail", [128, DEPTH, 2, 3])
    hgS = sb("hgS", [64, DEPTH, 4, 64])
    hgSb = sb("hgSb", [64, 17, 64], BF16)
    smp_h = sb("smp_h", [128, NSB, 2])
    smp_tail = sb("smp_tail", [128, NSB, 2, 3])
    smp_S = sb("smp_S", [64, NSB, 4, 64])
    mkT_flat = actb[0][:].rearrange("p a b -> p (a b)")
    mv_flat = actb[1][:].rearrange("p a b -> p (a b)")
    mkT = mkT_flat.rearrange("p (c n) -> p c n", c=8)
    mv = mv_flat.rearrange("p (c n) -> p c n", c=2)
    KMK, KMV, KMT = "actb0", "actb1", "wtb"
    xin = [sb("xin%d" % i, [128, D]) for i in range(1)]
    xinring = Ring([0])

    def sbq(mi, n):
        return BS[1 + mi // 2][:, (mi % 2) * 512:(mi % 2) * 512 + n]

    def sbqk(mi):
        return "bs%d" % (1 + mi // 2)

    psm = [ps("psm%d" % i, [128, 512]) for i in range(4)]
    pring = Ring([0, 1, 2, 3])
    psacc = [ps("psacc%d" % i, [128, 512]) for i in range(2)]
    pst = [ps("pst%d" % i, [128, 1024], BF16) for i in range(2)]
    ptring = Ring([0, 1])

    def P(i):
        return "psm%d" % i

    def act(out, in_, func, reads, writes, scale=1.0, bias=None, eng="act"):
        if bias is None:
            S.op(eng, lambda E: E.activation(out=out, in_=in_, func=func, scale=scale), reads, writes)
        else:
            S.op(eng, lambda E: E.activation(out=out, in_=in_, func=func, scale=scale, bias=bias), reads, writes)

    def tt(out, a, b, op, reads, writes, eng="dve"):
        S.op(eng, lambda E: E.tensor_tensor(out=out, in0=a, in1=b, op=op), reads, writes)

    def ts(out, a, s1, s2, op0, op1, reads, writes, eng="dve"):
        S.op(eng, lambda E: E.tensor_scalar(out=out, in0=a, scalar1=s1, scalar2=s2, op0=op0, op1=op1), reads, writes)

    def stt(out, a, s, b, op0, op1, reads, writes):
        S.op("dve", lambda E: E.scalar_tensor_tensor(out=out, in0=a, scalar=s, in1=b, op0=op0, op1=op1), reads, writes)

    def cp(out, in_, reads, writes, eng="dve"):
        if eng == "act":
            S.op(eng, lambda E: E.activation(out=out, in_=in_, func=AF.Copy), reads, writes)
        else:
            S.op(eng, lambda E: E.tensor_copy(out=out, in_=in_), reads, writes)

    def mm(out, lhsT, rhs, start, stop, reads, writes):
        S.op("pe", lambda E: E.matmul(out, lhsT=lhsT, rhs=rhs, start=start, stop=stop), reads, writes, acc=not start)

    def tr(out, in_, ident, reads, writes):
        S.op("pe", lambda E: E.transpose(out=out, in_=in_, identity=ident), reads, writes)

    def wload(src, kparts, kc, n):
        i = wring.get()
        assert kc * n <= 4096
        view = wbufs[i][0:kparts, 0:kc * n].rearrange("p (c n) -> p c n", c=kc)
        S.dma("pool", view, src.rearrange("(c p) n -> p c n", p=kparts), writes=["wb%d" % i])
        return view, "wb%d" % i

    def coltiles(p):
        t = [(c0, 512) for c0 in range(0, TC, 512)]
        if p == 0:
            t.append((TC, NST))
        return t

    S.dma("sp", ptab[:], ptab_d, writes=["ptab"])
    S.dma("sp", ident_f[:], ident_d, writes=["ident_f"])
    S.dma("pool", ident_b[:], ident_d, writes=["ident_b"])
    S.dma("sp", triu[:], triu_d, writes=["triu"])
    S.dma("sp", rmask[:], rmask_d, writes=["rmask"])
    S.op("dve", lambda E: E.memset(ones_b[:], 1.0), writes=["ones_b"])
    S.op("dve", lambda E: E.memset(lru_h[:], 0.0), writes=["lru_h"])
    S.op("dve", lambda E: E.memset(lru_tail[:], 0.0), writes=["lru_tail"])
    S.op("dve", lambda E: E.memset(hgS[:], 0.0), writes=["hgS"])
    S.op("dve", lambda E: E.memset(kT[:], 0.0), writes=["kT"])
    S.op("pool", lambda E: E.memset(vh[:], 0.0), writes=["vh"])

    def pcol(l, off, n=1):
        return ptab[:, l * PL + off: l * PL + off + n]

    for l in range(DEPTH):
        lam = pcol(l, 46, 2)
        e_ = FS[0][:, 0:2]
        z_ = FS[0][:, 2:4]
        z2 = FS[0][:, 4:6]
        pl_ = FS[0][:, 6:8]
        act(e_, lam, AF.Exp, ["ptab"], ["fs0"], scale=-1.0)
        ts(z_, e_, 2.0, None, ALU.add, ALU.bypass, ["fs0"], ["fs0"])
        S.op("dve", lambda E, z_=z_: E.reciprocal(out=z_, in_=z_), ["fs0"], ["fs0"])
        tt(z_, z_, e_, ALU.mult, ["fs0"], ["fs0"])
        tt(z2, z_, z_, ALU.mult, ["fs0"], ["fs0"])
        ts(pl_, z2, 1.0 / 7.0, 1.0 / 5.0, ALU.mult, ALU.add, ["fs0"], ["fs0"])
        tt(pl_, pl_, z2, ALU.mult, ["fs0"], ["fs0"])
        ts(pl_, pl_, 1.0 / 3.0, None, ALU.add, ALU.bypass, ["fs0"], ["fs0"])
        tt(pl_, pl_, z2, ALU.mult, ["fs0"], ["fs0"])
        ts(pl_, pl_, 1.0, None, ALU.add, ALU.bypass, ["fs0"], ["fs0"])
        tt(pl_, pl_, z_, ALU.mult, ["fs0"], ["fs0"])
        ts(dpar[:, l, 0:2], pl_, -16.0, None, ALU.mult, ALU.bypass, ["fs0"], ["dpar"])
        ts(dpar[:, l, 2:4], pl_, -32.0, None, ALU.mult, ALU.bypass, ["fs0"], ["dpar"])
    esum = FS[1][0:64, 0:4]
    ecum = FS[1][0:64, 4:8]
    for l in range(DEPTH):
        el = FS[1][0:64, 8 + 4 * l: 12 + 4 * l]
        act(el, ptab[0:64, l * PL + 59: l * PL + 63], AF.Exp, ["ptab"], ["fs1"])
        if l == 0:
            cp(esum, el, ["fs1"], ["fs1"])
        else:
            tt(esum, esum, el, ALU.add, ["fs1"], ["fs1"])
    S.op("dve", lambda E: E.reciprocal(out=esum, in_=esum), ["fs1"], ["fs1"])
    S.op("dve", lambda E: E.memset(ecum, 0.0), ["fs1"], ["fs1"])
    for l in range(DEPTH):
        el = FS[1][0:64, 8 + 4 * l: 12 + 4 * l]
        if l > 0:
            tt(ecum, ecum, el, ALU.add, ["fs1"], ["fs1"])
        tt(dpar[0:64, l, 4:8], ecum, esum, ALU.mult, ["fs1"], ["dpar"])
        ts(dpar[0:64, l, 8:12], dpar[0:64, l, 4:8], -1.0, 1.0, ALU.mult, ALU.add, ["dpar"], ["dpar"])

    def rmsnorm(l_off, p, out_fn, out_key, cts):
        okey = out_key if callable(out_key) else (lambda c: out_key)
        for (c0, n) in cts:
            pi = pring.get()
            for c in range(8):
                bi = bring.get()
                act(b512[bi][:, 0:n], x[:, c, c0:c0 + n], AF.Square, ["x"], ["b512_%d" % bi])
                mm(psm[pi][:, 0:n], ones_b[:], b512[bi][:, 0:n], c == 0, c == 7, ["ones_b", "b512_%d" % bi], [P(pi)])
            ti = tring.get()
            rs = t512[ti][:, 0:n]
            ts(rs, psm[pi][:, 0:n], 1.0 / D, EPS, ALU.mult, ALU.add, [P(pi)], ["t512_%d" % ti])
            act(rs, rs, AF.Sqrt, ["t512_%d" % ti], ["t512_%d" % ti])
            S.op("dve", lambda E, rs=rs: E.reciprocal(out=rs, in_=rs), ["t512_%d" % ti], ["t512_%d" % ti])
            for c in range(8):
                stt(out_fn(c, c0, n), x[:, c, c0:c0 + n], ptab[:, l_off + c: l_off + c + 1], rs,
                    ALU.mult, ALU.mult, ["x", "ptab", "t512_%d" % ti], [okey(c)])

    def hview(c, c0, n):
        return h[:, c, c0:c0 + n]

    def ffn(l, which, p):
        cts = coltiles(p)
        rmsnorm(l * PL + (0 if which == 1 else 24), p, hview, "h", cts)
        wg, wu, wd = w["ffn%d_wg" % which], w["ffn%d_wu" % which], w["ffn%d_wd" % which]
        nch = DFF // 128
        for g0 in range(0, nch, GSZ):
            gs = min(GSZ, nch - g0)
            wgv, wgk = wload(wg[l][:, g0 * 128:(g0 + gs) * 128], 128, 8, gs * 128)
            wuv, wuk = wload(wu[l][:, g0 * 128:(g0 + gs) * 128], 128, 8, gs * 128)
            wdv, wdk = wload(wd[l][g0 * 128:(g0 + gs) * 128, :], 128, gs, D)
            for (c0, n) in cts:
                ai = actring.get()
                ak = "actb%d" % ai
                for j in range(gs):
                    pg = pring.get()
                    for c in range(8):
                        mm(psm[pg][:, 0:n], wgv[:, c, j * 128:(j + 1) * 128], h[:, c, c0:c0 + n], c == 0, c == 7,
                           [wgk, "h"], [P(pg)])
                    pu = pring.get()
                    for c in range(8):
                        mm(psm[pu][:, 0:n], wuv[:, c, j * 128:(j + 1) * 128], h[:, c, c0:c0 + n], c == 0, c == 7,
                           [wuk, "h"], [P(pu)])
                    ti = tring.get()
                    act(t512[ti][:, 0:n], psm[pg][:, 0:n], AF.Silu, [P(pg)], ["t512_%d" % ti])
                    tt(actb[ai][:, j, 0:n], t512[ti][:, 0:n], psm[pu][:, 0:n], ALU.mult,
                       ["t512_%d" % ti, P(pu)], [ak])
                for m in range(8):
                    pd = pring.get()
                    for j in range(gs):
                        mm(psm[pd][:, 0:n], wdv[:, j, m * 128:(m + 1) * 128], actb[ai][:, j, 0:n], j == 0, j == gs - 1,
                           [wdk, ak], [P(pd)])
                    stt(x[:, m, c0:c0 + n], psm[pd][:, 0:n], 0.5, x[:, m, c0:c0 + n], ALU.mult, ALU.add,
                        [P(pd), "x"], ["x"])

    def proj_fm(wsrc, kparts, kc, mtot, msz, rhs_fn, rhs_keys, cts, sink):
        wpiece = 4096 // kc
        wpiece = (wpiece // msz) * msz
        for m0 in range(0, mtot, wpiece):
            mw = min(wpiece, mtot - m0)
            wv, wk = wload(wsrc[:, m0:m0 + mw], kparts, kc, mw)
            for mi in range(mw // msz):
                for (c0, n) in cts:
                    pi = pring.get()
                    for c in range(kc):
                        mm(psm[pi][0:msz, 0:n], wv[:, c, mi * msz:(mi + 1) * msz], rhs_fn(c, c0, n), c == 0, c == kc - 1,
                           [wk] + rhs_keys, [P(pi)])
                    sink(m0 // msz + mi, c0, n, psm[pi][0:msz, 0:n], P(pi))

    def resid_add(scale):
        def sink(mi, c0, n, pap, pk):
            if scale == 1.0:
                tt(x[:, mi, c0:c0 + n], pap, x[:, mi, c0:c0 + n], ALU.add, [pk, "x"], ["x"])
            else:
                stt(x[:, mi, c0:c0 + n], pap, scale, x[:, mi, c0:c0 + n], ALU.mult, ALU.add, [pk, "x"], ["x"])
        return sink

    def load_x(p):
        for b in range(NBLK):
            xi = xinring.get()
            S.dma("sp", xin[xi][:], xp[p * TC + b * 128: p * TC + (b + 1) * 128, :], writes=["xin%d" % xi])
            for c4 in range(2):
                pi = pring.get()
                for cc in range(4):
                    c = c4 * 4 + cc
                    tr(psm[pi][:, cc * 128:(cc + 1) * 128], xin[xi][:, c * 128:(c + 1) * 128], ident_f[:],
                       ["xin%d" % xi, "ident_f"], [P(pi)])
                cp(x[:, c4 * 4:(c4 + 1) * 4, b * 128:(b + 1) * 128],
                   psm[pi][:].rearrange("p (c n) -> p c n", c=4), [P(pi)], ["x"], eng="act" if c4 else "dve")
        if p == 0:
            xi = xinring.get()
            S.dma("sp", xin[xi][0:NST, :], xs, writes=["xin%d" % xi])
            for c4 in range(2):
                pi = pring.get()
                for cc in range(4):
                    c = c4 * 4 + cc
                    tr(psm[pi][:, cc * 128: cc * 128 + NST], xin[xi][0:NST, c * 128:(c + 1) * 128],
                       ident_f[0:NST, 0:NST], ["xin%d" % xi, "ident_f"], [P(pi)])
                cp(x[:, c4 * 4:(c4 + 1) * 4, TC:TC + NST],
                   psm[pi][:].rearrange("p (c n) -> p c n", c=4)[:, :, 0:NST], [P(pi)], ["x"])

    def store_y(p):
        cts = coltiles(p)
        yf = FS
        rmsnorm(DEPTH * PL, p, lambda c, c0, n: yf[c][:, c0:c0 + n], lambda c: "fs%d" % c, cts)
        nb = NBLK + (1 if p == 0 else 0)
        for b in range(nb):
            ntok = 128 if b < NBLK else NST
            xi = xinring.get()
            for c4 in range(2):
                pi = pring.get()
                for cc in range(4):
                    c = c4 * 4 + cc
                    tr(psm[pi][0:ntok, cc * 128:(cc + 1) * 128], yf[c][:, b * 128: b * 128 + ntok], ident_f[:],
                       ["fs%d" % c, "ident_f"], [P(pi)])
                cp(xin[xi][0:ntok, c4 * 512:(c4 + 1) * 512], psm[pi][0:ntok, :], [P(pi)], ["xin%d" % xi],
                   eng="act" if c4 else "dve")
            if b < NBLK:
                ev = S.dma("sp", yp[p * TC + b * 128: p * TC + (b + 1) * 128, :], xin[xi][:], reads=["xin%d" % xi])
            else:
                ev = S.dma("sp", ys, xin[xi][0:NST, :], reads=["xin%d" % xi])
            finals.append(ev)

    def mixer_a(l, p):
        cts = coltiles(p)
        win = w["w_in"][l]
        segs = [("p", 0, TC, None)]
        if p == 0:
            segs += [("s", TC + DS * b, DS, b) for b in range(NSB)]
        yA = ybuf[:, 0:2 * NTX].rearrange("p (c n) -> p c n", c=2)
        ypre = [FS[8], FS[9]]
        for pt in range(2):
            xaext, xc, r_, i_, a_, u_, hh, ga_, tmp = FS[0], FS[1], FS[2], FS[3], FS[4], FS[5], FS[6], FS[7], ypre[pt]
            xcb = BS[0]
            K = lambda i: "fs%d" % i
            wxa, wxak = wload(win[:, pt * 128:(pt + 1) * 128], 128, 8, 128)
            wga, wgak = wload(win[:, 256 + pt * 128: 256 + (pt + 1) * 128], 128, 8, 128)
            for (c0, n) in cts:
                pi = pring.get()
                for c in range(8):
                    mm(psm[pi][:, 0:n], wxa[:, c, :], h[:, c, c0:c0 + n], c == 0, c == 7, [wxak, "h"], [P(pi)])
                if c0 < TC:
                    cp(xaext[:, 3 + c0: 3 + c0 + n], psm[pi][:, 0:n], [P(pi)], [K(0)], eng="act")
                else:
                    cp(tmp[:, 0:n], psm[pi][:, 0:n], [P(pi)], [K(8 + pt)], eng="act")
                pj = pring.get()
                for c in range(8):
                    mm(psm[pj][:, 0:n], wga[:, c, :], h[:, c, c0:c0 + n], c == 0, c == 7, [wgak, "h"], [P(pj)])
                cp(ga_[:, c0:c0 + n], psm[pj][:, 0:n], [P(pj)], [K(7)], eng="act")
            cw = lambda tap: pcol(l, 32 + pt * 4 + tap)
            cb = pcol(l, 40 + pt)
            for (kind, c0, T, b) in segs:
                if kind == "p":
                    cp(xaext[:, 0:3], lru_tail[:, l, pt, :], ["lru_tail"], [K(0)])
                    src = xaext
                    so = 0
                else:
                    src = a_
                    so = 16 * b
                    cp(src[:, so:so + 3], smp_tail[:, b, pt, :], ["smp_tail"], [K(4)])
                    cp(src[:, so + 3:so + 3 + T], tmp[:, DS * b: DS * b + T], [K(8 + pt)], [K(4)])
                sk = K(0) if kind == "p" else K(4)
                ts(xc[:, c0:c0 + T], src[:, so:so + T], cw(0), cb, ALU.mult, ALU.add, [sk, "ptab"], [K(1)])
                for tap in range(1, 4):
                    stt(xc[:, c0:c0 + T], src[:, so + tap:so + tap + T], cw(tap), xc[:, c0:c0 + T], ALU.mult, ALU.add,
                        [sk, "ptab", K(1)], [K(1)])
                if kind == "p":
                    cp(lru_tail[:, l, pt, :], xaext[:, T:T + 3], [K(0)], ["lru_tail"])
                else:
                    cp(smp_tail[:, b, pt, :], src[:, so + T:so + T + 3], [K(4)], ["smp_tail"])
            ncols = TC + (NST if p == 0 else 0)
            cp(xcb[:, 0:ncols], xc[:, 0:ncols], [K(1)], ["bs0"], eng="act")
            wa, wak = wload(w["lru_wbd"][l, 0, pt], 128, 1, 128)
            wx, wxk = wload(w["lru_wbd"][l, 1, pt], 128, 1, 128)
            for (c0, n) in cts:
                pi = pring.get()
                mm(psm[pi][:, 0:n], wa[:, 0, :], xcb[:, c0:c0 + n], True, True, [wak, "bs0"], [P(pi)])
                act(r_[:, c0:c0 + n], psm[pi][:, 0:n], AF.Sigmoid, [P(pi), "ptab"], [K(2)], bias=pcol(l, 42 + pt))
                pj = pring.get()
                mm(psm[pj][:, 0:n], wx[:, 0, :], xcb[:, c0:c0 + n], True, True, [wxk, "bs0"], [P(pj)])
                act(i_[:, c0:c0 + n], psm[pj][:, 0:n], AF.Sigmoid, [P(pj), "ptab"], [K(3)], bias=pcol(l, 44 + pt))
            A = slice(0, ncols)
            act(a_[:, A], r_[:, A], AF.Exp, [K(2), "dpar"], [K(4)], scale=dpar[:, l, pt:pt + 1])
            y_ = u_
            ts(y_[:, A], r_[:, A], dpar[:, l, 2 + pt:3 + pt], None, ALU.mult, ALU.bypass, [K(2), "dpar"], [K(5)])
            pol = r_
            ts(pol[:, A], y_[:, A], 1.0 / 720.0, 1.0 / 120.0, ALU.mult, ALU.add, [K(5)], [K(2)])
            for coef in (1.0 / 24.0, 1.0 / 6.0, 0.5, 1.0):
                tt(pol[:, A], pol[:, A], y_[:, A], ALU.mult, [K(2), K(5)], [K(2)])
                ts(pol[:, A], pol[:, A], coef, None, ALU.add, ALU.bypass, [K(2)], [K(2)])
            stt(pol[:, A], pol[:, A], -1.0, y_[:, A], ALU.mult, ALU.mult, [K(2), K(5)], [K(2)])
            ts(pol[:, A], pol[:, A], 0.0, None, ALU.max, ALU.bypass, [K(2)], [K(2)])
            act(pol[:, A], pol[:, A], AF.Sqrt, [K(2)], [K(2)])
            tt(u_[:, A], i_[:, A], xc[:, A], ALU.mult, [K(3), K(1)], [K(5)])
            tt(u_[:, A], u_[:, A], pol[:, A], ALU.mult, [K(5), K(2)], [K(5)])
            for (kind, c0, T, b) in segs:
                init = lru_h[:, l, pt:pt + 1] if kind == "p" else smp_h[:, b, pt:pt + 1]
                ik = "lru_h" if kind == "p" else "smp_h"
                S.op("dve", lambda E, c0=c0, T=T, init=init: E.tensor_tensor_scan(
                    out=hh[:, c0:c0 + T], data0=a_[:, c0:c0 + T], data1=u_[:, c0:c0 + T], initial=init,
                    op0=ALU.mult, op1=ALU.add), [K(4), K(5), ik], [K(6)])
                cp(init, hh[:, c0 + T - 1:c0 + T], [K(6)], [ik])
            g2 = i_
            tt(g2[:, A], ga_[:, A], ga_[:, A], ALU.mult, [K(7)], [K(3)])
            ts(g2[:, A], g2[:, A], 0.044715, 1.0, ALU.mult, ALU.add, [K(3)], [K(3)])
            tt(g2[:, A], g2[:, A], ga_[:, A], ALU.mult, [K(3), K(7)], [K(3)])
            act(g2[:, A], g2[:, A], AF.Sigmoid, [K(3)], [K(3)], scale=1.5957691216057308)
            tt(g2[:, A], g2[:, A], ga_[:, A], ALU.mult, [K(3), K(7)], [K(3)])
            tt(tmp[:, A], hh[:, A], g2[:, A], ALU.mult, [K(6), K(3)], [K(8 + pt)])
        for (c0, n) in cts:
            pi = pring.get()
            for pt in range(2):
                bi = bring.get()
                act(b512[bi][:, 0:n], ypre[pt][:, c0:c0 + n], AF.Square, ["fs%d" % (8 + pt)], ["b512_%d" % bi])
                mm(psm[pi][:, 0:n], ones_b[:], b512[bi][:, 0:n], pt == 0, pt == 1, ["ones_b", "b512_%d" % bi], [P(pi)])
            ti = tring.get()
            rs = t512[ti][:, 0:n]
            ts(rs, psm[pi][:, 0:n], 1.0 / 256.0, EPS, ALU.mult, ALU.add, [P(pi)], ["t512_%d" % ti])
            act(rs, rs, AF.Sqrt, ["t512_%d" % ti], ["t512_%d" % ti])
            S.op("dve", lambda E, rs=rs: E.reciprocal(out=rs, in_=rs), ["t512_%d" % ti], ["t512_%d" % ti])
            for pt in range(2):
                stt(yA[:, pt, c0:c0 + n], ypre[pt][:, c0:c0 + n], pcol(l, 48 + pt), rs, ALU.mult, ALU.mult,
                    ["fs%d" % (8 + pt), "ptab", "t512_%d" % ti], ["ybuf"])
        proj_fm(w["w_out"][l][0:256, :], 128, 2, D, 128, lambda c, c0, n: yA[:, c, c0:c0 + n], ["ybuf"], cts,
                resid_add(1.0))
        if p == NPASS - 1:
            finals.append(S.dma("sp", o_h[l], lru_h[:, l, :], reads=["lru_h"]))
            finals.append(S.dma("sp", o_conv[l], lru_tail[:, l, :, :], reads=["lru_tail"]))

    def mixer_b(l, p):
        cts = coltiles(p)
        win = w["w_in"][l]
        yB = ybuf[0:64, 0:4 * NTX].rearrange("p (c n) -> p c n", c=4)
        ncols = TC + (NST if p == 0 else 0)
        A = slice(0, ncols)
        K = lambda i: "fs%d" % i
        segs = [("p", 0, TC, None, 64)]
        if p == 0:
            segs += [("s", TC + DS * b, DS, b, DS) for b in range(NSB)]
        for hd in range(4):
            q_, f_, cum, ec, en, k_, g_, oT = FS[0], FS[1], FS[2], FS[3], FS[4], FS[5], FS[6], FS[7]
            qt, kt, khat = BS[0], BS[1], BS[2]
            vtm = BS[3]
            khtm = BS[4]

            def fm_proj(col0, dst, dk, func=None, bias=None, scale=1.0):
                wv, wk = wload(win[:, col0 + hd * 64: col0 + (hd + 1) * 64], 128, 8, 64)
                for (c0, n) in cts:
                    pi = pring.get()
                    for c in range(8):
                        mm(psm[pi][0:64, 0:n], wv[:, c, :], h[:, c, c0:c0 + n], c == 0, c == 7, [wk, "h"], [P(pi)])
                    if func is None:
                        cp(dst[0:64, c0:c0 + n], psm[pi][0:64, 0:n], [P(pi)], [dk], eng="act")
                    else:
                        act(dst[0:64, c0:c0 + n], psm[pi][0:64, 0:n], func, [P(pi)], [dk])
            fm_proj(512, q_, K(0))
            fm_proj(768, f_, K(1), func=AF.Sigmoid)
            fm_proj(1280, g_, K(6))
            ts(f_[0:64, A], f_[0:64, A], dpar[0:64, l, 8 + hd:9 + hd], dpar[0:64, l, 4 + hd:5 + hd], ALU.mult, ALU.add,
               [K(1), "dpar"], [K(1)])
            ts(k_[0:64, A], f_[0:64, A], -1.0, 1.0, ALU.mult, ALU.add, [K(1)], [K(5)])
            act(f_[0:64, A], f_[0:64, A], AF.Ln, [K(1)], [K(1)])
            S.op("dve", lambda E: E.tensor_tensor_scan(out=cum[0:64, A], data0=rmask[0:64, A], data1=f_[0:64, A],
                                                       initial=0.0, op0=ALU.mult, op1=ALU.add),
                 ["rmask", K(1)], [K(2)])
            act(ec[0:64, A], cum[0:64, A], AF.Exp, [K(2)], [K(3)])
            act(en[0:64, A], cum[0:64, A], AF.Exp, [K(2)], [K(4)], scale=-1.0)
            tt(qt[0:64, A], q_[0:64, A], ec[0:64, A], ALU.mult, [K(0), K(3)], ["bs0"])
            tt(k_[0:64, A], k_[0:64, A], en[0:64, A], ALU.mult, [K(5), K(4)], [K(5)])
            cp(kt[0:64, A], k_[0:64, A], [K(5)], ["bs1"], eng="act")
            wv, wvk = wload(win[:, 1024 + hd * 64: 1024 + (hd + 1) * 64], 128, 8, 64)
            chunks = []
            for (kind, c0, T, b, C) in segs:
                for j in range(T // C):
                    chunks.append((kind, c0 + j * C, C, b, j == T // C - 1))
            for gi in range(0, len(chunks), 8):
                grp = chunks[gi:gi + 8]
                pi = pring.get()
                for jj, (kind, cc, C, b, last) in enumerate(grp):
                    for c in range(8):
                        mm(psm[pi][0:C, jj * 64:(jj + 1) * 64], h[:, c, cc:cc + C], wv[:, c, :], c == 0, c == 7,
                           ["h", wvk], [P(pi)])
                Cg = grp[0][2]
                cp(vtm[0:Cg, gi * 64:(gi + len(grp)) * 64], psm[pi][0:Cg, 0:len(grp) * 64], [P(pi)], ["bs3"], eng="act")
            for ci, (kind, cc, C, b, last) in enumerate(chunks):
                ts(khat[0:64, cc:cc + C], k_[0:64, cc:cc + C], ec[0:64, cc + C - 1:cc + C], None, ALU.mult, ALU.bypass,
                   [K(5), K(3)], ["bs2"])
            for gi in range(0, len(chunks), 8):
                grp = chunks[gi:gi + 8]
                ti = ptring.get()
                for jj, (kind, cc, C, b, last) in enumerate(grp):
                    tr(pst[ti][0:C, jj * 64:(jj + 1) * 64], khat[0:64, cc:cc + C], ident_b[0:64, 0:64],
                       ["bs2", "ident_b"], ["pst%d" % ti])
                Cg = grp[0][2]
                cp(khtm[0:Cg, gi * 64:(gi + len(grp)) * 64], pst[ti][0:Cg, 0:len(grp) * 64], ["pst%d" % ti], ["bs4"])
            for gi in range(0, len(chunks), 8):
                grp = chunks[gi:gi + 8]
                Cg = grp[0][2]
                ng = len(grp)
                pds = pring.get()
                for jj in range(ng):
                    ci = gi + jj
                    mm(psm[pds][0:64, jj * 64:(jj + 1) * 64], khtm[0:Cg, ci * 64:(ci + 1) * 64],
                       vtm[0:Cg, ci * 64:(ci + 1) * 64], True, True, ["bs4", "bs3"], [P(pds)])
                pin = pring.get()
                for jj, (kind, cc, C, b, last) in enumerate(grp):
                    mm(psm[pin][0:C, jj * 64: jj * 64 + C], kt[0:64, cc:cc + C], qt[0:64, cc:cc + C], True, True,
                       ["bs1", "bs0"], [P(pin)])
                bi = bring.get()
                AT = b512[bi]
                if Cg == 64:
                    tt(AT[0:64, 0:ng * 64], psm[pin][0:64, 0:ng * 64], triu[0:64, 0:ng * 64], ALU.mult,
                       [P(pin), "triu"], ["b512_%d" % bi])
                else:
                    for jj in range(ng):
                        tt(AT[0:Cg, jj * 64: jj * 64 + Cg], psm[pin][0:Cg, jj * 64: jj * 64 + Cg], triu[0:Cg, 0:Cg],
                           ALU.mult, [P(pin), "triu"], ["b512_%d" % bi])
                for jj, (kind, cc, C, b, last) in enumerate(grp):
                    if kind == "p":
                        Sst = hgS[:, l, hd, :]
                        sk = "hgS"
                    else:
                        Sst = smp_S[:, b, hd, :]
                        sk = "smp_S"
                    cp(hgSb[:, jj, :], Sst, [sk], ["hgSb"], eng="act")
                    stt(Sst, Sst, ec[0:64, cc + C - 1:cc + C], psm[pds][0:64, jj * 64:(jj + 1) * 64], ALU.mult, ALU.add,
                        [sk, K(3), P(pds)], [sk])
                po = pring.get()
                for jj, (kind, cc, C, b, last) in enumerate(grp):
                    ci = gi + jj
                    mm(psm[po][0:64, jj * 64: jj * 64 + C], vtm[0:C, ci * 64:(ci + 1) * 64], AT[0:C, jj * 64: jj * 64 + C],
                       True, False, ["bs3", "b512_%d" % bi], [P(po)])
                    mm(psm[po][0:64, jj * 64: jj * 64 + C], hgSb[:, jj, :], qt[0:64, cc:cc + C], False, True,
                       ["hgSb", "bs0"], [P(po)])
                if Cg == 64:
                    cc0 = grp[0][1]
                    cp(oT[0:64, cc0:cc0 + ng * 64], psm[po][0:64, 0:ng * 64], [P(po)], [K(7)], eng="act")
                else:
                    for jj, (kind, cc, C, b, last) in enumerate(grp):
                        cp(oT[0:64, cc:cc + C], psm[po][0:64, jj * 64: jj * 64 + C], [P(po)], [K(7)], eng="act")
            for (c0, n) in cts:
                bi = bring.get()
                act(b512[bi][0:64, 0:n], oT[0:64, c0:c0 + n], AF.Square, [K(7)], ["b512_%d" % bi])
                pi = pring.get()
                mm(psm[pi][0:64, 0:n], ones_b[0:64, 0:64], b512[bi][0:64, 0:n], True, True, ["ones_b", "b512_%d" % bi],
                   [P(pi)])
                ti = tring.get()
                rs = t512[ti][0:64, 0:n]
                ts(rs, psm[pi][0:64, 0:n], 1.0 / 64.0, EPS, ALU.mult, ALU.add, [P(pi)], ["t512_%d" % ti])
                act(rs, rs, AF.Sqrt, ["t512_%d" % ti], ["t512_%d" % ti])
                S.op("dve", lambda E, rs=rs: E.reciprocal(out=rs, in_=rs), ["t512_%d" % ti], ["t512_%d" % ti])
                stt(oT[0:64, c0:c0 + n], oT[0:64, c0:c0 + n], ptab[0:64, l * PL + 50: l * PL + 51], rs, ALU.mult, ALU.mult,
                    [K(7), "ptab", "t512_%d" % ti], [K(7)])
                tj = tring.get()
                act(t512[tj][0:64, 0:n], g_[0:64, c0:c0 + n], AF.Silu, [K(6)], ["t512_%d" % tj])
                tt(yB[:, hd, c0:c0 + n], oT[0:64, c0:c0 + n], t512[tj][0:64, 0:n], ALU.mult, [K(7), "t512_%d" % tj],
                   ["ybuf"])
        proj_fm(w["w_out"][l][256:512, :], 64, 4, D, 128, lambda c, c0, n: yB[:, c, c0:c0 + n], ["ybuf"], cts,
                resid_add(1.0))
        if p == NPASS - 1:
            finals.append(S.dma("sp", o_hg[l], hgS[:, l, :, :], reads=["hgS"]))

    def attn_tile(qT_ap, qk, n, kbs, yout, accw):
        nk = len(kbs)
        for i, (ka, kk, va, vk, wa) in enumerate(kbs):
            pi = pring.get()
            mm(psm[pi][:, 0:n], ka, qT_ap, True, True, [kk, qk], [P(pi)])
            bi = bring.get()
            pt_ = b512[bi][:, 0:n]
            act(pt_, psm[pi][:, 0:n], AF.Exp, [P(pi)], ["b512_%d" % bi], scale=0.125)
            tt(pt_, pt_, wa, ALU.mult, ["b512_%d" % bi, "wtb"], ["b512_%d" % bi])
            mm(psacc[0][0:64, 0:n], va, pt_, i == 0, i == nk - 1, [vk, "b512_%d" % bi], ["psacc0"])
            mm(psacc[1][0:64, 0:n], ones_b[:, 0:64], pt_, i == 0, i == nk - 1, ["ones_b", "b512_%d" % bi], ["psacc1"])
        ti = tring.get()
        rd = t512[ti][0:64, 0:n]
        S.op("dve", lambda E: E.reciprocal(out=rd, in_=psacc[1][0:64, 0:n]), ["psacc1"], ["t512_%d" % ti])
        tt(yout, psacc[0][0:64, 0:n], rd, ALU.mult, ["psacc0", "t512_%d" % ti], [accw])

    def mixer_c(l, p):
        cts = coltiles(p)
        pcts = [(c0, n) for (c0, n) in cts if c0 < TC]
        win = w["w_in"][l]
        t0 = p * TC
        hist = min(W, t0)
        hb = hist // 128
        yC = FS
        yCb = ybuf[0:64, 0:8 * NTX].rearrange("p (c n) -> p c n", c=8)
        emit_kv = (t0 + TC > SEQ - KEEP)
        if emit_kv or p == 0:
            wv, wvk = wload(win[:, 2560:3072], 128, 8, 512)
            wk_, wkk = wload(win[:, 2048:2560], 128, 8, 512)
        if emit_kv:
            for b in range(NBLK):
                row = t0 + b * 128 - (SEQ - KEEP)
                xi = xinring.get()
                pi = pring.get()
                for c in range(8):
                    mm(psm[pi][:, :], h[:, c, b * 128:(b + 1) * 128], wv[:, c, :], c == 0, c == 7, ["h", wvk], [P(pi)])
                cp(xin[xi][:, 0:512], psm[pi][:, :], [P(pi)], ["xin%d" % xi])
                pj = pring.get()
                for c in range(8):
                    mm(psm[pj][:, :], h[:, c, b * 128:(b + 1) * 128], wk_[:, c, :], c == 0, c == 7, ["h", wkk], [P(pj)])
                cp(xin[xi][:, 512:1024], psm[pj][:, :], [P(pj)], ["xin%d" % xi], eng="act")
                finals.append(S.dma("sp", o_v[l][row:row + 128, :], xin[xi][:, 0:512], reads=["xin%d" % xi]))
                finals.append(S.dma("sp", o_k[l][row:row + 128, :], xin[xi][:, 512:1024], reads=["xin%d" % xi]))
        if p == 0:
            skv = FS[9]
            pi = pring.get()
            for c in range(8):
                mm(psm[pi][0:NST, :], h[:, c, TC:TC + NST], wk_[:, c, :], c == 0, c == 7, ["h", wkk], [P(pi)])
            cp(skv[0:NST, 0:512], psm[pi][0:NST, :], [P(pi)], ["fs9"])
            pj = pring.get()
            for c in range(8):
                mm(psm[pj][0:NST, :], h[:, c, TC:TC + NST], wv[:, c, :], c == 0, c == 7, ["h", wvk], [P(pj)])
            cp(skv[0:NST, 512:1024], psm[pj][0:NST, :], [P(pj)], ["fs9"], eng="act")
            for b in range(NSB):
                finals.append(S.dma("sp", s_k[l, b, W - DS:W, :], skv[b * DS:(b + 1) * DS, 0:512], reads=["fs9"]))
                finals.append(S.dma("sp", s_v[l, b, W - DS:W, :], skv[b * DS:(b + 1) * DS, 512:1024], reads=["fs9"]))
                finals.append(S.dma("act", s_k[l, b, 0:W - DS, :], c_k[l, b, DS:W, :]))
                finals.append(S.dma("act", s_v[l, b, 0:W - DS, :], c_v[l, b, DS:W, :]))
        for hd in range(8):
            qT = BS[0]
            S.dma("pool", wtb[:], wt_d[hd], writes=["wtb"])
            wq, wqk = wload(win[:, 1536 + hd * 64: 1536 + (hd + 1) * 64], 128, 8, 64)
            wkh, wkhk = wload(win[:, 2048 + hd * 64: 2048 + (hd + 1) * 64], 128, 8, 64)
            wvh, wvhk = wload(win[:, 2560 + hd * 64: 2560 + (hd + 1) * 64], 128, 8, 64)
            if hb > 0:
                S.dma("sp", kT[:, W - hist:W], kT_scr[l][hd * 64:(hd + 1) * 64, t0 - hist:t0],
                      reads=["kT_scr%d" % l], writes=["kT"])
                S.dma("sp", vh[:, WB - hb:WB, :], v_scr[l][hd, t0 - hist:t0, :].rearrange("(b p) d -> p b d", p=128),
                      reads=["v_scr%d" % l], writes=["vh"])
            for b0 in range(0, NBLK, 8):
                pi = pring.get()
                for bb in range(8):
                    b = b0 + bb
                    for c in range(8):
                        mm(psm[pi][:, bb * 64:(bb + 1) * 64], h[:, c, b * 128:(b + 1) * 128], wvh[:, c, :], c == 0, c == 7,
                           ["h", wvhk], [P(pi)])
                cp(vh[:, WB + b0:WB + b0 + 8, :], psm[pi][:].rearrange("p (b d) -> p b d", b=8), [P(pi)], ["vh"], eng="act")
            S.dma("sp", v_scr[l][hd, t0:t0 + TC, :].rearrange("(b p) d -> p b d", p=128), vh[:, WB:WB + NBLK, :],
                  reads=["vh"], writes=["v_scr%d" % l])
            for (c0, n) in pcts:
                pi = pring.get()
                for c in range(8):
                    mm(psm[pi][0:64, 0:n], wq[:, c, :], h[:, c, c0:c0 + n], c == 0, c == 7, [wqk, "h"], [P(pi)])
                cp(qT[0:64, c0:c0 + n], psm[pi][0:64, 0:n], [P(pi)], ["bs0"], eng="act")
                pj = pring.get()
                for c in range(8):
                    mm(psm[pj][0:64, 0:n], wkh[:, c, :], h[:, c, c0:c0 + n], c == 0, c == 7, [wkhk, "h"], [P(pj)])
                cp(kT[:, W + c0: W + c0 + n], psm[pj][0:64, 0:n], [P(pj)], ["kT"])
            S.dma("sp", kT_scr[l][hd * 64:(hd + 1) * 64, t0:t0 + TC], kT[:, W:W + TC], reads=["kT"],
                  writes=["kT_scr%d" % l])
            for (c0, n) in pcts:
                qb0 = c0 // 128
                kbs = []
                for kb in range(qb0 - WB, qb0 + 4):
                    if kb < -hb:
                        continue
                    col = W + kb * 128
                    dbi = qb0 - kb + 3
                    kbs.append((kT[:, col:col + 128], "kT", vh[:, WB + kb, :], "vh",
                                wtb[:, dbi * 128:(dbi + 4) * 128]))
                attn_tile(qT[0:64, c0:c0 + n], "bs0", n, kbs, yC[hd][0:64, c0:c0 + n], "fs%d" % hd)
        if p == 0:
            for b in range(NSB):
                cs = TC + b * DS
                for hd in range(8):
                    S.dma("pool", wtb[:], wt_d[hd], writes=["wtb"])
                    wq, wqk = wload(win[:, 1536 + hd * 64: 1536 + (hd + 1) * 64], 128, 8, 64)
                    wkh, wkhk = wload(win[:, 2048 + hd * 64: 2048 + (hd + 1) * 64], 128, 8, 64)
                    wvh, wvhk = wload(win[:, 2560 + hd * 64: 2560 + (hd + 1) * 64], 128, 8, 64)
                    S.dma("pool", vh[:, 0:WB, :],
                          c_v[l, b].rearrange("(k p) n -> p k n", p=128)[:, :, hd * 64:(hd + 1) * 64], writes=["vh"])
                    pv = pring.get()
                    for c in range(8):
                        mm(psm[pv][0:DS, 0:64], h[:, c, cs:cs + DS], wvh[:, c, :], c == 0, c == 7, ["h", wvhk], [P(pv)])
                    cp(vh[0:DS, WB, :], psm[pv][0:DS, 0:64], [P(pv)], ["vh"])
                    kch = BS[2]
                    S.dma("pool", kch[:, 0:WB * 64].rearrange("p (k d) -> p k d", k=WB),
                          c_k[l, b].rearrange("(k p) n -> p k n", p=128)[:, :, hd * 64:(hd + 1) * 64], writes=["bs2"])
                    for g in range(WB // 8):
                        ti = ptring.get()
                        for jj in range(8):
                            kb = g * 8 + jj
                            tr(pst[ti][0:64, jj * 128:(jj + 1) * 128], kch[:, kb * 64:(kb + 1) * 64], ident_b[:],
                               ["bs2", "ident_b"], ["pst%d" % ti])
                        cp(kT[:, g * 1024:(g + 1) * 1024], pst[ti][0:64, :], ["pst%d" % ti], ["kT"],
                           eng="act" if g % 2 else "dve")
                    pi = pring.get()
                    for c in range(8):
                        mm(psm[pi][0:64, 0:DS], wkh[:, c, :], h[:, c, cs:cs + DS], c == 0, c == 7, [wkhk, "h"], [P(pi)])
                    S.op("dve", lambda E: E.memset(kT[:, W:W + 128], 0.0), [], ["kT"])
                    cp(kT[:, W:W + DS], psm[pi][0:64, 0:DS], [P(pi)], ["kT"])
                    pj = pring.get()
                    for c in range(8):
                        mm(psm[pj][0:64, 0:DS], wq[:, c, :], h[:, c, cs:cs + DS], c == 0, c == 7, [wqk, "h"], [P(pj)])
                    qs = BS[0]
                    cp(qs[0:64, 0:DS], psm[pj][0:64, 0:DS], [P(pj)], ["bs0"], eng="act")
                    kbs = []
                    for kb in range(WB + 1):
                        dbi = (WB - kb) + 3
                        kbs.append((kT[:, kb * 128:(kb + 1) * 128], "kT", vh[:, kb, :], "vh",
                                    wtb[:, dbi * 128: dbi * 128 + DS]))
                    attn_tile(qs[0:64, 0:DS], "bs0", DS, kbs, yC[hd][0:64, cs:cs + DS], "fs%d" % hd)
        for (c0, n) in cts:
            pi = pring.get()
            for hd in range(8):
                bi = bring.get()
                act(b512[bi][0:64, 0:n], yC[hd][0:64, c0:c0 + n], AF.Square, ["fs%d" % hd], ["b512_%d" % bi])
                mm(psm[pi][0:64, 0:n], ones_b[0:64, 0:64], b512[bi][0:64, 0:n], hd == 0, hd == 7,
                   ["ones_b", "b512_%d" % bi], [P(pi)])
            ti = tring.get()
            rs = t512[ti][0:64, 0:n]
            ts(rs, psm[pi][0:64, 0:n], 1.0 / 512.0, EPS, ALU.mult, ALU.add, [P(pi)], ["t512_%d" % ti])
            act(rs, rs, AF.Sqrt, ["t512_%d" % ti], ["t512_%d" % ti])
            S.op("dve", lambda E, rs=rs: E.reciprocal(out=rs, in_=rs), ["t512_%d" % ti], ["t512_%d" % ti])
            for hd in range(8):
                stt(yCb[:, hd, c0:c0 + n], yC[hd][0:64, c0:c0 + n], ptab[0:64, l * PL + 51 + hd: l * PL + 52 + hd], rs,
                    ALU.mult, ALU.mult, ["fs%d" % hd, "ptab", "t512_%d" % ti], ["ybuf"])
        proj_fm(w["w_out"][l][512:1024, :], 64, 8, D, 128, lambda c, c0, n: yCb[:, c, c0:c0 + n], ["ybuf"], cts,
                resid_add(1.0))

    def load_memT():
        for mb in range(NMEM // 128):
            xi = xinring.get()
            S.dma("sp", xin[xi][:], memp[mb * 128:(mb + 1) * 128, :], writes=["xin%d" % xi])
            for c4 in range(2):
                pi = pring.get()
                for cc in range(4):
                    c = c4 * 4 + cc
                    tr(psm[pi][:, cc * 128:(cc + 1) * 128], xin[xi][:, c * 128:(c + 1) * 128], ident_f[:],
                       ["xin%d" % xi, "ident_f"], [P(pi)])
                cp(memT[:, c4 * 4:(c4 + 1) * 4, mb * 128:(mb + 1) * 128], psm[pi][:].rearrange("p (c n) -> p c n", c=4),
                   [P(pi)], [KMT])

    def prep_mem(l, p):
        if p == 0:
            if "p_lt" not in cfg.get("xskip", ()):
                load_memT()
            else:
                S.op("dve", lambda E: E.memset(memT, 0.5), [], [KMT])
            for nm, osrc in (("x_wk", o_mk), ("x_wv", o_mv)):
                if nm == "x_wv" and "p_v" in cfg.get("xskip", ()):
                    continue
                for half in range(2):
                    wv_, wk_ = wload(w[nm][l][:, half * 512:(half + 1) * 512], 128, 8, 512)
                    for mb in range(NMEM // 128):
                        pi = pring.get()
                        for c in range(8):
                            mm(psm[pi][:, :], memT[:, c, mb * 128:(mb + 1) * 128], wv_[:, c, :], c == 0, c == 7,
                               [KMT, wk_], [P(pi)])
                        xi = xinring.get()
                        cp(xin[xi][:, 0:512], psm[pi][:, :], [P(pi)], ["xin%d" % xi])
                        if "p_out" not in cfg.get("xskip", ()):
                            finals.append(S.dma("sp", osrc[l][mb * 128:(mb + 1) * 128, half * 512:(half + 1) * 512],
                                                xin[xi][:, 0:512], reads=["xin%d" % xi]))
                        if nm == "x_wv":
                            cp(mv[:, mb, half * 512:(half + 1) * 512], xin[xi][:, 0:512], ["xin%d" % xi], [KMV], eng="act")
                    if nm == "x_wk" and "p_fm" not in cfg.get("xskip", ()):
                        for mi in range(4):
                            pi = pring.get()
                            for c in range(8):
                                mm(psm[pi][:, 0:NMEM], wv_[:, c, mi * 128:(mi + 1) * 128], memT[:, c, :], c == 0, c == 7,
                                   [wk_, KMT], [P(pi)])
                            cp(mkT[:, half * 4 + mi, :], psm[pi][:, 0:NMEM], [P(pi)], [KMK], eng="act")
            if NPASS > 1 and "p_scr" not in cfg.get("xskip", ()):
                S.dma("sp", mkT_scr[l], mkT_flat, reads=[KMK], writes=["mkT_scr%d" % l])
                S.dma("sp", mv_scr[l], mv_flat, reads=[KMV], writes=["mv_scr%d" % l])
        elif "p_scr" not in cfg.get("xskip", ()):
            S.dma("sp", mkT_flat, mkT_scr[l], reads=["mkT_scr%d" % l], writes=[KMK])
            S.dma("sp", mv_flat, mv_scr[l], reads=["mv_scr%d" % l], writes=[KMV])

    def cross_core(l, c0, n, qoff):
        oX = ybuf[:, 0:8 * NTX].rearrange("p (c n) -> p c n", c=8)
        for hx in range(4):
            pts = []
            for mb in range(NMEM // 128):
                pi = pring.get()
                for dc in range(2):
                    mm(psm[pi][:, 0:n], mkT[:, hx * 2 + dc, mb * 128:(mb + 1) * 128],
                       BS[1 + (hx * 2 + dc) // 2][:, ((hx * 2 + dc) % 2) * 512 + qoff:((hx * 2 + dc) % 2) * 512 + qoff + n],
                       dc == 0, dc == 1, [KMK, sbqk(hx * 2 + dc)], [P(pi)])
                bi = bring.get()
                act(b512[bi][:, 0:n], psm[pi][:, 0:n], AF.Exp, [P(pi)], ["b512_%d" % bi], scale=1.0 / 16.0)
                pts.append(bi)
            pdn = pring.get()
            for i, bi in enumerate(pts):
                mm(psm[pdn][:, 0:n], ones_b[:], b512[bi][:, 0:n], i == 0, i == len(pts) - 1, ["ones_b", "b512_%d" % bi],
                   [P(pdn)])
            ti = tring.get()
            rd = t512[ti][:, 0:n]
            S.op("dve", lambda E, rd=rd, pdn=pdn: E.reciprocal(out=rd, in_=psm[pdn][:, 0:n]), [P(pdn)], ["t512_%d" % ti])
            for dc in range(2):
                po = pring.get()
                for i, bi in enumerate(pts):
                    mm(psm[po][:, 0:n], mv[:, i, (hx * 2 + dc) * 128:(hx * 2 + dc + 1) * 128], b512[bi][:, 0:n],
                       i == 0, i == len(pts) - 1, [KMV, "b512_%d" % bi], [P(po)])
                tt(oX[:, hx * 2 + dc, c0:c0 + n], psm[po][:, 0:n], rd, ALU.mult, [P(po), "t512_%d" % ti], ["ybuf"])

    def cross(l, p):
        cts = coltiles(p)
        rmsnorm(l * PL + 16, p, hview, "h", cts)
        if "prep" not in cfg.get("xskip", ()):
            prep_mem(l, p)

        def qsink(mi, c0_, n_, pap, pk):
            cp(sbq(mi, n_), pap, [pk], [sbqk(mi)], eng="act" if mi % 2 else "dve")
        for (c0, n) in cts:
            if c0 >= TC:
                continue
            proj_fm(w["x_wq"][l], 128, 8, D, 128, lambda c, c0_, n_: h[:, c, c0_:c0_ + n_], ["h"], [(c0, n)], qsink)
            if "core" not in cfg.get("xskip", ()):
                cross_core(l, c0, n, 0)
        if p == 0 and "sample" not in cfg.get("xskip", ()):
            proj_fm(w["x_wq"][l], 128, 8, D, 128, lambda c, c0_, n_: h[:, c, c0_:c0_ + n_], ["h"], [(TC, NST)], qsink)
            for b in range(NSB):
                S.dma("pool", mv, cm_v[l, b].rearrange("(k p) n -> p k n", p=128), writes=[KMV])
                kst = BS[0]
                for mb in range(NMEM // 128):
                    S.dma("pool", kst[:, 0:D], cm_k[l, b, mb * 128:(mb + 1) * 128, :], writes=["bs0"])
                    ti = ptring.get()
                    for c in range(8):
                        tr(pst[ti][:, c * 128:(c + 1) * 128], kst[:, c * 128:(c + 1) * 128], ident_b[:],
                           ["bs0", "ident_b"], ["pst%d" % ti])
                    cp(mkT[:, :, mb * 128:(mb + 1) * 128], pst[ti][:].rearrange("p (c n) -> p c n", c=8),
                       ["pst%d" % ti], [KMK])
                cs = TC + b * DS
                cross_core(l, cs, DS, b * DS)
        proj_fm(w["x_wo"][l], 128, 8, D, 128,
                lambda c, c0, n: ybuf[:, 0:8 * NTX].rearrange("p (c n) -> p c n", c=8)[:, c, c0:c0 + n], ["ybuf"], cts,
                resid_add(1.0))

    stages = cfg.get("stages", ("ffn1", "a", "b", "c", "x", "ffn2"))
    for p in range(NPASS):
        load_x(p)
        for l in range(DEPTH):
            if p == 0:
                for b in range(NSB):
                    S.dma("sp", smp_h[:, b, :], st_h[l, b], writes=["smp_h"])
                    S.dma("sp", smp_tail[:, b, :, :], st_conv[l, b], writes=["smp_tail"])
                    S.dma("sp", smp_S[:, b, :, :], st_hg[l, b], writes=["smp_S"])
            if "ffn1" in stages:
                ffn(l, 1, p)
            rmsnorm(l * PL + 8, p, hview, "h", coltiles(p))
            if "a" in stages:
                mixer_a(l, p)
            if "b" in stages:
                mixer_b(l, p)
            if "c" in stages:
                mixer_c(l, p)
            if "x" in stages:
                cross(l, p)
            if "ffn2" in stages:
                ffn(l, 2, p)
            if p == 0:
                for b in range(NSB):
                    finals.append(S.dma("sp", s_h[l, b], smp_h[:, b, :], reads=["smp_h"]))
                    finals.append(S.dma("sp", s_conv[l, b], smp_tail[:, b, :, :], reads=["smp_tail"]))
                    finals.append(S.dma("sp", s_hg[l, b], smp_S[:, b, :, :], reads=["smp_S"]))
        store_y(p)
    S.finish(finals)
    st.close()
    print("[kernel] instructions=%d waits=%d sems=%d" % (S.n_ins, S.n_wait, S.nsem), flush=True)
    return nc


def host_consts(cfg):
    TC, NSB = cfg["TC"], cfg["NSB"]
    NTX = TC + NSB * DS
    triu = np.tile(np.triu(np.ones((64, 64), np.float32)), (1, 8))
    rmask = np.ones((64, NTX), np.float32)
    rmask[:, 0:TC:64] = 0.0
    rmask[:, TC::DS] = 0.0
    return {"wt_tab": wt_table().reshape(8, 128, 23 * 128), "ident": np.eye(128, dtype=np.float32),
            "triu": triu, "rmask": rmask}


def pack_ptab(inp, DEPTH):
    NPT = DEPTH * PL + 8
    t = np.zeros((128, NPT), np.float32)

    def fm8(v):
        return np.ascontiguousarray(v.reshape(8, 128).T)

    def fm2(v):
        return np.ascontiguousarray(v.reshape(2, 128).T)
    for l in range(DEPTH):
        b = l * PL
        t[:, b + 0:b + 8] = fm8(inp["n_ffn1"][l])
        t[:, b + 8:b + 16] = fm8(inp["n_mix"][l])
        t[:, b + 16:b + 24] = fm8(inp["n_cross"][l])
        t[:, b + 24:b + 32] = fm8(inp["n_ffn2"][l])
        cw = inp["lru_conv_w"][l]
        for pt in range(2):
            for tap in range(4):
                t[:, b + 32 + pt * 4 + tap] = cw[tap, pt * 128:(pt + 1) * 128]
        t[:, b + 40:b + 42] = fm2(inp["lru_conv_b"][l])
        t[:, b + 42:b + 44] = fm2(inp["lru_ba"][l])
        t[:, b + 44:b + 46] = fm2(inp["lru_bx"][l])
        t[:, b + 46:b + 48] = fm2(inp["lru_lambda"][l])
        t[:, b + 48:b + 50] = fm2(inp["gn_a"][l])
        t[0:64, b + 50] = inp["hgrn_norm"][l]
        t[0:64, b + 51:b + 59] = inp["gn_c"][l].reshape(8, 64).T
        t[0:64, b + 59:b + 63] = inp["hgrn_lb"][l].reshape(4, 64).T
    t[:, DEPTH * PL:DEPTH * PL + 8] = fm8(inp["n_final"])
    return t


def block_diag(wb):
    DEPTH = wb.shape[0]
    o = np.zeros((DEPTH, 2, 128, 128), np.float32)
    for pt in range(2):
        for j in range(2):
            o[:, pt, j * 64:(j + 1) * 64, j * 64:(j + 1) * 64] = wb[:, pt * 2 + j]
    return o


_NC_CACHE = {}


def run(inp, cfg, n_cores=8):
    DEPTH, NSB, SEQ = cfg["DEPTH"], cfg["NSB"], cfg["SEQ"]
    BATCH = inp["x_prompt"].shape[0]
    key = tuple(sorted((k, str(v)) for k, v in cfg.items()))
    if key not in _NC_CACHE:
        _NC_CACHE[key] = build(cfg)
    nc = _NC_CACHE[key]
    consts = host_consts(cfg)
    shared = {k: np.ascontiguousarray(inp[k]) for k in
              ["ffn1_wg", "ffn1_wu", "ffn1_wd", "ffn2_wg", "ffn2_wu", "ffn2_wd", "w_in", "w_out", "x_wq", "x_wk",
               "x_wv", "x_wo"]}
    shared["lru_wbd"] = np.ascontiguousarray(
        np.stack([block_diag(inp["lru_wa"]), block_diag(inp["lru_wx"])], axis=1))
    shared["ptab"] = pack_ptab(inp, DEPTH)
    shared.update(consts)
    in_maps = []
    for c in range(n_cores):
        m = dict(shared)
        if c < BATCH:
            m["xp"] = np.ascontiguousarray(inp["x_prompt"][c])
            m["memp"] = np.ascontiguousarray(inp["mem_prompt"][c])
        else:
            m["xp"] = np.zeros((SEQ, D), np.float32)
            m["memp"] = np.zeros((NMEM, D), np.float32)
        bs = slice(c * NSB, (c + 1) * NSB)
        m["xs"] = np.ascontiguousarray(inp["x_sample"][bs].reshape(NSB * DS, D))
        m["st_h"] = np.ascontiguousarray(inp["state_lru_h"][:, bs].reshape(DEPTH, NSB, 2, 128).transpose(0, 1, 3, 2))
        m["st_conv"] = np.ascontiguousarray(
            inp["state_lru_conv"][:, bs].reshape(DEPTH, NSB, 3, 2, 128).transpose(0, 1, 4, 3, 2))
        m["st_hg"] = np.ascontiguousarray(inp["state_hgrn"][:, bs].transpose(0, 1, 3, 2, 4))
        m["c_k"] = np.ascontiguousarray(inp["cache_swa_k"][:, bs].reshape(DEPTH, NSB, W, 512))
        m["c_v"] = np.ascontiguousarray(inp["cache_swa_v"][:, bs].reshape(DEPTH, NSB, W, 512))
        m["cm_k"] = np.ascontiguousarray(inp["cache_mem_k"][:, bs].reshape(DEPTH, NSB, NMEM, D))
        m["cm_v"] = np.ascontiguousarray(inp["cache_mem_v"][:, bs].reshape(DEPTH, NSB, NMEM, D))
        in_maps.append(m)
    res = run_bass_kernel_spmd(nc, in_maps, core_ids=list(range(n_cores)))
    R = res.results
    KEEP = min(W, SEQ)
    y_prompt = np.stack([R[b]["yp"] for b in range(BATCH)], 0)
    y_sample = np.concatenate([R[c]["ys"].reshape(NSB, DS, D) for c in range(n_cores)], 0)
    p_h = np.stack([R[b]["o_h"].transpose(0, 2, 1).reshape(DEPTH, 256) for b in range(BATCH)], 1)
    p_c = np.stack([R[b]["o_conv"].transpose(0, 3, 2, 1).reshape(DEPTH, 3, 256) for b in range(BATCH)], 1)
    p_s = np.stack([R[b]["o_hg"].transpose(0, 2, 1, 3) for b in range(BATCH)], 1)
    p_k = np.stack([R[b]["o_k"].reshape(DEPTH, KEEP, 8, 64) for b in range(BATCH)], 1)
    p_v = np.stack([R[b]["o_v"].reshape(DEPTH, KEEP, 8, 64) for b in range(BATCH)], 1)
    p_mk = np.stack([R[b]["o_mk"].reshape(DEPTH, NMEM, 4, 256) for b in range(BATCH)], 1)
    p_mv = np.stack([R[b]["o_mv"].reshape(DEPTH, NMEM, 4, 256) for b in range(BATCH)], 1)
    s_h = np.concatenate([R[c]["s_h"].transpose(0, 1, 3, 2).reshape(DEPTH, NSB, 256) for c in range(n_cores)], 1)
    s_c = np.concatenate([R[c]["s_conv"].transpose(0, 1, 4, 3, 2).reshape(DEPTH, NSB, 3, 256) for c in range(n_cores)], 1)
    s_s = np.concatenate([R[c]["s_hg"].transpose(0, 1, 3, 2, 4) for c in range(n_cores)], 1)
    s_k = np.concatenate([R[c]["s_k"].reshape(DEPTH, NSB, W, 8, 64) for c in range(n_cores)], 1)
    s_v = np.concatenate([R[c]["s_v"].reshape(DEPTH, NSB, W, 8, 64) for c in range(n_cores)], 1)
    outs = (y_prompt, y_sample, p_h, p_c, p_s, p_k, p_v, p_mk, p_mv, s_h, s_c, s_s, s_k, s_v)
    return tuple(np.ascontiguousarray(o, dtype=np.float32) for o in outs)


def kernel(**inputs):
    inp = {k: np.asarray(v) for k, v in inputs.items()}
    cfg = {"SEQ": int(inp["x_prompt"].shape[1]), "TC": 1024, "DEPTH": int(inp["w_in"].shape[0]),
           "NSB": int(inp["x_sample"].shape[0]) // 8}
    return run(inp, cfg)
```

```python
from contextlib import ExitStack
import numpy as np
import ml_dtypes
import concourse.bass as bass
import concourse.mybir as mybir
from concourse.bass_utils import run_bass_kernel_spmd

F32 = mybir.dt.float32
BF16 = mybir.dt.bfloat16
AF = mybir.ActivationFunctionType
ALU = mybir.AluOpType

ENGS = ["pe", "act", "dve", "pool", "sp"]
SEM_ROLL = 30000
NDMASEM = 6

D = 1024
DFF = 2816
DIN = 3072
NMEM = 256
DS = 4
W = 2048
DIL = ((128, 1), (512, 4), (2048, 16))
EPS = 1e-6
GSZ = 2
WSZ = 2048
PL = 64


class Sched:
    def __init__(self, nc, stack):
        self.nc = nc
        self.stack = stack
        self.q = {e: [] for e in ENGS}
        self.cur_sem = {}
        self.cur_cnt = {}
        self.nsem = 0
        for e in ENGS:
            self._new_sem(e)
        self.dsem = {}
        self.dcnt = {}
        self.dnext = {}
        for e in ["sp", "pool", "act"]:
            self.dsem[e] = [self._alloc_sem("d%s%d" % (e, k)) for k in range(NDMASEM)]
            self.dcnt[e] = [0] * NDMASEM
            self.dnext[e] = 0
        self.seen = {e: {} for e in ENGS}
        self.res = {}
        self.n_wait = 0
        self.n_ins = 0

    def _alloc_sem(self, name):
        self.nsem += 1
        return self.stack.enter_context(self.nc.semaphore(name))

    def _new_sem(self, e):
        self.cur_sem[e] = self._alloc_sem("s%s%d" % (e, self.nsem))
        self.cur_cnt[e] = 0

    def _need(self, eng, ev):
        if ev is None:
            return
        sem, val = ev
        k = id(sem)
        if self.seen[eng].get(k, 0) >= val:
            return
        self.seen[eng][k] = val
        self.n_wait += 1
        self.q[eng].append(lambda E, sem=sem, val=val: E.wait_ge(sem, val))

    def _deps(self, eng, reads, writes, acc=False):
        for r in reads:
            st = self.res.get(r)
            if st is not None:
                self._need(eng, st[0])
        for w in writes:
            st = self.res.get(w)
            if st is not None:
                if not (acc and st[2] == eng):
                    self._need(eng, st[0])
                for ev in st[1]:
                    self._need(eng, ev)

    def _record(self, eng, ev, reads, writes):
        for r in reads:
            st = self.res.get(r)
            if st is None:
                st = [None, [], None]
                self.res[r] = st
            st[1].append(ev)
            if len(st[1]) > 10:
                d = {}
                for s, v in st[1]:
                    if id(s) not in d or d[id(s)][1] < v:
                        d[id(s)] = (s, v)
                st[1] = list(d.values())
        for w in writes:
            self.res[w] = [ev, [], eng]

    def op(self, eng, fn, reads=(), writes=(), acc=False):
        self._deps(eng, reads, writes, acc=acc)
        if self.cur_cnt[eng] >= SEM_ROLL:
            self._new_sem(eng)
        sem = self.cur_sem[eng]
        self.cur_cnt[eng] += 1
        val = self.cur_cnt[eng]
        self.n_ins += 1
        self.q[eng].append(lambda E, fn=fn, sem=sem: fn(E).then_inc(sem, 1))
        ev = (sem, val)
        self._record(eng, ev, reads, writes)
        return ev

    def dma(self, eng, out, in_, reads=(), writes=(), **kw):
        k = self.dnext[eng]
        self.dnext[eng] = (k + 1) % NDMASEM
        sem = self.dsem[eng][k]
        if self.dcnt[eng][k] > 0:
            self._need(eng, (sem, self.dcnt[eng][k]))
        self._deps(eng, reads, writes)
        self.dcnt[eng][k] += 16
        val = self.dcnt[eng][k]
        self.n_ins += 1
        self.q[eng].append(
            lambda E, out=out, in_=in_, sem=sem, kw=kw: E.dma_start(out=out, in_=in_, **kw).then_inc(sem, 16))
        ev = (sem, val)
        self._record(eng, ev, reads, writes)
        return ev

    def finish(self, final_events):
        for ev in final_events:
            self._need("sp", ev)
        nc = self.nc
        with nc.Block() as block:
            @block.sync
            def _(E):
                for f in self.q["sp"]:
                    f(E)

            @block.tensor
            def _(E):
                for f in self.q["pe"]:
                    f(E)

            @block.scalar
            def _(E):
                for f in self.q["act"]:
                    f(E)

            @block.vector
            def _(E):
                for f in self.q["dve"]:
                    f(E)

            @block.gpsimd
            def _(E):
                for f in self.q["pool"]:
                    f(E)


class Ring:
    def __init__(self, items):
        self.items = items
        self.i = 0

    def get(self):
        it = self.items[self.i]
        self.i = (self.i + 1) % len(self.items)
        return it


def wt_table():
    slopes = 2.0 ** (-8.0 * np.arange(1, 9) / 8.0)
    db = np.arange(-3, 20)[None, :, None]
    ik = np.arange(128)[:, None, None]
    iq = np.arange(128)[None, None, :]
    delta = 128 * db + iq - ik
    mult = np.zeros(delta.shape, np.float64)
    for win, dil in DIL:
        mult += ((delta >= 0) & (delta <= win) & (delta % dil == 0))
    dpos = np.maximum(delta, 0).astype(np.float64)
    tab = np.stack([mult * np.exp(-s * dpos) for s in slopes], 0)
    return tab.astype(np.float32)


def build(cfg):
    SEQ = cfg["SEQ"]
    TC = cfg["TC"]
    DEPTH = cfg["DEPTH"]
    NSB = cfg["NSB"]
    NPASS = SEQ // TC
    NBLK = TC // 128
    WB = W // 128
    NST = NSB * DS
    NTX = TC + NST
    KEEP = min(W, SEQ)
    NPT = DEPTH * PL + 8
    assert TC % 512 == 0 and SEQ % TC == 0

    nc = bass.Bass("TRN2", target_bir_lowering=False)

    def din(name, shape, dt=F32):
        return nc.dram_tensor(name, list(shape), dt, kind="ExternalInput").ap()

    def dout(name, shape, dt=F32):
        return nc.dram_tensor(name, list(shape), dt, kind="ExternalOutput").ap()

    def dint(name, shape, dt=F32):
        return nc.dram_tensor(name, list(shape), dt, kind=cfg.get("scr_kind", "Internal")).ap()

    xp = din("xp", [SEQ, D])
    xs = din("xs", [NST, D])
    st_h = din("st_h", [DEPTH, NSB, 128, 2])
    st_conv = din("st_conv", [DEPTH, NSB, 128, 2, 3])
    st_hg = din("st_hg", [DEPTH, NSB, 64, 4, 64])
    c_k = din("c_k", [DEPTH, NSB, W, 512])
    c_v = din("c_v", [DEPTH, NSB, W, 512])
    cm_k = din("cm_k", [DEPTH, NSB, NMEM, D])
    cm_v = din("cm_v", [DEPTH, NSB, NMEM, D])
    memp = din("memp", [NMEM, D])
    w = {}
    for nm, shp in [("ffn1_wg", [DEPTH, D, DFF]), ("ffn1_wu", [DEPTH, D, DFF]), ("ffn1_wd", [DEPTH, DFF, D]),
                    ("ffn2_wg", [DEPTH, D, DFF]), ("ffn2_wu", [DEPTH, D, DFF]), ("ffn2_wd", [DEPTH, DFF, D]),
                    ("w_in", [DEPTH, D, DIN]), ("w_out", [DEPTH, D, D]), ("x_wq", [DEPTH, D, D]),
                    ("x_wk", [DEPTH, D, D]), ("x_wv", [DEPTH, D, D]), ("x_wo", [DEPTH, D, D]),
                    ("lru_wbd", [DEPTH, 2, 2, 128, 128])]:
        w[nm] = din(nm, shp)
    ptab_d = din("ptab", [128, NPT])
    wt_d = din("wt_tab", [8, 128, 23 * 128])
    ident_d = din("ident", [128, 128])
    triu_d = din("triu", [64, 512])
    rmask_d = din("rmask", [64, NTX])

    yp = dout("yp", [SEQ, D])
    ys = dout("ys", [NST, D])
    o_h = dout("o_h", [DEPTH, 128, 2])
    o_conv = dout("o_conv", [DEPTH, 128, 2, 3])
    o_hg = dout("o_hg", [DEPTH, 64, 4, 64])
    o_k = dout("o_k", [DEPTH, KEEP, 512])
    o_v = dout("o_v", [DEPTH, KEEP, 512])
    o_mk = dout("o_mk", [DEPTH, NMEM, D])
    o_mv = dout("o_mv", [DEPTH, NMEM, D])
    s_h = dout("s_h", [DEPTH, NSB, 128, 2])
    s_conv = dout("s_conv", [DEPTH, NSB, 128, 2, 3])
    s_hg = dout("s_hg", [DEPTH, NSB, 64, 4, 64])
    s_k = dout("s_k", [DEPTH, NSB, W, 512])
    s_v = dout("s_v", [DEPTH, NSB, W, 512])

    kT_scr = dint("kT_scr", [DEPTH, 512, SEQ], BF16)
    v_scr = dint("v_scr", [DEPTH, 8, SEQ, 64], BF16)
    mkT_scr = dint("mkT_scr", [DEPTH, 128, 8 * NMEM], BF16)
    mv_scr = dint("mv_scr", [DEPTH, 128, 2 * D], BF16)

    st = ExitStack()
    S = Sched(nc, st)
    finals = []

    def sb(name, shape, dt=F32):
        return st.enter_context(nc.sbuf_tensor(name, list(shape), dt))

    def ps(name, shape, dt=F32):
        return st.enter_context(nc.psum_tensor(name, list(shape), dt))

    x = sb("x", [128, 8, NTX])
    h = sb("h", [128, 8, NTX], BF16)
    ybuf = sb("ybuf", [128, 8 * NTX], BF16)
    NWB = 8
    wbufs = [sb("wb%d" % i, [128, WSZ], BF16) for i in range(NWB)]
    wring = Ring(list(range(NWB)))
    actb = [sb("actb%d" % i, [128, 4, 512], BF16) for i in range(2)]
    actring = Ring([0, 1])
    FS = [sb("fs%d" % i, [128, NTX + 8]) for i in range(10)]
    SW = max(NTX + 8, 64 * (TC // 64 + NSB))
    BS = [sb("bs%d" % i, [128, SW], BF16) for i in range(5)]
    t512 = [sb("t512_%d" % i, [128, 512]) for i in range(3)]
    tring = Ring([0, 1, 2])
    b512 = [sb("b512_%d" % i, [128, 512], BF16) for i in range(4)]
    bring = Ring([0, 1, 2, 3])
    vh = sb("vh", [128, WB + NBLK, 64], BF16)
    kT = sb("kT", [64, W + TC], BF16)
    wtb = sb("wtb", [128, 23 * 128], BF16)
    memT = wtb[:, 0:8 * NMEM].rearrange("p (c n) -> p c n", c=8)
    ptab = sb("ptab_sb", [128, NPT])
    dpar = sb("dpar", [128, DEPTH, 16])
    ident_f = sb("ident_f", [128, 128])
    ident_b = sb("ident_b", [128, 128], BF16)
    ones_b = sb("ones_b", [128, 128], BF16)
    triu = sb("triu_sb", [64, 512])
    rmask = sb("rmask_sb", [64, NTX])
    lru_h = sb("lru_h", [128, DEPTH, 2])
    lru_tail = sb("lru_tail", [128, DEPTH, 2, 3])
    hgS = sb("hgS", [64, DEPTH, 4, 64])
    hgSb = sb("hgSb", [64, 17, 64], BF16)
    smp_h = sb("smp_h", [128, NSB, 2])
    smp_tail = sb("smp_tail", [128, NSB, 2, 3])
    smp_S = sb("smp_S", [64, NSB, 4, 64])
    mkT_flat = actb[0][:].rearrange("p a b -> p (a b)")
    mv_flat = actb[1][:].rearrange("p a b -> p (a b)")
    mkT = mkT_flat.rearrange("p (c n) -> p c n", c=8)
    mv = mv_flat.rearrange("p (c n) -> p c n", c=2)
    KMK, KMV, KMT = "actb0", "actb1", "wtb"
    xin = [sb("xin%d" % i, [128, D]) for i in range(1)]
    xinring = Ring([0])

    def sbq(mi, n):
        return BS[1 + mi // 2][:, (mi % 2) * 512:(mi % 2) * 512 + n]

    def sbqk(mi):
        return "bs%d" % (1 + mi // 2)

    psm = [ps("psm%d" % i, [128, 512]) for i in range(4)]
    pring = Ring([0, 1, 2, 3])
    psacc = [ps("psacc%d" % i, [128, 512]) for i in range(2)]
    pst = [ps("pst%d" % i, [128, 1024], BF16) for i in range(2)]
    ptring = Ring([0, 1])

    def P(i):
        return "psm%d" % i

    def act(out, in_, func, reads, writes, scale=1.0, bias=None, eng="act"):
        if bias is None:
            S.op(eng, lambda E: E.activation(out=out, in_=in_, func=func, scale=scale), reads, writes)
        else:
            S.op(eng, lambda E: E.activation(out=out, in_=in_, func=func, scale=scale, bias=bias), reads, writes)

    def tt(out, a, b, op, reads, writes, eng="dve"):
        S.op(eng, lambda E: E.tensor_tensor(out=out, in0=a, in1=b, op=op), reads, writes)

    def ts(out, a, s1, s2, op0, op1, reads, writes, eng="dve"):
        S.op(eng, lambda E: E.tensor_scalar(out=out, in0=a, scalar1=s1, scalar2=s2, op0=op0, op1=op1), reads, writes)

    def stt(out, a, s, b, op0, op1, reads, writes):
        S.op("dve", lambda E: E.scalar_tensor_tensor(out=out, in0=a, scalar=s, in1=b, op0=op0, op1=op1), reads, writes)

    def cp(out, in_, reads, writes, eng="dve"):
        if eng == "act":
            S.op(eng, lambda E: E.activation(out=out, in_=in_, func=AF.Copy), reads, writes)
        else:
            S.op(eng, lambda E: E.tensor_copy(out=out, in_=in_), reads, writes)

    def mm(out, lhsT, rhs, start, stop, reads, writes):
        S.op("pe", lambda E: E.matmul(out, lhsT=lhsT, rhs=rhs, start=start, stop=stop), reads, writes, acc=not start)

    def tr(out, in_, ident, reads, writes):
        S.op("pe", lambda E: E.transpose(out=out, in_=in_, identity=ident), reads, writes)

    def wload(src, kparts, kc, n):
        i = wring.get()
        assert kc * n <= WSZ, (kc, n)
        view = wbufs[i][0:kparts, 0:kc * n].rearrange("p (c n) -> p c n", c=kc)
        S.dma("pool", view, src.rearrange("(c p) n -> p c n", p=kparts), writes=["wb%d" % i])
        return view, "wb%d" % i

    def coltiles(p):
        t = [(c0, 512) for c0 in range(0, TC, 512)]
        if p == 0:
            t.append((TC, NST))
        return t

    S.dma("sp", ptab[:], ptab_d, writes=["ptab"])
    S.dma("sp", ident_f[:], ident_d, writes=["ident_f"])
    S.dma("pool", ident_b[:], ident_d, writes=["ident_b"])
    S.dma("sp", triu[:], triu_d, writes=["triu"])
    S.dma("sp", rmask[:], rmask_d, writes=["rmask"])
    S.op("dve", lambda E: E.memset(ones_b[:], 1.0), writes=["ones_b"])
    S.op("dve", lambda E: E.memset(lru_h[:], 0.0), writes=["lru_h"])
    S.op("dve", lambda E: E.memset(lru_tail[:], 0.0), writes=["lru_tail"])
    S.op("dve", lambda E: E.memset(hgS[:], 0.0), writes=["hgS"])
    S.op("dve", lambda E: E.memset(kT[:], 0.0), writes=["kT"])
    S.op("pool", lambda E: E.memset(vh[:], 0.0), writes=["vh"])

    def pcol(l, off, n=1):
        return ptab[:, l * PL + off: l * PL + off + n]

    for l in range(DEPTH):
        lam = pcol(l, 46, 2)
        e_ = FS[0][:, 0:2]
        z_ = FS[0][:, 2:4]
        z2 = FS[0][:, 4:6]
        pl_ = FS[0][:, 6:8]
        act(e_, lam, AF.Exp, ["ptab"], ["fs0"], scale=-1.0)
        ts(z_, e_, 2.0, None, ALU.add, ALU.bypass, ["fs0"], ["fs0"])
        S.op("dve", lambda E, z_=z_: E.reciprocal(out=z_, in_=z_), ["fs0"], ["fs0"])
        tt(z_, z_, e_, ALU.mult, ["fs0"], ["fs0"])
        tt(z2, z_, z_, ALU.mult, ["fs0"], ["fs0"])
        ts(pl_, z2, 1.0 / 7.0, 1.0 / 5.0, ALU.mult, ALU.add, ["fs0"], ["fs0"])
        tt(pl_, pl_, z2, ALU.mult, ["fs0"], ["fs0"])
        ts(pl_, pl_, 1.0 / 3.0, None, ALU.add, ALU.bypass, ["fs0"], ["fs0"])
        tt(pl_, pl_, z2, ALU.mult, ["fs0"], ["fs0"])
        ts(pl_, pl_, 1.0, None, ALU.add, ALU.bypass, ["fs0"], ["fs0"])
        tt(pl_, pl_, z_, ALU.mult, ["fs0"], ["fs0"])
        ts(dpar[:, l, 0:2], pl_, -16.0, None, ALU.mult, ALU.bypass, ["fs0"], ["dpar"])
        ts(dpar[:, l, 2:4], pl_, -32.0, None, ALU.mult, ALU.bypass, ["fs0"], ["dpar"])
    esum = FS[1][0:64, 0:4]
    ecum = FS[1][0:64, 4:8]
    for l in range(DEPTH):
        el = FS[1][0:64, 8 + 4 * l: 12 + 4 * l]
        act(el, ptab[0:64, l * PL + 59: l * PL + 63], AF.Exp, ["ptab"], ["fs1"])
        if l == 0:
            cp(esum, el, ["fs1"], ["fs1"])
        else:
            tt(esum, esum, el, ALU.add, ["fs1"], ["fs1"])
    S.op("dve", lambda E: E.reciprocal(out=esum, in_=esum), ["fs1"], ["fs1"])
    S.op("dve", lambda E: E.memset(ecum, 0.0), ["fs1"], ["fs1"])
    for l in range(DEPTH):
        el = FS[1][0:64, 8 + 4 * l: 12 + 4 * l]
        if l > 0:
            tt(ecum, ecum, el, ALU.add, ["fs1"], ["fs1"])
        tt(dpar[0:64, l, 4:8], ecum, esum, ALU.mult, ["fs1"], ["dpar"])
        ts(dpar[0:64, l, 8:12], dpar[0:64, l, 4:8], -1.0, 1.0, ALU.mult, ALU.add, ["dpar"], ["dpar"])

    def rmsnorm(l_off, p, out_fn, out_key, cts):
        okey = out_key if callable(out_key) else (lambda c: out_key)
        for (c0, n) in cts:
            pi = pring.get()
            for c in range(8):
                bi = bring.get()
                act(b512[bi][:, 0:n], x[:, c, c0:c0 + n], AF.Square, ["x"], ["b512_%d" % bi])
                mm(psm[pi][:, 0:n], ones_b[:], b512[bi][:, 0:n], c == 0, c == 7, ["ones_b", "b512_%d" % bi], [P(pi)])
            ti = tring.get()
            rs = t512[ti][:, 0:n]
            ts(rs, psm[pi][:, 0:n], 1.0 / D, EPS, ALU.mult, ALU.add, [P(pi)], ["t512_%d" % ti])
            act(rs, rs, AF.Sqrt, ["t512_%d" % ti], ["t512_%d" % ti])
            S.op("dve", lambda E, rs=rs: E.reciprocal(out=rs, in_=rs), ["t512_%d" % ti], ["t512_%d" % ti])
            for c in range(8):
                stt(out_fn(c, c0, n), x[:, c, c0:c0 + n], ptab[:, l_off + c: l_off + c + 1], rs,
                    ALU.mult, ALU.mult, ["x", "ptab", "t512_%d" % ti], [okey(c)])

    def hview(c, c0, n):
        return h[:, c, c0:c0 + n]

    def ffn(l, which, p):
        cts = coltiles(p)
        rmsnorm(l * PL + (0 if which == 1 else 24), p, hview, "h", cts)
        wg, wu, wd = w["ffn%d_wg" % which], w["ffn%d_wu" % which], w["ffn%d_wd" % which]
        nch = DFF // 128
        for g0 in range(0, nch, GSZ):
            gs = min(GSZ, nch - g0)
            wgv, wgk = wload(wg[l][:, g0 * 128:(g0 + gs) * 128], 128, 8, gs * 128)
            wuv, wuk = wload(wu[l][:, g0 * 128:(g0 + gs) * 128], 128, 8, gs * 128)
            wdv, wdk = wload(wd[l][g0 * 128:(g0 + gs) * 128, :], 128, gs, D)
            for (c0, n) in cts:
                ai = actring.get()
                ak = "actb%d" % ai
                for j in range(gs):
                    pg = pring.get()
                    for c in range(8):
                        mm(psm[pg][:, 0:n], wgv[:, c, j * 128:(j + 1) * 128], h[:, c, c0:c0 + n], c == 0, c == 7,
                           [wgk, "h"], [P(pg)])
                    pu = pring.get()
                    for c in range(8):
                        mm(psm[pu][:, 0:n], wuv[:, c, j * 128:(j + 1) * 128], h[:, c, c0:c0 + n], c == 0, c == 7,
                           [wuk, "h"], [P(pu)])
                    ti = tring.get()
                    act(t512[ti][:, 0:n], psm[pg][:, 0:n], AF.Silu, [P(pg)], ["t512_%d" % ti])
                    tt(actb[ai][:, j, 0:n], t512[ti][:, 0:n], psm[pu][:, 0:n], ALU.mult,
                       ["t512_%d" % ti, P(pu)], [ak])
                for m in range(8):
                    pd = pring.get()
                    for j in range(gs):
                        mm(psm[pd][:, 0:n], wdv[:, j, m * 128:(m + 1) * 128], actb[ai][:, j, 0:n], j == 0, j == gs - 1,
                           [wdk, ak], [P(pd)])
                    stt(x[:, m, c0:c0 + n], psm[pd][:, 0:n], 0.5, x[:, m, c0:c0 + n], ALU.mult, ALU.add,
                        [P(pd), "x"], ["x"])

    def proj_fm(wsrc, kparts, kc, mtot, msz, rhs_fn, rhs_keys, cts, sink):
        wpiece = WSZ // kc
        wpiece = (wpiece // msz) * msz
        for m0 in range(0, mtot, wpiece):
            mw = min(wpiece, mtot - m0)
            wv, wk = wload(wsrc[:, m0:m0 + mw], kparts, kc, mw)
            for mi in range(mw // msz):
                for (c0, n) in cts:
                    pi = pring.get()
                    for c in range(kc):
                        mm(psm[pi][0:msz, 0:n], wv[:, c, mi * msz:(mi + 1) * msz], rhs_fn(c, c0, n), c == 0, c == kc - 1,
                           [wk] + rhs_keys, [P(pi)])
                    sink(m0 // msz + mi, c0, n, psm[pi][0:msz, 0:n], P(pi))

    def resid_add(scale):
        def sink(mi, c0, n, pap, pk):
            if scale == 1.0:
                tt(x[:, mi, c0:c0 + n], pap, x[:, mi, c0:c0 + n], ALU.add, [pk, "x"], ["x"])
            else:
                stt(x[:, mi, c0:c0 + n], pap, scale, x[:, mi, c0:c0 + n], ALU.mult, ALU.add, [pk, "x"], ["x"])
        return sink

    def load_x(p):
        for b in range(NBLK):
            xi = xinring.get()
            S.dma("sp", xin[xi][:], xp[p * TC + b * 128: p * TC + (b + 1) * 128, :], writes=["xin%d" % xi])
            for c4 in range(2):
                pi = pring.get()
                for cc in range(4):
                    c = c4 * 4 + cc
                    tr(psm[pi][:, cc * 128:(cc + 1) * 128], xin[xi][:, c * 128:(c + 1) * 128], ident_f[:],
                       ["xin%d" % xi, "ident_f"], [P(pi)])
                cp(x[:, c4 * 4:(c4 + 1) * 4, b * 128:(b + 1) * 128],
                   psm[pi][:].rearrange("p (c n) -> p c n", c=4), [P(pi)], ["x"], eng="act" if c4 else "dve")
        if p == 0:
            xi = xinring.get()
            S.dma("sp", xin[xi][0:NST, :], xs, writes=["xin%d" % xi])
            for c4 in range(2):
                pi = pring.get()
                for cc in range(4):
                    c = c4 * 4 + cc
                    tr(psm[pi][:, cc * 128: cc * 128 + NST], xin[xi][0:NST, c * 128:(c + 1) * 128],
                       ident_f[0:NST, 0:NST], ["xin%d" % xi, "ident_f"], [P(pi)])
                cp(x[:, c4 * 4:(c4 + 1) * 4, TC:TC + NST],
                   psm[pi][:].rearrange("p (c n) -> p c n", c=4)[:, :, 0:NST], [P(pi)], ["x"])

    def store_y(p):
        cts = coltiles(p)
        yf = FS
        rmsnorm(DEPTH * PL, p, lambda c, c0, n: yf[c][:, c0:c0 + n], lambda c: "fs%d" % c, cts)
        nb = NBLK + (1 if p == 0 else 0)
        for b in range(nb):
            ntok = 128 if b < NBLK else NST
            xi = xinring.get()
            for c4 in range(2):
                pi = pring.get()
                for cc in range(4):
                    c = c4 * 4 + cc
                    tr(psm[pi][0:ntok, cc * 128:(cc + 1) * 128], yf[c][:, b * 128: b * 128 + ntok], ident_f[:],
                       ["fs%d" % c, "ident_f"], [P(pi)])
                cp(xin[xi][0:ntok, c4 * 512:(c4 + 1) * 512], psm[pi][0:ntok, :], [P(pi)], ["xin%d" % xi],
                   eng="act" if c4 else "dve")
            if b < NBLK:
                ev = S.dma("sp", yp[p * TC + b * 128: p * TC + (b + 1) * 128, :], xin[xi][:], reads=["xin%d" % xi])
            else:
                ev = S.dma("sp", ys, xin[xi][0:NST, :], reads=["xin%d" % xi])
            finals.append(ev)

    def mixer_a(l, p):
        cts = coltiles(p)
        win = w["w_in"][l]
        segs = [("p", 0, TC, None)]
        if p == 0:
            segs += [("s", TC + DS * b, DS, b) for b in range(NSB)]
        yA = ybuf[:, 0:2 * NTX].rearrange("p (c n) -> p c n", c=2)
        ypre = [FS[8], FS[9]]
        for pt in range(2):
            xaext, xc, r_, i_, a_, u_, hh, ga_, tmp = FS[0], FS[1], FS[2], FS[3], FS[4], FS[5], FS[6], FS[7], ypre[pt]
            xcb = BS[0]
            K = lambda i: "fs%d" % i
            wxa, wxak = wload(win[:, pt * 128:(pt + 1) * 128], 128, 8, 128)
            wga, wgak = wload(win[:, 256 + pt * 128: 256 + (pt + 1) * 128], 128, 8, 128)
            for (c0, n) in cts:
                pi = pring.get()
                for c in range(8):
                    mm(psm[pi][:, 0:n], wxa[:, c, :], h[:, c, c0:c0 + n], c == 0, c == 7, [wxak, "h"], [P(pi)])
                if c0 < TC:
                    cp(xaext[:, 3 + c0: 3 + c0 + n], psm[pi][:, 0:n], [P(pi)], [K(0)], eng="act")
                else:
                    cp(tmp[:, 0:n], psm[pi][:, 0:n], [P(pi)], [K(8 + pt)], eng="act")
                pj = pring.get()
                for c in range(8):
                    mm(psm[pj][:, 0:n], wga[:, c, :], h[:, c, c0:c0 + n], c == 0, c == 7, [wgak, "h"], [P(pj)])
                cp(ga_[:, c0:c0 + n], psm[pj][:, 0:n], [P(pj)], [K(7)], eng="act")
            cw = lambda tap: pcol(l, 32 + pt * 4 + tap)
            cb = pcol(l, 40 + pt)
            for (kind, c0, T, b) in segs:
                if kind == "p":
                    cp(xaext[:, 0:3], lru_tail[:, l, pt, :], ["lru_tail"], [K(0)])
                    src = xaext
                    so = 0
                else:
                    src = a_
                    so = 16 * b
                    cp(src[:, so:so + 3], smp_tail[:, b, pt, :], ["smp_tail"], [K(4)])
                    cp(src[:, so + 3:so + 3 + T], tmp[:, DS * b: DS * b + T], [K(8 + pt)], [K(4)])
                sk = K(0) if kind == "p" else K(4)
                ts(xc[:, c0:c0 + T], src[:, so:so + T], cw(0), cb, ALU.mult, ALU.add, [sk, "ptab"], [K(1)])
                for tap in range(1, 4):
                    stt(xc[:, c0:c0 + T], src[:, so + tap:so + tap + T], cw(tap), xc[:, c0:c0 + T], ALU.mult, ALU.add,
                        [sk, "ptab", K(1)], [K(1)])
                if kind == "p":
                    cp(lru_tail[:, l, pt, :], xaext[:, T:T + 3], [K(0)], ["lru_tail"])
                else:
                    cp(smp_tail[:, b, pt, :], src[:, so + T:so + T + 3], [K(4)], ["smp_tail"])
            ncols = TC + (NST if p == 0 else 0)
            cp(xcb[:, 0:ncols], xc[:, 0:ncols], [K(1)], ["bs0"], eng="act")
            wa, wak = wload(w["lru_wbd"][l, 0, pt], 128, 1, 128)
            wx, wxk = wload(w["lru_wbd"][l, 1, pt], 128, 1, 128)
            for (c0, n) in cts:
                pi = pring.get()
                mm(psm[pi][:, 0:n], wa[:, 0, :], xcb[:, c0:c0 + n], True, True, [wak, "bs0"], [P(pi)])
                act(r_[:, c0:c0 + n], psm[pi][:, 0:n], AF.Sigmoid, [P(pi), "ptab"], [K(2)], bias=pcol(l, 42 + pt))
                pj = pring.get()
                mm(psm[pj][:, 0:n], wx[:, 0, :], xcb[:, c0:c0 + n], True, True, [wxk, "bs0"], [P(pj)])
                act(i_[:, c0:c0 + n], psm[pj][:, 0:n], AF.Sigmoid, [P(pj), "ptab"], [K(3)], bias=pcol(l, 44 + pt))
            A = slice(0, ncols)
            act(a_[:, A], r_[:, A], AF.Exp, [K(2), "dpar"], [K(4)], scale=dpar[:, l, pt:pt + 1])
            y_ = u_
            ts(y_[:, A], r_[:, A], dpar[:, l, 2 + pt:3 + pt], None, ALU.mult, ALU.bypass, [K(2), "dpar"], [K(5)])
            pol = r_
            ts(pol[:, A], y_[:, A], 1.0 / 720.0, 1.0 / 120.0, ALU.mult, ALU.add, [K(5)], [K(2)])
            for coef in (1.0 / 24.0, 1.0 / 6.0, 0.5, 1.0):
                tt(pol[:, A], pol[:, A], y_[:, A], ALU.mult, [K(2), K(5)], [K(2)])
                ts(pol[:, A], pol[:, A], coef, None, ALU.add, ALU.bypass, [K(2)], [K(2)])
            stt(pol[:, A], pol[:, A], -1.0, y_[:, A], ALU.mult, ALU.mult, [K(2), K(5)], [K(2)])
            ts(pol[:, A], pol[:, A], 0.0, None, ALU.max, ALU.bypass, [K(2)], [K(2)])
            act(pol[:, A], pol[:, A], AF.Sqrt, [K(2)], [K(2)])
            tt(u_[:, A], i_[:, A], xc[:, A], ALU.mult, [K(3), K(1)], [K(5)])
            tt(u_[:, A], u_[:, A], pol[:, A], ALU.mult, [K(5), K(2)], [K(5)])
            for (kind, c0, T, b) in segs:
                init = lru_h[:, l, pt:pt + 1] if kind == "p" else smp_h[:, b, pt:pt + 1]
                ik = "lru_h" if kind == "p" else "smp_h"
                S.op("dve", lambda E, c0=c0, T=T, init=init: E.tensor_tensor_scan(
                    out=hh[:, c0:c0 + T], data0=a_[:, c0:c0 + T], data1=u_[:, c0:c0 + T], initial=init,
                    op0=ALU.mult, op1=ALU.add), [K(4), K(5), ik], [K(6)])
                cp(init, hh[:, c0 + T - 1:c0 + T], [K(6)], [ik])
            g2 = i_
            tt(g2[:, A], ga_[:, A], ga_[:, A], ALU.mult, [K(7)], [K(3)])
            ts(g2[:, A], g2[:, A], 0.044715, 1.0, ALU.mult, ALU.add, [K(3)], [K(3)])
            tt(g2[:, A], g2[:, A], ga_[:, A], ALU.mult, [K(3), K(7)], [K(3)])
            act(g2[:, A], g2[:, A], AF.Sigmoid, [K(3)], [K(3)], scale=1.5957691216057308)
            tt(g2[:, A], g2[:, A], ga_[:, A], ALU.mult, [K(3), K(7)], [K(3)])
            tt(tmp[:, A], hh[:, A], g2[:, A], ALU.mult, [K(6), K(3)], [K(8 + pt)])
        for (c0, n) in cts:
            pi = pring.get()
            for pt in range(2):
                bi = bring.get()
                act(b512[bi][:, 0:n], ypre[pt][:, c0:c0 + n], AF.Square, ["fs%d" % (8 + pt)], ["b512_%d" % bi])
                mm(psm[pi][:, 0:n], ones_b[:], b512[bi][:, 0:n], pt == 0, pt == 1, ["ones_b", "b512_%d" % bi], [P(pi)])
            ti = tring.get()
            rs = t512[ti][:, 0:n]
            ts(rs, psm[pi][:, 0:n], 1.0 / 256.0, EPS, ALU.mult, ALU.add, [P(pi)], ["t512_%d" % ti])
            act(rs, rs, AF.Sqrt, ["t512_%d" % ti], ["t512_%d" % ti])
            S.op("dve", lambda E, rs=rs: E.reciprocal(out=rs, in_=rs), ["t512_%d" % ti], ["t512_%d" % ti])
            for pt in range(2):
                stt(yA[:, pt, c0:c0 + n], ypre[pt][:, c0:c0 + n], pcol(l, 48 + pt), rs, ALU.mult, ALU.mult,
                    ["fs%d" % (8 + pt), "ptab", "t512_%d" % ti], ["ybuf"])
        proj_fm(w["w_out"][l][0:256, :], 128, 2, D, 128, lambda c, c0, n: yA[:, c, c0:c0 + n], ["ybuf"], cts,
                resid_add(1.0))
        if p == NPASS - 1:
            finals.append(S.dma("sp", o_h[l], lru_h[:, l, :], reads=["lru_h"]))
            finals.append(S.dma("sp", o_conv[l], lru_tail[:, l, :, :], reads=["lru_tail"]))

    def mixer_b(l, p):
        cts = coltiles(p)
        win = w["w_in"][l]
        yB = ybuf[0:64, 0:4 * NTX].rearrange("p (c n) -> p c n", c=4)
        ncols = TC + (NST if p == 0 else 0)
        A = slice(0, ncols)
        K = lambda i: "fs%d" % i
        segs = [("p", 0, TC, None, 64)]
        if p == 0:
            segs += [("s", TC + DS * b, DS, b, DS) for b in range(NSB)]
        for hd in range(4):
            q_, f_, cum, ec, en, k_, g_, oT = FS[0], FS[1], FS[2], FS[3], FS[4], FS[5], FS[6], FS[7]
            qt, kt, khat = BS[0], BS[1], BS[2]
            vtm = BS[3]
            khtm = BS[4]

            def fm_proj(col0, dst, dk, func=None, bias=None, scale=1.0):
                wv, wk = wload(win[:, col0 + hd * 64: col0 + (hd + 1) * 64], 128, 8, 64)
                for (c0, n) in cts:
                    pi = pring.get()
                    for c in range(8):
                        mm(psm[pi][0:64, 0:n], wv[:, c, :], h[:, c, c0:c0 + n], c == 0, c == 7, [wk, "h"], [P(pi)])
                    if func is None:
                        cp(dst[0:64, c0:c0 + n], psm[pi][0:64, 0:n], [P(pi)], [dk], eng="act")
                    else:
                        act(dst[0:64, c0:c0 + n], psm[pi][0:64, 0:n], func, [P(pi)], [dk])
            fm_proj(512, q_, K(0))
            fm_proj(768, f_, K(1), func=AF.Sigmoid)
            fm_proj(1280, g_, K(6))
            ts(f_[0:64, A], f_[0:64, A], dpar[0:64, l, 8 + hd:9 + hd], dpar[0:64, l, 4 + hd:5 + hd], ALU.mult, ALU.add,
               [K(1), "dpar"], [K(1)])
            ts(k_[0:64, A], f_[0:64, A], -1.0, 1.0, ALU.mult, ALU.add, [K(1)], [K(5)])
            act(f_[0:64, A], f_[0:64, A], AF.Ln, [K(1)], [K(1)])
            S.op("dve", lambda E: E.tensor_tensor_scan(out=cum[0:64, A], data0=rmask[0:64, A], data1=f_[0:64, A],
                                                       initial=0.0, op0=ALU.mult, op1=ALU.add),
                 ["rmask", K(1)], [K(2)])
            act(ec[0:64, A], cum[0:64, A], AF.Exp, [K(2)], [K(3)])
            act(en[0:64, A], cum[0:64, A], AF.Exp, [K(2)], [K(4)], scale=-1.0)
            tt(qt[0:64, A], q_[0:64, A], ec[0:64, A], ALU.mult, [K(0), K(3)], ["bs0"])
            tt(k_[0:64, A], k_[0:64, A], en[0:64, A], ALU.mult, [K(5), K(4)], [K(5)])
            cp(kt[0:64, A], k_[0:64, A], [K(5)], ["bs1"], eng="act")
            wv, wvk = wload(win[:, 1024 + hd * 64: 1024 + (hd + 1) * 64], 128, 8, 64)
            chunks = []
            for (kind, c0, T, b, C) in segs:
                for j in range(T // C):
                    chunks.append((kind, c0 + j * C, C, b, j == T // C - 1))
            for gi in range(0, len(chunks), 8):
                grp = chunks[gi:gi + 8]
                pi = pring.get()
                for jj, (kind, cc, C, b, last) in enumerate(grp):
                    for c in range(8):
                        mm(psm[pi][0:C, jj * 64:(jj + 1) * 64], h[:, c, cc:cc + C], wv[:, c, :], c == 0, c == 7,
                           ["h", wvk], [P(pi)])
                Cg = grp[0][2]
                cp(vtm[0:Cg, gi * 64:(gi + len(grp)) * 64], psm[pi][0:Cg, 0:len(grp) * 64], [P(pi)], ["bs3"], eng="act")
            for ci, (kind, cc, C, b, last) in enumerate(chunks):
                ts(khat[0:64, cc:cc + C], k_[0:64, cc:cc + C], ec[0:64, cc + C - 1:cc + C], None, ALU.mult, ALU.bypass,
                   [K(5), K(3)], ["bs2"])
            for gi in range(0, len(chunks), 8):
                grp = chunks[gi:gi + 8]
                ti = ptring.get()
                for jj, (kind, cc, C, b, last) in enumerate(grp):
                    tr(pst[ti][0:C, jj * 64:(jj + 1) * 64], khat[0:64, cc:cc + C], ident_b[0:64, 0:64],
                       ["bs2", "ident_b"], ["pst%d" % ti])
                Cg = grp[0][2]
                cp(khtm[0:Cg, gi * 64:(gi + len(grp)) * 64], pst[ti][0:Cg, 0:len(grp) * 64], ["pst%d" % ti], ["bs4"])
            for gi in range(0, len(chunks), 8):
                grp = chunks[gi:gi + 8]
                Cg = grp[0][2]
                ng = len(grp)
                pds = pring.get()
                for jj in range(ng):
                    ci = gi + jj
                    mm(psm[pds][0:64, jj * 64:(jj + 1) * 64], khtm[0:Cg, ci * 64:(ci + 1) * 64],
                       vtm[0:Cg, ci * 64:(ci + 1) * 64], True, True, ["bs4", "bs3"], [P(pds)])
                pin = pring.get()
                for jj, (kind, cc, C, b, last) in enumerate(grp):
                    mm(psm[pin][0:C, jj * 64: jj * 64 + C], kt[0:64, cc:cc + C], qt[0:64, cc:cc + C], True, True,
                       ["bs1", "bs0"], [P(pin)])
                bi = bring.get()
                AT = b512[bi]
                if Cg == 64:
                    tt(AT[0:64, 0:ng * 64], psm[pin][0:64, 0:ng * 64], triu[0:64, 0:ng * 64], ALU.mult,
                       [P(pin), "triu"], ["b512_%d" % bi])
                else:
                    for jj in range(ng):
                        tt(AT[0:Cg, jj * 64: jj * 64 + Cg], psm[pin][0:Cg, jj * 64: jj * 64 + Cg], triu[0:Cg, 0:Cg],
                           ALU.mult, [P(pin), "triu"], ["b512_%d" % bi])
                for jj, (kind, cc, C, b, last) in enumerate(grp):
                    if kind == "p":
                        Sst = hgS[:, l, hd, :]
                        sk = "hgS"
                    else:
                        Sst = smp_S[:, b, hd, :]
                        sk = "smp_S"
                    cp(hgSb[:, jj, :], Sst, [sk], ["hgSb"], eng="act")
                    stt(Sst, Sst, ec[0:64, cc + C - 1:cc + C], psm[pds][0:64, jj * 64:(jj + 1) * 64], ALU.mult, ALU.add,
                        [sk, K(3), P(pds)], [sk])
                po = pring.get()
                for jj, (kind, cc, C, b, last) in enumerate(grp):
                    ci = gi + jj
                    mm(psm[po][0:64, jj * 64: jj * 64 + C], vtm[0:C, ci * 64:(ci + 1) * 64], AT[0:C, jj * 64: jj * 64 + C],
                       True, False, ["bs3", "b512_%d" % bi], [P(po)])
                    mm(psm[po][0:64, jj * 64: jj * 64 + C], hgSb[:, jj, :], qt[0:64, cc:cc + C], False, True,
                       ["hgSb", "bs0"], [P(po)])
                if Cg == 64:
                    cc0 = grp[0][1]
                    cp(oT[0:64, cc0:cc0 + ng * 64], psm[po][0:64, 0:ng * 64], [P(po)], [K(7)], eng="act")
                else:
                    for jj, (kind, cc, C, b, last) in enumerate(grp):
                        cp(oT[0:64, cc:cc + C], psm[po][0:64, jj * 64: jj * 64 + C], [P(po)], [K(7)], eng="act")
            for (c0, n) in cts:
                bi = bring.get()
                act(b512[bi][0:64, 0:n], oT[0:64, c0:c0 + n], AF.Square, [K(7)], ["b512_%d" % bi])
                pi = pring.get()
                mm(psm[pi][0:64, 0:n], ones_b[0:64, 0:64], b512[bi][0:64, 0:n], True, True, ["ones_b", "b512_%d" % bi],
                   [P(pi)])
                ti = tring.get()
                rs = t512[ti][0:64, 0:n]
                ts(rs, psm[pi][0:64, 0:n], 1.0 / 64.0, EPS, ALU.mult, ALU.add, [P(pi)], ["t512_%d" % ti])
                act(rs, rs, AF.Sqrt, ["t512_%d" % ti], ["t512_%d" % ti])
                S.op("dve", lambda E, rs=rs: E.reciprocal(out=rs, in_=rs), ["t512_%d" % ti], ["t512_%d" % ti])
                stt(oT[0:64, c0:c0 + n], oT[0:64, c0:c0 + n], ptab[0:64, l * PL + 50: l * PL + 51], rs, ALU.mult, ALU.mult,
                    [K(7), "ptab", "t512_%d" % ti], [K(7)])
                tj = tring.get()
                act(t512[tj][0:64, 0:n], g_[0:64, c0:c0 + n], AF.Silu, [K(6)], ["t512_%d" % tj])
                tt(yB[:, hd, c0:c0 + n], oT[0:64, c0:c0 + n], t512[tj][0:64, 0:n], ALU.mult, [K(7), "t512_%d" % tj],
                   ["ybuf"])
        proj_fm(w["w_out"][l][256:512, :], 64, 4, D, 128, lambda c, c0, n: yB[:, c, c0:c0 + n], ["ybuf"], cts,
                resid_add(1.0))
        if p == NPASS - 1:
            finals.append(S.dma("sp", o_hg[l], hgS[:, l, :, :], reads=["hgS"]))

    def attn_tile(qT_ap, qk, n, kbs, yout, accw):
        nk = len(kbs)
        LOOK = 2
        sc = [None] * nk

        def emit_qk(i):
            ka, kk, va, vk, wa = kbs[i]
            pi = pring.get()
            mm(psm[pi][:, 0:n], ka, qT_ap, True, True, [kk, qk], [P(pi)])
            bi = bring.get()
            pt_ = b512[bi][:, 0:n]
            act(pt_, psm[pi][:, 0:n], AF.Exp, [P(pi)], ["b512_%d" % bi], scale=0.125)
            tt(pt_, pt_, wa, ALU.mult, ["b512_%d" % bi, "wtb"], ["b512_%d" % bi])
            sc[i] = (pt_, bi)
        for i in range(min(LOOK, nk)):
            emit_qk(i)
        for i in range(nk):
            if i + LOOK < nk:
                emit_qk(i + LOOK)
            ka, kk, va, vk, wa = kbs[i]
            pt_, bi = sc[i]
            mm(psacc[0][0:64, 0:n], va, pt_, i == 0, i == nk - 1, [vk, "b512_%d" % bi], ["psacc0"])
            mm(psacc[1][0:64, 0:n], ones_b[:, 0:64], pt_, i == 0, i == nk - 1, ["ones_b", "b512_%d" % bi], ["psacc1"])
        ti = tring.get()
        rd = t512[ti][0:64, 0:n]
        S.op("dve", lambda E: E.reciprocal(out=rd, in_=psacc[1][0:64, 0:n]), ["psacc1"], ["t512_%d" % ti])
        tt(yout, psacc[0][0:64, 0:n], rd, ALU.mult, ["psacc0", "t512_%d" % ti], [accw])

    def mixer_c(l, p):
        cts = coltiles(p)
        pcts = [(c0, n) for (c0, n) in cts if c0 < TC]
        win = w["w_in"][l]
        t0 = p * TC
        hist = min(W, t0)
        hb = hist // 128
        yC = FS
        yCb = ybuf[0:64, 0:8 * NTX].rearrange("p (c n) -> p c n", c=8)
        emit_kv = (t0 + TC > SEQ - KEEP)
        if emit_kv or p == 0:
            wvp = [wload(win[:, 2560 + q * 256: 2560 + (q + 1) * 256], 128, 8, 256) for q in range(2)]
            wkp = [wload(win[:, 2048 + q * 256: 2048 + (q + 1) * 256], 128, 8, 256) for q in range(2)]

        def kv_tok(pi, lo, hi, pieces, rows):
            for q, (wv_, wk_) in enumerate(pieces):
                for c in range(8):
                    mm(psm[pi][0:rows, q * 256:(q + 1) * 256], h[:, c, lo:hi], wv_[:, c, :], c == 0, c == 7, ["h", wk_],
                       [P(pi)])
        if emit_kv:
            for b in range(NBLK):
                row = t0 + b * 128 - (SEQ - KEEP)
                xi = xinring.get()
                pi = pring.get()
                kv_tok(pi, b * 128, (b + 1) * 128, wvp, 128)
                cp(xin[xi][:, 0:512], psm[pi][:, :], [P(pi)], ["xin%d" % xi])
                pj = pring.get()
                kv_tok(pj, b * 128, (b + 1) * 128, wkp, 128)
                cp(xin[xi][:, 512:1024], psm[pj][:, :], [P(pj)], ["xin%d" % xi], eng="act")
                finals.append(S.dma("sp", o_v[l][row:row + 128, :], xin[xi][:, 0:512], reads=["xin%d" % xi]))
                finals.append(S.dma("sp", o_k[l][row:row + 128, :], xin[xi][:, 512:1024], reads=["xin%d" % xi]))
        if p == 0:
            skv = FS[9]
            pi = pring.get()
            kv_tok(pi, TC, TC + NST, wkp, NST)
            cp(skv[0:NST, 0:512], psm[pi][0:NST, :], [P(pi)], ["fs9"])
            pj = pring.get()
            kv_tok(pj, TC, TC + NST, wvp, NST)
            cp(skv[0:NST, 512:1024], psm[pj][0:NST, :], [P(pj)], ["fs9"], eng="act")
            for b in range(NSB):
                finals.append(S.dma("sp", s_k[l, b, W - DS:W, :], skv[b * DS:(b + 1) * DS, 0:512], reads=["fs9"]))
                finals.append(S.dma("sp", s_v[l, b, W - DS:W, :], skv[b * DS:(b + 1) * DS, 512:1024], reads=["fs9"]))
                finals.append(S.dma("act", s_k[l, b, 0:W - DS, :], c_k[l, b, DS:W, :]))
                finals.append(S.dma("act", s_v[l, b, 0:W - DS, :], c_v[l, b, DS:W, :]))
        for hd in range(8):
            qT = BS[0]
            S.dma("pool", wtb[:], wt_d[hd], writes=["wtb"])
            wq, wqk = wload(win[:, 1536 + hd * 64: 1536 + (hd + 1) * 64], 128, 8, 64)
            wkh, wkhk = wload(win[:, 2048 + hd * 64: 2048 + (hd + 1) * 64], 128, 8, 64)
            wvh, wvhk = wload(win[:, 2560 + hd * 64: 2560 + (hd + 1) * 64], 128, 8, 64)
            if hb > 0:
                S.dma("sp", kT[:, W - hist:W], kT_scr[l][hd * 64:(hd + 1) * 64, t0 - hist:t0],
                      reads=["kT_scr%d" % l], writes=["kT"])
                S.dma("sp", vh[:, WB - hb:WB, :], v_scr[l][hd, t0 - hist:t0, :].rearrange("(b p) d -> p b d", p=128),
                      reads=["v_scr%d" % l], writes=["vh"])
            for b0 in range(0, NBLK, 8):
                pi = pring.get()
                for bb in range(8):
                    b = b0 + bb
                    for c in range(8):
                        mm(psm[pi][:, bb * 64:(bb + 1) * 64], h[:, c, b * 128:(b + 1) * 128], wvh[:, c, :], c == 0, c == 7,
                           ["h", wvhk], [P(pi)])
                cp(vh[:, WB + b0:WB + b0 + 8, :], psm[pi][:].rearrange("p (b d) -> p b d", b=8), [P(pi)], ["vh"], eng="act")
            S.dma("sp", v_scr[l][hd, t0:t0 + TC, :].rearrange("(b p) d -> p b d", p=128), vh[:, WB:WB + NBLK, :],
                  reads=["vh"], writes=["v_scr%d" % l])
            for (c0, n) in pcts:
                pi = pring.get()
                for c in range(8):
                    mm(psm[pi][0:64, 0:n], wq[:, c, :], h[:, c, c0:c0 + n], c == 0, c == 7, [wqk, "h"], [P(pi)])
                cp(qT[0:64, c0:c0 + n], psm[pi][0:64, 0:n], [P(pi)], ["bs0"], eng="act")
                pj = pring.get()
                for c in range(8):
                    mm(psm[pj][0:64, 0:n], wkh[:, c, :], h[:, c, c0:c0 + n], c == 0, c == 7, [wkhk, "h"], [P(pj)])
                cp(kT[:, W + c0: W + c0 + n], psm[pj][0:64, 0:n], [P(pj)], ["kT"])
            S.dma("sp", kT_scr[l][hd * 64:(hd + 1) * 64, t0:t0 + TC], kT[:, W:W + TC], reads=["kT"],
                  writes=["kT_scr%d" % l])
            for (c0, n) in pcts:
                qb0 = c0 // 128
                kbs = []
                for kb in range(qb0 - WB, qb0 + 4):
                    if kb < -hb:
                        continue
                    col = W + kb * 128
                    dbi = qb0 - kb + 3
                    kbs.append((kT[:, col:col + 128], "kT", vh[:, WB + kb, :], "vh",
                                wtb[:, dbi * 128:(dbi + 4) * 128]))
                attn_tile(qT[0:64, c0:c0 + n], "bs0", n, kbs, yC[hd][0:64, c0:c0 + n], "fs%d" % hd)
        if p == 0:
            for b in range(NSB):
                cs = TC + b * DS
                for hd in range(8):
                    S.dma("pool", wtb[:], wt_d[hd], writes=["wtb"])
                    wq, wqk = wload(win[:, 1536 + hd * 64: 1536 + (hd + 1) * 64], 128, 8, 64)
                    wkh, wkhk = wload(win[:, 2048 + hd * 64: 2048 + (hd + 1) * 64], 128, 8, 64)
                    wvh, wvhk = wload(win[:, 2560 + hd * 64: 2560 + (hd + 1) * 64], 128, 8, 64)
                    S.dma("pool", vh[:, 0:WB, :],
                          c_v[l, b].rearrange("(k p) n -> p k n", p=128)[:, :, hd * 64:(hd + 1) * 64], writes=["vh"])
                    pv = pring.get()
                    for c in range(8):
                        mm(psm[pv][0:DS, 0:64], h[:, c, cs:cs + DS], wvh[:, c, :], c == 0, c == 7, ["h", wvhk], [P(pv)])
                    cp(vh[0:DS, WB, :], psm[pv][0:DS, 0:64], [P(pv)], ["vh"])
                    kch = BS[2]
                    S.dma("pool", kch[:, 0:WB * 64].rearrange("p (k d) -> p k d", k=WB),
                          c_k[l, b].rearrange("(k p) n -> p k n", p=128)[:, :, hd * 64:(hd + 1) * 64], writes=["bs2"])
                    for g in range(WB // 8):
                        ti = ptring.get()
                        for jj in range(8):
                            kb = g * 8 + jj
                            tr(pst[ti][0:64, jj * 128:(jj + 1) * 128], kch[:, kb * 64:(kb + 1) * 64], ident_b[:],
                               ["bs2", "ident_b"], ["pst%d" % ti])
                        cp(kT[:, g * 1024:(g + 1) * 1024], pst[ti][0:64, :], ["pst%d" % ti], ["kT"],
                           eng="act" if g % 2 else "dve")
                    pi = pring.get()
                    for c in range(8):
                        mm(psm[pi][0:64, 0:DS], wkh[:, c, :], h[:, c, cs:cs + DS], c == 0, c == 7, [wkhk, "h"], [P(pi)])
                    S.op("dve", lambda E: E.memset(kT[:, W:W + 128], 0.0), [], ["kT"])
                    cp(kT[:, W:W + DS], psm[pi][0:64, 0:DS], [P(pi)], ["kT"])
                    pj = pring.get()
                    for c in range(8):
                        mm(psm[pj][0:64, 0:DS], wq[:, c, :], h[:, c, cs:cs + DS], c == 0, c == 7, [wqk, "h"], [P(pj)])
                    qs = BS[0]
                    cp(qs[0:64, 0:DS], psm[pj][0:64, 0:DS], [P(pj)], ["bs0"], eng="act")
                    kbs = []
                    for kb in range(WB + 1):
                        dbi = (WB - kb) + 3
                        kbs.append((kT[:, kb * 128:(kb + 1) * 128], "kT", vh[:, kb, :], "vh",
                                    wtb[:, dbi * 128: dbi * 128 + DS]))
                    attn_tile(qs[0:64, 0:DS], "bs0", DS, kbs, yC[hd][0:64, cs:cs + DS], "fs%d" % hd)
        for (c0, n) in cts:
            pi = pring.get()
            for hd in range(8):
                bi = bring.get()
                act(b512[bi][0:64, 0:n], yC[hd][0:64, c0:c0 + n], AF.Square, ["fs%d" % hd], ["b512_%d" % bi])
                mm(psm[pi][0:64, 0:n], ones_b[0:64, 0:64], b512[bi][0:64, 0:n], hd == 0, hd == 7,
                   ["ones_b", "b512_%d" % bi], [P(pi)])
            ti = tring.get()
            rs = t512[ti][0:64, 0:n]
            ts(rs, psm[pi][0:64, 0:n], 1.0 / 512.0, EPS, ALU.mult, ALU.add, [P(pi)], ["t512_%d" % ti])
            act(rs, rs, AF.Sqrt, ["t512_%d" % ti], ["t512_%d" % ti])
            S.op("dve", lambda E, rs=rs: E.reciprocal(out=rs, in_=rs), ["t512_%d" % ti], ["t512_%d" % ti])
            for hd in range(8):
                stt(yCb[:, hd, c0:c0 + n], yC[hd][0:64, c0:c0 + n], ptab[0:64, l * PL + 51 + hd: l * PL + 52 + hd], rs,
                    ALU.mult, ALU.mult, ["fs%d" % hd, "ptab", "t512_%d" % ti], ["ybuf"])
        proj_fm(w["w_out"][l][512:1024, :], 64, 8, D, 128, lambda c, c0, n: yCb[:, c, c0:c0 + n], ["ybuf"], cts,
                resid_add(1.0))

    def load_memT():
        for mb in range(NMEM // 128):
            xi = xinring.get()
            S.dma("sp", xin[xi][:], memp[mb * 128:(mb + 1) * 128, :], writes=["xin%d" % xi])
            for c4 in range(2):
                pi = pring.get()
                for cc in range(4):
                    c = c4 * 4 + cc
                    tr(psm[pi][:, cc * 128:(cc + 1) * 128], xin[xi][:, c * 128:(c + 1) * 128], ident_f[:],
                       ["xin%d" % xi, "ident_f"], [P(pi)])
                cp(memT[:, c4 * 4:(c4 + 1) * 4, mb * 128:(mb + 1) * 128], psm[pi][:].rearrange("p (c n) -> p c n", c=4),
                   [P(pi)], [KMT])

    def prep_mem(l, p):
        if p == 0:
            load_memT()
            for nm, osrc in (("x_wk", o_mk), ("x_wv", o_mv)):
                for q4 in range(4):
                    wv_, wk_ = wload(w[nm][l][:, q4 * 256:(q4 + 1) * 256], 128, 8, 256)
                    for mb in range(NMEM // 128):
                        pi = pring.get()
                        for c in range(8):
                            mm(psm[pi][:, 0:256], memT[:, c, mb * 128:(mb + 1) * 128], wv_[:, c, :], c == 0, c == 7,
                               [KMT, wk_], [P(pi)])
                        xi = xinring.get()
                        cp(xin[xi][:, 0:256], psm[pi][:, 0:256], [P(pi)], ["xin%d" % xi])
                        finals.append(S.dma("sp", osrc[l][mb * 128:(mb + 1) * 128, q4 * 256:(q4 + 1) * 256],
                                            xin[xi][:, 0:256], reads=["xin%d" % xi]))
                        if nm == "x_wv":
                            cp(mv[:, mb, q4 * 256:(q4 + 1) * 256], xin[xi][:, 0:256], ["xin%d" % xi], [KMV], eng="act")
                    if nm == "x_wk":
                        for mi in range(2):
                            pi = pring.get()
                            for c in range(8):
                                mm(psm[pi][:, 0:NMEM], wv_[:, c, mi * 128:(mi + 1) * 128], memT[:, c, :], c == 0, c == 7,
                                   [wk_, KMT], [P(pi)])
                            cp(mkT[:, q4 * 2 + mi, :], psm[pi][:, 0:NMEM], [P(pi)], [KMK], eng="act")
            if NPASS > 1 and "p_scr" not in cfg.get("xskip", ()):
                S.dma("sp", mkT_scr[l], mkT_flat, reads=[KMK], writes=["mkT_scr%d" % l])
                S.dma("sp", mv_scr[l], mv_flat, reads=[KMV], writes=["mv_scr%d" % l])
        elif "p_scr" not in cfg.get("xskip", ()):
            S.dma("sp", mkT_flat, mkT_scr[l], reads=["mkT_scr%d" % l], writes=[KMK])
            S.dma("sp", mv_flat, mv_scr[l], reads=["mv_scr%d" % l], writes=[KMV])

    def cross_core(l, c0, n, qoff):
        oX = ybuf[:, 0:8 * NTX].rearrange("p (c n) -> p c n", c=8)
        for hx in range(4):
            pts = []
            for mb in range(NMEM // 128):
                pi = pring.get()
                for dc in range(2):
                    mm(psm[pi][:, 0:n], mkT[:, hx * 2 + dc, mb * 128:(mb + 1) * 128],
                       BS[1 + (hx * 2 + dc) // 2][:, ((hx * 2 + dc) % 2) * 512 + qoff:((hx * 2 + dc) % 2) * 512 + qoff + n],
                       dc == 0, dc == 1, [KMK, sbqk(hx * 2 + dc)], [P(pi)])
                bi = bring.get()
                act(b512[bi][:, 0:n], psm[pi][:, 0:n], AF.Exp, [P(pi)], ["b512_%d" % bi], scale=1.0 / 16.0)
                pts.append(bi)
            pdn = pring.get()
            for i, bi in enumerate(pts):
                mm(psm[pdn][:, 0:n], ones_b[:], b512[bi][:, 0:n], i == 0, i == len(pts) - 1, ["ones_b", "b512_%d" % bi],
                   [P(pdn)])
            ti = tring.get()
            rd = t512[ti][:, 0:n]
            S.op("dve", lambda E, rd=rd, pdn=pdn: E.reciprocal(out=rd, in_=psm[pdn][:, 0:n]), [P(pdn)], ["t512_%d" % ti])
            for dc in range(2):
                po = pring.get()
                for i, bi in enumerate(pts):
                    mm(psm[po][:, 0:n], mv[:, i, (hx * 2 + dc) * 128:(hx * 2 + dc + 1) * 128], b512[bi][:, 0:n],
                       i == 0, i == len(pts) - 1, [KMV, "b512_%d" % bi], [P(po)])
                tt(oX[:, hx * 2 + dc, c0:c0 + n], psm[po][:, 0:n], rd, ALU.mult, [P(po), "t512_%d" % ti], ["ybuf"])

    def cross(l, p):
        cts = coltiles(p)
        rmsnorm(l * PL + 16, p, hview, "h", cts)
        if "prep" not in cfg.get("xskip", ()):
            prep_mem(l, p)

        def qsink(mi, c0_, n_, pap, pk):
            cp(sbq(mi, n_), pap, [pk], [sbqk(mi)], eng="act" if mi % 2 else "dve")
        for (c0, n) in cts:
            if c0 >= TC:
                continue
            proj_fm(w["x_wq"][l], 128, 8, D, 128, lambda c, c0_, n_: h[:, c, c0_:c0_ + n_], ["h"], [(c0, n)], qsink)
            if "core" not in cfg.get("xskip", ()):
                cross_core(l, c0, n, 0)
        if p == 0 and "sample" not in cfg.get("xskip", ()):
            proj_fm(w["x_wq"][l], 128, 8, D, 128, lambda c, c0_, n_: h[:, c, c0_:c0_ + n_], ["h"], [(TC, NST)], qsink)
            for b in range(NSB):
                S.dma("pool", mv, cm_v[l, b].rearrange("(k p) n -> p k n", p=128), writes=[KMV])
                kst = BS[0]
                for mb in range(NMEM // 128):
                    S.dma("pool", kst[:, 0:D], cm_k[l, b, mb * 128:(mb + 1) * 128, :], writes=["bs0"])
                    ti = ptring.get()
                    for c in range(8):
                        tr(pst[ti][:, c * 128:(c + 1) * 128], kst[:, c * 128:(c + 1) * 128], ident_b[:],
                           ["bs0", "ident_b"], ["pst%d" % ti])
                    cp(mkT[:, :, mb * 128:(mb + 1) * 128], pst[ti][:].rearrange("p (c n) -> p c n", c=8),
                       ["pst%d" % ti], [KMK])
                cs = TC + b * DS
                cross_core(l, cs, DS, b * DS)
        proj_fm(w["x_wo"][l], 128, 8, D, 128,
                lambda c, c0, n: ybuf[:, 0:8 * NTX].rearrange("p (c n) -> p c n", c=8)[:, c, c0:c0 + n], ["ybuf"], cts,
                resid_add(1.0))

    stages = cfg.get("stages", ("ffn1", "a", "b", "c", "x", "ffn2"))
    for p in range(NPASS):
        load_x(p)
        for l in range(DEPTH):
            if p == 0:
                for b in range(NSB):
                    S.dma("sp", smp_h[:, b, :], st_h[l, b], writes=["smp_h"])
                    S.dma("sp", smp_tail[:, b, :, :], st_conv[l, b], writes=["smp_tail"])
                    S.dma("sp", smp_S[:, b, :, :], st_hg[l, b], writes=["smp_S"])
            if "ffn1" in stages:
                ffn(l, 1, p)
            rmsnorm(l * PL + 8, p, hview, "h", coltiles(p))
            if "a" in stages:
                mixer_a(l, p)
            if "b" in stages:
                mixer_b(l, p)
            if "c" in stages:
                mixer_c(l, p)
            if "x" in stages:
                cross(l, p)
            if "ffn2" in stages:
                ffn(l, 2, p)
            if p == 0:
                for b in range(NSB):
                    finals.append(S.dma("sp", s_h[l, b], smp_h[:, b, :], reads=["smp_h"]))
                    finals.append(S.dma("sp", s_conv[l, b], smp_tail[:, b, :, :], reads=["smp_tail"]))
                    finals.append(S.dma("sp", s_hg[l, b], smp_S[:, b, :, :], reads=["smp_S"]))
        store_y(p)
    S.finish(finals)
    st.close()
    print("[kernel] instructions=%d waits=%d sems=%d" % (S.n_ins, S.n_wait, S.nsem), flush=True)
    return nc


def host_consts(cfg):
    TC, NSB = cfg["TC"], cfg["NSB"]
    NTX = TC + NSB * DS
    triu = np.tile(np.triu(np.ones((64, 64), np.float32)), (1, 8))
    rmask = np.ones((64, NTX), np.float32)
    rmask[:, 0:TC:64] = 0.0
    rmask[:, TC::DS] = 0.0
    return {"wt_tab": wt_table().reshape(8, 128, 23 * 128), "ident": np.eye(128, dtype=np.float32),
            "triu": triu, "rmask": rmask}


def pack_ptab(inp, DEPTH):
    NPT = DEPTH * PL + 8
    t = np.zeros((128, NPT), np.float32)

    def fm8(v):
        return np.ascontiguousarray(v.reshape(8, 128).T)

    def fm2(v):
        return np.ascontiguousarray(v.reshape(2, 128).T)
    for l in range(DEPTH):
        b = l * PL
        t[:, b + 0:b + 8] = fm8(inp["n_ffn1"][l])
        t[:, b + 8:b + 16] = fm8(inp["n_mix"][l])
        t[:, b + 16:b + 24] = fm8(inp["n_cross"][l])
        t[:, b + 24:b + 32] = fm8(inp["n_ffn2"][l])
        cw = inp["lru_conv_w"][l]
        for pt in range(2):
            for tap in range(4):
                t[:, b + 32 + pt * 4 + tap] = cw[tap, pt * 128:(pt + 1) * 128]
        t[:, b + 40:b + 42] = fm2(inp["lru_conv_b"][l])
        t[:, b + 42:b + 44] = fm2(inp["lru_ba"][l])
        t[:, b + 44:b + 46] = fm2(inp["lru_bx"][l])
        t[:, b + 46:b + 48] = fm2(inp["lru_lambda"][l])
        t[:, b + 48:b + 50] = fm2(inp["gn_a"][l])
        t[0:64, b + 50] = inp["hgrn_norm"][l]
        t[0:64, b + 51:b + 59] = inp["gn_c"][l].reshape(8, 64).T
        t[0:64, b + 59:b + 63] = inp["hgrn_lb"][l].reshape(4, 64).T
    t[:, DEPTH * PL:DEPTH * PL + 8] = fm8(inp["n_final"])
    return t


def block_diag(wb):
    DEPTH = wb.shape[0]
    o = np.zeros((DEPTH, 2, 128, 128), np.float32)
    for pt in range(2):
        for j in range(2):
            o[:, pt, j * 64:(j + 1) * 64, j * 64:(j + 1) * 64] = wb[:, pt * 2 + j]
    return o


_NC_CACHE = {}


def run(inp, cfg, n_cores=8):
    DEPTH, NSB, SEQ = cfg["DEPTH"], cfg["NSB"], cfg["SEQ"]
    BATCH = inp["x_prompt"].shape[0]
    key = tuple(sorted((k, str(v)) for k, v in cfg.items()))
    if key not in _NC_CACHE:
        _NC_CACHE[key] = build(cfg)
    nc = _NC_CACHE[key]
    consts = host_consts(cfg)
    shared = {k: np.ascontiguousarray(inp[k]) for k in
              ["ffn1_wg", "ffn1_wu", "ffn1_wd", "ffn2_wg", "ffn2_wu", "ffn2_wd", "w_in", "w_out", "x_wq", "x_wk",
               "x_wv", "x_wo"]}
    shared["lru_wbd"] = np.ascontiguousarray(
        np.stack([block_diag(inp["lru_wa"]), block_diag(inp["lru_wx"])], axis=1))
    shared["ptab"] = pack_ptab(inp, DEPTH)
    shared.update(consts)
    in_maps = []
    for c in range(n_cores):
        m = dict(shared)
        if c < BATCH:
            m["xp"] = np.ascontiguousarray(inp["x_prompt"][c])
            m["memp"] = np.ascontiguousarray(inp["mem_prompt"][c])
        else:
            m["xp"] = np.zeros((SEQ, D), np.float32)
            m["memp"] = np.zeros((NMEM, D), np.float32)
        bs = slice(c * NSB, (c + 1) * NSB)
        m["xs"] = np.ascontiguousarray(inp["x_sample"][bs].reshape(NSB * DS, D))
        m["st_h"] = np.ascontiguousarray(inp["state_lru_h"][:, bs].reshape(DEPTH, NSB, 2, 128).transpose(0, 1, 3, 2))
        m["st_conv"] = np.ascontiguousarray(
            inp["state_lru_conv"][:, bs].reshape(DEPTH, NSB, 3, 2, 128).transpose(0, 1, 4, 3, 2))
        m["st_hg"] = np.ascontiguousarray(inp["state_hgrn"][:, bs].transpose(0, 1, 3, 2, 4))
        m["c_k"] = np.ascontiguousarray(inp["cache_swa_k"][:, bs].reshape(DEPTH, NSB, W, 512))
        m["c_v"] = np.ascontiguousarray(inp["cache_swa_v"][:, bs].reshape(DEPTH, NSB, W, 512))
        m["cm_k"] = np.ascontiguousarray(inp["cache_mem_k"][:, bs].reshape(DEPTH, NSB, NMEM, D))
        m["cm_v"] = np.ascontiguousarray(inp["cache_mem_v"][:, bs].reshape(DEPTH, NSB, NMEM, D))
        in_maps.append(m)
    res = run_bass_kernel_spmd(nc, in_maps, core_ids=list(range(n_cores)))
    R = res.results
    KEEP = min(W, SEQ)
    y_prompt = np.stack([R[b]["yp"] for b in range(BATCH)], 0)
    y_sample = np.concatenate([R[c]["ys"].reshape(NSB, DS, D) for c in range(n_cores)], 0)
    p_h = np.stack([R[b]["o_h"].transpose(0, 2, 1).reshape(DEPTH, 256) for b in range(BATCH)], 1)
    p_c = np.stack([R[b]["o_conv"].transpose(0, 3, 2, 1).reshape(DEPTH, 3, 256) for b in range(BATCH)], 1)
    p_s = np.stack([R[b]["o_hg"].transpose(0, 2, 1, 3) for b in range(BATCH)], 1)
    p_k = np.stack([R[b]["o_k"].reshape(DEPTH, KEEP, 8, 64) for b in range(BATCH)], 1)
    p_v = np.stack([R[b]["o_v"].reshape(DEPTH, KEEP, 8, 64) for b in range(BATCH)], 1)
    p_mk = np.stack([R[b]["o_mk"].reshape(DEPTH, NMEM, 4, 256) for b in range(BATCH)], 1)
    p_mv = np.stack([R[b]["o_mv"].reshape(DEPTH, NMEM, 4, 256) for b in range(BATCH)], 1)
    s_h = np.concatenate([R[c]["s_h"].transpose(0, 1, 3, 2).reshape(DEPTH, NSB, 256) for c in range(n_cores)], 1)
    s_c = np.concatenate([R[c]["s_conv"].transpose(0, 1, 4, 3, 2).reshape(DEPTH, NSB, 3, 256) for c in range(n_cores)], 1)
    s_s = np.concatenate([R[c]["s_hg"].transpose(0, 1, 3, 2, 4) for c in range(n_cores)], 1)
    s_k = np.concatenate([R[c]["s_k"].reshape(DEPTH, NSB, W, 8, 64) for c in range(n_cores)], 1)
    s_v = np.concatenate([R[c]["s_v"].reshape(DEPTH, NSB, W, 8, 64) for c in range(n_cores)], 1)
    outs = (y_prompt, y_sample, p_h, p_c, p_s, p_k, p_v, p_mk, p_mv, s_h, s_c, s_s, s_k, s_v)
    return tuple(np.ascontiguousarray(o, dtype=np.float32) for o in outs)


def kernel(**inputs):
    inp = {k: np.asarray(v) for k, v in inputs.items()}
    cfg = {"SEQ": int(inp["x_prompt"].shape[1]), "TC": 1024, "DEPTH": int(inp["w_in"].shape[0]),
           "NSB": int(inp["x_sample"].shape[0]) // 8}
    return run(inp, cfg)
```

```python
from contextlib import ExitStack
import numpy as np
import ml_dtypes
import concourse.bass as bass
import concourse.mybir as mybir
from concourse.bass_utils import run_bass_kernel_spmd

F32 = mybir.dt.float32
BF16 = mybir.dt.bfloat16
AF = mybir.ActivationFunctionType
ALU = mybir.AluOpType

ENGS = ["pe", "act", "dve", "pool", "sp"]
SEM_ROLL = 30000
NDMASEM = 6

D = 1024
DFF = 2816
DIN = 3072
NMEM = 256
DS = 4
W = 2048
DIL = ((128, 1), (512, 4), (2048, 16))
EPS = 1e-6
GSZ = 2
WSZ = 2048
PL = 64


class Sched:
    def __init__(self, nc, stack):
        self.nc = nc
        self.stack = stack
        self.q = {e: [] for e in ENGS}
        self.cur_sem = {}
        self.cur_cnt = {}
        self.nsem = 0
        for e in ENGS:
            self._new_sem(e)
        self.dsem = {}
        self.dcnt = {}
        self.dnext = {}
        for e in ["sp", "pool", "act"]:
            self.dsem[e] = [self._alloc_sem("d%s%d" % (e, k)) for k in range(NDMASEM)]
            self.dcnt[e] = [0] * NDMASEM
            self.dnext[e] = 0
        self.seen = {e: {} for e in ENGS}
        self.res = {}
        self.n_wait = 0
        self.n_ins = 0

    def _alloc_sem(self, name):
        self.nsem += 1
        return self.stack.enter_context(self.nc.semaphore(name))

    def _new_sem(self, e):
        self.cur_sem[e] = self._alloc_sem("s%s%d" % (e, self.nsem))
        self.cur_cnt[e] = 0

    def _need(self, eng, ev):
        if ev is None:
            return
        sem, val = ev
        k = id(sem)
        if self.seen[eng].get(k, 0) >= val:
            return
        self.seen[eng][k] = val
        self.n_wait += 1
        self.q[eng].append(lambda E, sem=sem, val=val: E.wait_ge(sem, val))

    def _deps(self, eng, reads, writes, acc=False):
        for r in reads:
            st = self.res.get(r)
            if st is not None:
                self._need(eng, st[0])
        for w in writes:
            st = self.res.get(w)
            if st is not None:
                if not (acc and st[2] == eng):
                    self._need(eng, st[0])
                for ev in st[1]:
                    self._need(eng, ev)

    def _record(self, eng, ev, reads, writes):
        for r in reads:
            st = self.res.get(r)
            if st is None:
                st = [None, [], None]
                self.res[r] = st
            st[1].append(ev)
            if len(st[1]) > 10:
                d = {}
                for s, v in st[1]:
                    if id(s) not in d or d[id(s)][1] < v:
                        d[id(s)] = (s, v)
                st[1] = list(d.values())
        for w in writes:
            self.res[w] = [ev, [], eng]

    def op(self, eng, fn, reads=(), writes=(), acc=False):
        self._deps(eng, reads, writes, acc=acc)
        if self.cur_cnt[eng] >= SEM_ROLL:
            self._new_sem(eng)
        sem = self.cur_sem[eng]
        self.cur_cnt[eng] += 1
        val = self.cur_cnt[eng]
        self.n_ins += 1
        self.q[eng].append(lambda E, fn=fn, sem=sem: fn(E).then_inc(sem, 1))
        ev = (sem, val)
        self._record(eng, ev, reads, writes)
        return ev

    def dma(self, eng, out, in_, reads=(), writes=(), **kw):
        k = self.dnext[eng]
        self.dnext[eng] = (k + 1) % NDMASEM
        sem = self.dsem[eng][k]
        if self.dcnt[eng][k] > 0:
            self._need(eng, (sem, self.dcnt[eng][k]))
        self._deps(eng, reads, writes)
        self.dcnt[eng][k] += 16
        val = self.dcnt[eng][k]
        self.n_ins += 1
        self.q[eng].append(
            lambda E, out=out, in_=in_, sem=sem, kw=kw: E.dma_start(out=out, in_=in_, **kw).then_inc(sem, 16))
        ev = (sem, val)
        self._record(eng, ev, reads, writes)
        return ev

    def finish(self, final_events):
        for ev in final_events:
            self._need("sp", ev)
        nc = self.nc
        with nc.Block() as block:
            @block.sync
            def _(E):
                for f in self.q["sp"]:
                    f(E)

            @block.tensor
            def _(E):
                for f in self.q["pe"]:
                    f(E)

            @block.scalar
            def _(E):
                for f in self.q["act"]:
                    f(E)

            @block.vector
            def _(E):
                for f in self.q["dve"]:
                    f(E)

            @block.gpsimd
            def _(E):
                for f in self.q["pool"]:
                    f(E)


class Ring:
    def __init__(self, items):
        self.items = items
        self.i = 0

    def get(self):
        it = self.items[self.i]
        self.i = (self.i + 1) % len(self.items)
        return it


def wt_table():
    slopes = 2.0 ** (-8.0 * np.arange(1, 9) / 8.0)
    db = np.arange(-3, 20)[None, :, None]
    ik = np.arange(128)[:, None, None]
    iq = np.arange(128)[None, None, :]
    delta = 128 * db + iq - ik
    mult = np.zeros(delta.shape, np.float64)
    for win, dil in DIL:
        mult += ((delta >= 0) & (delta <= win) & (delta % dil == 0))
    dpos = np.maximum(delta, 0).astype(np.float64)
    tab = np.stack([mult * np.exp(-s * dpos) for s in slopes], 0)
    return tab.astype(np.float32)


def build(cfg):
    SEQ = cfg["SEQ"]
    TC = cfg["TC"]
    DEPTH = cfg["DEPTH"]
    NSB = cfg["NSB"]
    NPASS = SEQ // TC
    NBLK = TC // 128
    WB = W // 128
    NST = NSB * DS
    NTX = TC + NST
    KEEP = min(W, SEQ)
    NPT = DEPTH * PL + 8
    assert TC % 512 == 0 and SEQ % TC == 0

    nc = bass.Bass("TRN2", target_bir_lowering=False)

    def din(name, shape, dt=F32):
        return nc.dram_tensor(name, list(shape), dt, kind="ExternalInput").ap()

    def dout(name, shape, dt=F32):
        return nc.dram_tensor(name, list(shape), dt, kind="ExternalOutput").ap()

    def dint(name, shape, dt=F32):
        return nc.dram_tensor(name, list(shape), dt, kind=cfg.get("scr_kind", "Internal")).ap()

    xp = din("xp", [SEQ, D])
    xs = din("xs", [NST, D])
    st_h = din("st_h", [DEPTH, NSB, 128, 2])
    st_conv = din("st_conv", [DEPTH, NSB, 128, 2, 3])
    st_hg = din("st_hg", [DEPTH, NSB, 64, 4, 64])
    c_k = din("c_k", [DEPTH, NSB, W, 512])
    c_v = din("c_v", [DEPTH, NSB, W, 512])
    cm_k = din("cm_k", [DEPTH, NSB, NMEM, D])
    cm_v = din("cm_v", [DEPTH, NSB, NMEM, D])
    memp = din("memp", [NMEM, D])
    w = {}
    for nm, shp in [("ffn1_wg", [DEPTH, D, DFF]), ("ffn1_wu", [DEPTH, D, DFF]), ("ffn1_wd", [DEPTH, DFF, D]),
                    ("ffn2_wg", [DEPTH, D, DFF]), ("ffn2_wu", [DEPTH, D, DFF]), ("ffn2_wd", [DEPTH, DFF, D]),
                    ("w_in", [DEPTH, D, DIN]), ("w_out", [DEPTH, D, D]), ("x_wq", [DEPTH, D, D]),
                    ("x_wk", [DEPTH, D, D]), ("x_wv", [DEPTH, D, D]), ("x_wo", [DEPTH, D, D]),
                    ("lru_wbd", [DEPTH, 2, 2, 128, 128])]:
        w[nm] = din(nm, shp)
    ptab_d = din("ptab", [128, NPT])
    wt_d = din("wt_tab", [8, 128, 23 * 128], BF16)
    ident_d = din("ident", [128, 128])
    triu_d = din("triu", [64, 512])
    rmask_d = din("rmask", [64, NTX])

    yp = dout("yp", [SEQ, D])
    ys = dout("ys", [NST, D])
    o_h = dout("o_h", [DEPTH, 128, 2])
    o_conv = dout("o_conv", [DEPTH, 128, 2, 3])
    o_hg = dout("o_hg", [DEPTH, 64, 4, 64])
    o_k = dout("o_k", [DEPTH, KEEP, 512])
    o_v = dout("o_v", [DEPTH, KEEP, 512])
    o_mk = dout("o_mk", [DEPTH, NMEM, D])
    o_mv = dout("o_mv", [DEPTH, NMEM, D])
    s_h = dout("s_h", [DEPTH, NSB, 128, 2])
    s_conv = dout("s_conv", [DEPTH, NSB, 128, 2, 3])
    s_hg = dout("s_hg", [DEPTH, NSB, 64, 4, 64])
    s_k = dout("s_k", [DEPTH, NSB, W, 512])
    s_v = dout("s_v", [DEPTH, NSB, W, 512])

    kT_scr = dint("kT_scr", [DEPTH, 512, SEQ], BF16)
    v_scr = dint("v_scr", [DEPTH, 8, SEQ, 64], BF16)
    mkT_scr = dint("mkT_scr", [DEPTH, 128, 8 * NMEM], BF16)
    mv_scr = dint("mv_scr", [DEPTH, 128, 2 * D], BF16)

    st = ExitStack()
    S = Sched(nc, st)
    finals = []

    def sb(name, shape, dt=F32):
        return st.enter_context(nc.sbuf_tensor(name, list(shape), dt))

    def ps(name, shape, dt=F32):
        return st.enter_context(nc.psum_tensor(name, list(shape), dt))

    x = sb("x", [128, 8, NTX])
    h = sb("h", [128, 8, NTX], BF16)
    ybuf = sb("ybuf", [128, 8 * NTX], BF16)
    NWB = 8
    wbufs = [sb("wb%d" % i, [128, WSZ], BF16) for i in range(NWB)]
    wring = Ring(list(range(NWB)))
    actb = [sb("actb%d" % i, [128, 4, 512], BF16) for i in range(2)]
    actring = Ring([0, 1])
    FS = [sb("fs%d" % i, [128, NTX + 8]) for i in range(10)]
    SW = max(NTX + 8, 64 * (TC // 64 + NSB))
    BS = [sb("bs%d" % i, [128, SW], BF16) for i in range(5)]
    t512 = [sb("t512_%d" % i, [128, 512]) for i in range(3)]
    tring = Ring([0, 1, 2])
    b512 = [sb("b512_%d" % i, [128, 512], BF16) for i in range(4)]
    bring = Ring([0, 1, 2, 3])
    vh = sb("vh", [128, WB + NBLK, 64], BF16)
    kT = sb("kT", [64, W + TC], BF16)
    wtb = sb("wtb", [128, 23 * 128], BF16)
    memT = wtb[:, 0:8 * NMEM].rearrange("p (c n) -> p c n", c=8)
    ptab = sb("ptab_sb", [128, NPT])
    dpar = sb("dpar", [128, DEPTH, 16])
    ident_f = sb("ident_f", [128, 128])
    ident_b = sb("ident_b", [128, 128], BF16)
    ones_b = sb("ones_b", [128, 128], BF16)
    triu = sb("triu_sb", [64, 512])
    rmask = sb("rmask_sb", [64, NTX])
    lru_h = sb("lru_h", [128, DEPTH, 2])
    lru_tail = sb("lru_tail", [128, DEPTH, 2, 3])
    hgS = sb("hgS", [64, DEPTH, 4, 64])
    hgSb = sb("hgSb", [64, 17, 64], BF16)
    smp_h = sb("smp_h", [128, NSB, 2])
    smp_tail = sb("smp_tail", [128, NSB, 2, 3])
    smp_S = sb("smp_S", [64, NSB, 4, 64])
    mkT_flat = actb[0][:].rearrange("p a b -> p (a b)")
    mv_flat = actb[1][:].rearrange("p a b -> p (a b)")
    mkT = mkT_flat.rearrange("p (c n) -> p c n", c=8)
    mv = mv_flat.rearrange("p (c n) -> p c n", c=2)
    KMK, KMV, KMT = "actb0", "actb1", "wtb"
    xin = [sb("xin%d" % i, [128, D]) for i in range(1)]
    xinring = Ring([0])

    def sbq(mi, n):
        return BS[1 + mi // 2][:, (mi % 2) * 512:(mi % 2) * 512 + n]

    def sbqk(mi):
        return "bs%d" % (1 + mi // 2)

    psm = [ps("psm%d" % i, [128, 512]) for i in range(4)]
    pring = Ring([0, 1, 2, 3])
    psacc = [ps("psacc%d" % i, [128, 512]) for i in range(2)]
    pst = [ps("pst%d" % i, [128, 1024], BF16) for i in range(2)]
    ptring = Ring([0, 1])

    def P(i):
        return "psm%d" % i

    def act(out, in_, func, reads, writes, scale=1.0, bias=None, eng="act"):
        if bias is None:
            S.op(eng, lambda E: E.activation(out=out, in_=in_, func=func, scale=scale), reads, writes)
        else:
            S.op(eng, lambda E: E.activation(out=out, in_=in_, func=func, scale=scale, bias=bias), reads, writes)

    def tt(out, a, b, op, reads, writes, eng="dve"):
        S.op(eng, lambda E: E.tensor_tensor(out=out, in0=a, in1=b, op=op), reads, writes)

    def ts(out, a, s1, s2, op0, op1, reads, writes, eng="dve"):
        S.op(eng, lambda E: E.tensor_scalar(out=out, in0=a, scalar1=s1, scalar2=s2, op0=op0, op1=op1), reads, writes)

    def stt(out, a, s, b, op0, op1, reads, writes):
        S.op("dve", lambda E: E.scalar_tensor_tensor(out=out, in0=a, scalar=s, in1=b, op0=op0, op1=op1), reads, writes)

    def cp(out, in_, reads, writes, eng="dve"):
        if eng == "act":
            S.op(eng, lambda E: E.activation(out=out, in_=in_, func=AF.Copy), reads, writes)
        else:
            S.op(eng, lambda E: E.tensor_copy(out=out, in_=in_), reads, writes)

    def mm(out, lhsT, rhs, start, stop, reads, writes):
        S.op("pe", lambda E: E.matmul(out, lhsT=lhsT, rhs=rhs, start=start, stop=stop), reads, writes, acc=not start)

    def tr(out, in_, ident, reads, writes):
        S.op("pe", lambda E: E.transpose(out=out, in_=in_, identity=ident), reads, writes)

    def wload(src, kparts, kc, n):
        i = wring.get()
        assert kc * n <= WSZ, (kc, n)
        view = wbufs[i][0:kparts, 0:kc * n].rearrange("p (c n) -> p c n", c=kc)
        S.dma("pool", view, src.rearrange("(c p) n -> p c n", p=kparts), writes=["wb%d" % i])
        return view, "wb%d" % i

    def coltiles(p):
        t = [(c0, 512) for c0 in range(0, TC, 512)]
        if p == 0:
            t.append((TC, NST))
        return t

    S.dma("sp", ptab[:], ptab_d, writes=["ptab"])
    S.dma("sp", ident_f[:], ident_d, writes=["ident_f"])
    S.dma("pool", ident_b[:], ident_d, writes=["ident_b"])
    S.dma("sp", triu[:], triu_d, writes=["triu"])
    S.dma("sp", rmask[:], rmask_d, writes=["rmask"])
    S.op("dve", lambda E: E.memset(ones_b[:], 1.0), writes=["ones_b"])
    S.op("dve", lambda E: E.memset(lru_h[:], 0.0), writes=["lru_h"])
    S.op("dve", lambda E: E.memset(lru_tail[:], 0.0), writes=["lru_tail"])
    S.op("dve", lambda E: E.memset(hgS[:], 0.0), writes=["hgS"])
    S.op("dve", lambda E: E.memset(kT[:], 0.0), writes=["kT"])
    S.op("pool", lambda E: E.memset(vh[:], 0.0), writes=["vh"])

    def pcol(l, off, n=1):
        return ptab[:, l * PL + off: l * PL + off + n]

    for l in range(DEPTH):
        lam = pcol(l, 46, 2)
        e_ = FS[0][:, 0:2]
        z_ = FS[0][:, 2:4]
        z2 = FS[0][:, 4:6]
        pl_ = FS[0][:, 6:8]
        act(e_, lam, AF.Exp, ["ptab"], ["fs0"], scale=-1.0)
        ts(z_, e_, 2.0, None, ALU.add, ALU.bypass, ["fs0"], ["fs0"])
        S.op("dve", lambda E, z_=z_: E.reciprocal(out=z_, in_=z_), ["fs0"], ["fs0"])
        tt(z_, z_, e_, ALU.mult, ["fs0"], ["fs0"])
        tt(z2, z_, z_, ALU.mult, ["fs0"], ["fs0"])
        ts(pl_, z2, 1.0 / 7.0, 1.0 / 5.0, ALU.mult, ALU.add, ["fs0"], ["fs0"])
        tt(pl_, pl_, z2, ALU.mult, ["fs0"], ["fs0"])
        ts(pl_, pl_, 1.0 / 3.0, None, ALU.add, ALU.bypass, ["fs0"], ["fs0"])
        tt(pl_, pl_, z2, ALU.mult, ["fs0"], ["fs0"])
        ts(pl_, pl_, 1.0, None, ALU.add, ALU.bypass, ["fs0"], ["fs0"])
        tt(pl_, pl_, z_, ALU.mult, ["fs0"], ["fs0"])
        ts(dpar[:, l, 0:2], pl_, -16.0, None, ALU.mult, ALU.bypass, ["fs0"], ["dpar"])
        ts(dpar[:, l, 2:4], pl_, -32.0, None, ALU.mult, ALU.bypass, ["fs0"], ["dpar"])
    esum = FS[1][0:64, 0:4]
    ecum = FS[1][0:64, 4:8]
    for l in range(DEPTH):
        el = FS[1][0:64, 8 + 4 * l: 12 + 4 * l]
        act(el, ptab[0:64, l * PL + 59: l * PL + 63], AF.Exp, ["ptab"], ["fs1"])
        if l == 0:
            cp(esum, el, ["fs1"], ["fs1"])
        else:
            tt(esum, esum, el, ALU.add, ["fs1"], ["fs1"])
    S.op("dve", lambda E: E.reciprocal(out=esum, in_=esum), ["fs1"], ["fs1"])
    S.op("dve", lambda E: E.memset(ecum, 0.0), ["fs1"], ["fs1"])
    for l in range(DEPTH):
        el = FS[1][0:64, 8 + 4 * l: 12 + 4 * l]
        if l > 0:
            tt(ecum, ecum, el, ALU.add, ["fs1"], ["fs1"])
        tt(dpar[0:64, l, 4:8], ecum, esum, ALU.mult, ["fs1"], ["dpar"])
        ts(dpar[0:64, l, 8:12], dpar[0:64, l, 4:8], -1.0, 1.0, ALU.mult, ALU.add, ["dpar"], ["dpar"])

    def rmsnorm(l_off, p, out_fn, out_key, cts):
        okey = out_key if callable(out_key) else (lambda c: out_key)
        for (c0, n) in cts:
            pi = pring.get()
            for c in range(8):
                bi = bring.get()
                act(b512[bi][:, 0:n], x[:, c, c0:c0 + n], AF.Square, ["x"], ["b512_%d" % bi])
                mm(psm[pi][:, 0:n], ones_b[:], b512[bi][:, 0:n], c == 0, c == 7, ["ones_b", "b512_%d" % bi], [P(pi)])
            ti = tring.get()
            rs = t512[ti][:, 0:n]
            ts(rs, psm[pi][:, 0:n], 1.0 / D, EPS, ALU.mult, ALU.add, [P(pi)], ["t512_%d" % ti])
            act(rs, rs, AF.Sqrt, ["t512_%d" % ti], ["t512_%d" % ti])
            S.op("dve", lambda E, rs=rs: E.reciprocal(out=rs, in_=rs), ["t512_%d" % ti], ["t512_%d" % ti])
            for c in range(8):
                stt(out_fn(c, c0, n), x[:, c, c0:c0 + n], ptab[:, l_off + c: l_off + c + 1], rs,
                    ALU.mult, ALU.mult, ["x", "ptab", "t512_%d" % ti], [okey(c)])

    def hview(c, c0, n):
        return h[:, c, c0:c0 + n]

    def ffn(l, which, p):
        cts = coltiles(p)
        rmsnorm(l * PL + (0 if which == 1 else 24), p, hview, "h", cts)
        wg, wu, wd = w["ffn%d_wg" % which], w["ffn%d_wu" % which], w["ffn%d_wd" % which]
        nch = DFF // 128
        for g0 in range(0, nch, GSZ):
            gs = min(GSZ, nch - g0)
            wgv, wgk = wload(wg[l][:, g0 * 128:(g0 + gs) * 128], 128, 8, gs * 128)
            wuv, wuk = wload(wu[l][:, g0 * 128:(g0 + gs) * 128], 128, 8, gs * 128)
            wdv, wdk = wload(wd[l][g0 * 128:(g0 + gs) * 128, :], 128, gs, D)
            for (c0, n) in cts:
                ai = actring.get()
                ak = "actb%d" % ai
                for j in range(gs):
                    pg = pring.get()
                    for c in range(8):
                        mm(psm[pg][:, 0:n], wgv[:, c, j * 128:(j + 1) * 128], h[:, c, c0:c0 + n], c == 0, c == 7,
                           [wgk, "h"], [P(pg)])
                    pu = pring.get()
                    for c in range(8):
                        mm(psm[pu][:, 0:n], wuv[:, c, j * 128:(j + 1) * 128], h[:, c, c0:c0 + n], c == 0, c == 7,
                           [wuk, "h"], [P(pu)])
                    ti = tring.get()
                    act(t512[ti][:, 0:n], psm[pg][:, 0:n], AF.Silu, [P(pg)], ["t512_%d" % ti])
                    tt(actb[ai][:, j, 0:n], t512[ti][:, 0:n], psm[pu][:, 0:n], ALU.mult,
                       ["t512_%d" % ti, P(pu)], [ak])
                for m in range(8):
                    pd = pring.get()
                    for j in range(gs):
                        mm(psm[pd][:, 0:n], wdv[:, j, m * 128:(m + 1) * 128], actb[ai][:, j, 0:n], j == 0, j == gs - 1,
                           [wdk, ak], [P(pd)])
                    stt(x[:, m, c0:c0 + n], psm[pd][:, 0:n], 0.5, x[:, m, c0:c0 + n], ALU.mult, ALU.add,
                        [P(pd), "x"], ["x"])

    def proj_fm(wsrc, kparts, kc, mtot, msz, rhs_fn, rhs_keys, cts, sink):
        wpiece = WSZ // kc
        wpiece = (wpiece // msz) * msz
        for m0 in range(0, mtot, wpiece):
            mw = min(wpiece, mtot - m0)
            wv, wk = wload(wsrc[:, m0:m0 + mw], kparts, kc, mw)
            for mi in range(mw // msz):
                for (c0, n) in cts:
                    pi = pring.get()
                    for c in range(kc):
                        mm(psm[pi][0:msz, 0:n], wv[:, c, mi * msz:(mi + 1) * msz], rhs_fn(c, c0, n), c == 0, c == kc - 1,
                           [wk] + rhs_keys, [P(pi)])
                    sink(m0 // msz + mi, c0, n, psm[pi][0:msz, 0:n], P(pi))

    def resid_add(scale):
        def sink(mi, c0, n, pap, pk):
            if scale == 1.0:
                tt(x[:, mi, c0:c0 + n], pap, x[:, mi, c0:c0 + n], ALU.add, [pk, "x"], ["x"])
            else:
                stt(x[:, mi, c0:c0 + n], pap, scale, x[:, mi, c0:c0 + n], ALU.mult, ALU.add, [pk, "x"], ["x"])
        return sink

    def load_x(p):
        for b in range(NBLK):
            xi = xinring.get()
            S.dma("sp", xin[xi][:], xp[p * TC + b * 128: p * TC + (b + 1) * 128, :], writes=["xin%d" % xi])
            for c4 in range(2):
                pi = pring.get()
                for cc in range(4):
                    c = c4 * 4 + cc
                    tr(psm[pi][:, cc * 128:(cc + 1) * 128], xin[xi][:, c * 128:(c + 1) * 128], ident_f[:],
                       ["xin%d" % xi, "ident_f"], [P(pi)])
                cp(x[:, c4 * 4:(c4 + 1) * 4, b * 128:(b + 1) * 128],
                   psm[pi][:].rearrange("p (c n) -> p c n", c=4), [P(pi)], ["x"], eng="act" if c4 else "dve")
        if p == 0:
            xi = xinring.get()
            S.dma("sp", xin[xi][0:NST, :], xs, writes=["xin%d" % xi])
            for c4 in range(2):
                pi = pring.get()
                for cc in range(4):
                    c = c4 * 4 + cc
                    tr(psm[pi][:, cc * 128: cc * 128 + NST], xin[xi][0:NST, c * 128:(c + 1) * 128],
                       ident_f[0:NST, 0:NST], ["xin%d" % xi, "ident_f"], [P(pi)])
                cp(x[:, c4 * 4:(c4 + 1) * 4, TC:TC + NST],
                   psm[pi][:].rearrange("p (c n) -> p c n", c=4)[:, :, 0:NST], [P(pi)], ["x"])

    def store_y(p):
        cts = coltiles(p)
        yf = FS
        rmsnorm(DEPTH * PL, p, lambda c, c0, n: yf[c][:, c0:c0 + n], lambda c: "fs%d" % c, cts)
        nb = NBLK + (1 if p == 0 else 0)
        for b in range(nb):
            ntok = 128 if b < NBLK else NST
            xi = xinring.get()
            for c4 in range(2):
                pi = pring.get()
                for cc in range(4):
                    c = c4 * 4 + cc
                    tr(psm[pi][0:ntok, cc * 128:(cc + 1) * 128], yf[c][:, b * 128: b * 128 + ntok], ident_f[:],
                       ["fs%d" % c, "ident_f"], [P(pi)])
                cp(xin[xi][0:ntok, c4 * 512:(c4 + 1) * 512], psm[pi][0:ntok, :], [P(pi)], ["xin%d" % xi],
                   eng="act" if c4 else "dve")
            if b < NBLK:
                ev = S.dma("sp", yp[p * TC + b * 128: p * TC + (b + 1) * 128, :], xin[xi][:], reads=["xin%d" % xi])
            else:
                ev = S.dma("sp", ys, xin[xi][0:NST, :], reads=["xin%d" % xi])
            finals.append(ev)

    def mixer_a(l, p):
        cts = coltiles(p)
        win = w["w_in"][l]
        segs = [("p", 0, TC, None)]
        if p == 0:
            segs += [("s", TC + DS * b, DS, b) for b in range(NSB)]
        yA = ybuf[:, 0:2 * NTX].rearrange("p (c n) -> p c n", c=2)
        ypre = [FS[8], FS[9]]
        for pt in range(2):
            xaext, xc, r_, i_, a_, u_, hh, ga_, tmp = FS[0], FS[1], FS[2], FS[3], FS[4], FS[5], FS[6], FS[7], ypre[pt]
            xcb = BS[0]
            K = lambda i: "fs%d" % i
            wxa, wxak = wload(win[:, pt * 128:(pt + 1) * 128], 128, 8, 128)
            wga, wgak = wload(win[:, 256 + pt * 128: 256 + (pt + 1) * 128], 128, 8, 128)
            for (c0, n) in cts:
                pi = pring.get()
                for c in range(8):
                    mm(psm[pi][:, 0:n], wxa[:, c, :], h[:, c, c0:c0 + n], c == 0, c == 7, [wxak, "h"], [P(pi)])
                if c0 < TC:
                    cp(xaext[:, 3 + c0: 3 + c0 + n], psm[pi][:, 0:n], [P(pi)], [K(0)], eng="act")
                else:
                    cp(tmp[:, 0:n], psm[pi][:, 0:n], [P(pi)], [K(8 + pt)], eng="act")
                pj = pring.get()
                for c in range(8):
                    mm(psm[pj][:, 0:n], wga[:, c, :], h[:, c, c0:c0 + n], c == 0, c == 7, [wgak, "h"], [P(pj)])
                cp(ga_[:, c0:c0 + n], psm[pj][:, 0:n], [P(pj)], [K(7)], eng="act")
            cw = lambda tap: pcol(l, 32 + pt * 4 + tap)
            cb = pcol(l, 40 + pt)
            for (kind, c0, T, b) in segs:
                if kind == "p":
                    cp(xaext[:, 0:3], lru_tail[:, l, pt, :], ["lru_tail"], [K(0)])
                    src = xaext
                    so = 0
                else:
                    src = a_
                    so = 16 * b
                    cp(src[:, so:so + 3], smp_tail[:, b, pt, :], ["smp_tail"], [K(4)])
                    cp(src[:, so + 3:so + 3 + T], tmp[:, DS * b: DS * b + T], [K(8 + pt)], [K(4)])
                sk = K(0) if kind == "p" else K(4)
                ts(xc[:, c0:c0 + T], src[:, so:so + T], cw(0), cb, ALU.mult, ALU.add, [sk, "ptab"], [K(1)])
                for tap in range(1, 4):
                    stt(xc[:, c0:c0 + T], src[:, so + tap:so + tap + T], cw(tap), xc[:, c0:c0 + T], ALU.mult, ALU.add,
                        [sk, "ptab", K(1)], [K(1)])
                if kind == "p":
                    cp(lru_tail[:, l, pt, :], xaext[:, T:T + 3], [K(0)], ["lru_tail"])
                else:
                    cp(smp_tail[:, b, pt, :], src[:, so + T:so + T + 3], [K(4)], ["smp_tail"])
            ncols = TC + (NST if p == 0 else 0)
            cp(xcb[:, 0:ncols], xc[:, 0:ncols], [K(1)], ["bs0"], eng="act")
            wa, wak = wload(w["lru_wbd"][l, 0, pt], 128, 1, 128)
            wx, wxk = wload(w["lru_wbd"][l, 1, pt], 128, 1, 128)
            for (c0, n) in cts:
                pi = pring.get()
                mm(psm[pi][:, 0:n], wa[:, 0, :], xcb[:, c0:c0 + n], True, True, [wak, "bs0"], [P(pi)])
                act(r_[:, c0:c0 + n], psm[pi][:, 0:n], AF.Sigmoid, [P(pi), "ptab"], [K(2)], bias=pcol(l, 42 + pt))
                pj = pring.get()
                mm(psm[pj][:, 0:n], wx[:, 0, :], xcb[:, c0:c0 + n], True, True, [wxk, "bs0"], [P(pj)])
                act(i_[:, c0:c0 + n], psm[pj][:, 0:n], AF.Sigmoid, [P(pj), "ptab"], [K(3)], bias=pcol(l, 44 + pt))
            A = slice(0, ncols)
            act(a_[:, A], r_[:, A], AF.Exp, [K(2), "dpar"], [K(4)], scale=dpar[:, l, pt:pt + 1])
            y_ = u_
            ts(y_[:, A], r_[:, A], dpar[:, l, 2 + pt:3 + pt], None, ALU.mult, ALU.bypass, [K(2), "dpar"], [K(5)])
            pol = r_
            ts(pol[:, A], y_[:, A], 1.0 / 720.0, 1.0 / 120.0, ALU.mult, ALU.add, [K(5)], [K(2)])
            for coef in (1.0 / 24.0, 1.0 / 6.0, 0.5, 1.0):
                tt(pol[:, A], pol[:, A], y_[:, A], ALU.mult, [K(2), K(5)], [K(2)])
                ts(pol[:, A], pol[:, A], coef, None, ALU.add, ALU.bypass, [K(2)], [K(2)])
            stt(pol[:, A], pol[:, A], -1.0, y_[:, A], ALU.mult, ALU.mult, [K(2), K(5)], [K(2)])
            ts(pol[:, A], pol[:, A], 0.0, None, ALU.max, ALU.bypass, [K(2)], [K(2)])
            act(pol[:, A], pol[:, A], AF.Sqrt, [K(2)], [K(2)])
            tt(u_[:, A], i_[:, A], xc[:, A], ALU.mult, [K(3), K(1)], [K(5)])
            tt(u_[:, A], u_[:, A], pol[:, A], ALU.mult, [K(5), K(2)], [K(5)])
            for (kind, c0, T, b) in segs:
                init = lru_h[:, l, pt:pt + 1] if kind == "p" else smp_h[:, b, pt:pt + 1]
                ik = "lru_h" if kind == "p" else "smp_h"
                S.op("dve", lambda E, c0=c0, T=T, init=init: E.tensor_tensor_scan(
                    out=hh[:, c0:c0 + T], data0=a_[:, c0:c0 + T], data1=u_[:, c0:c0 + T], initial=init,
                    op0=ALU.mult, op1=ALU.add), [K(4), K(5), ik], [K(6)])
                cp(init, hh[:, c0 + T - 1:c0 + T], [K(6)], [ik])
            g2 = i_
            tt(g2[:, A], ga_[:, A], ga_[:, A], ALU.mult, [K(7)], [K(3)])
            ts(g2[:, A], g2[:, A], 0.044715, 1.0, ALU.mult, ALU.add, [K(3)], [K(3)])
            tt(g2[:, A], g2[:, A], ga_[:, A], ALU.mult, [K(3), K(7)], [K(3)])
            act(g2[:, A], g2[:, A], AF.Sigmoid, [K(3)], [K(3)], scale=1.5957691216057308)
            tt(g2[:, A], g2[:, A], ga_[:, A], ALU.mult, [K(3), K(7)], [K(3)])
            tt(tmp[:, A], hh[:, A], g2[:, A], ALU.mult, [K(6), K(3)], [K(8 + pt)])
        for (c0, n) in cts:
            pi = pring.get()
            for pt in range(2):
                bi = bring.get()
                act(b512[bi][:, 0:n], ypre[pt][:, c0:c0 + n], AF.Square, ["fs%d" % (8 + pt)], ["b512_%d" % bi])
                mm(psm[pi][:, 0:n], ones_b[:], b512[bi][:, 0:n], pt == 0, pt == 1, ["ones_b", "b512_%d" % bi], [P(pi)])
            ti = tring.get()
            rs = t512[ti][:, 0:n]
            ts(rs, psm[pi][:, 0:n], 1.0 / 256.0, EPS, ALU.mult, ALU.add, [P(pi)], ["t512_%d" % ti])
            act(rs, rs, AF.Sqrt, ["t512_%d" % ti], ["t512_%d" % ti])
            S.op("dve", lambda E, rs=rs: E.reciprocal(out=rs, in_=rs), ["t512_%d" % ti], ["t512_%d" % ti])
            for pt in range(2):
                stt(yA[:, pt, c0:c0 + n], ypre[pt][:, c0:c0 + n], pcol(l, 48 + pt), rs, ALU.mult, ALU.mult,
                    ["fs%d" % (8 + pt), "ptab", "t512_%d" % ti], ["ybuf"])
        proj_fm(w["w_out"][l][0:256, :], 128, 2, D, 128, lambda c, c0, n: yA[:, c, c0:c0 + n], ["ybuf"], cts,
                resid_add(1.0))
        if p == NPASS - 1:
            finals.append(S.dma("sp", o_h[l], lru_h[:, l, :], reads=["lru_h"]))
            finals.append(S.dma("sp", o_conv[l], lru_tail[:, l, :, :], reads=["lru_tail"]))

    def mixer_b(l, p):
        cts = coltiles(p)
        win = w["w_in"][l]
        yB = ybuf[0:64, 0:4 * NTX].rearrange("p (c n) -> p c n", c=4)
        ncols = TC + (NST if p == 0 else 0)
        A = slice(0, ncols)
        K = lambda i: "fs%d" % i
        segs = [("p", 0, TC, None, 64)]
        if p == 0:
            segs += [("s", TC + DS * b, DS, b, DS) for b in range(NSB)]
        for hd in range(4):
            q_, f_, cum, ec, en, k_, g_, oT = FS[0], FS[1], FS[2], FS[3], FS[4], FS[5], FS[6], FS[7]
            qt, kt, khat = BS[0], BS[1], BS[2]
            vtm = BS[3]
            khtm = BS[4]

            def fm_proj(col0, dst, dk, func=None, bias=None, scale=1.0):
                wv, wk = wload(win[:, col0 + hd * 64: col0 + (hd + 1) * 64], 128, 8, 64)
                for (c0, n) in cts:
                    pi = pring.get()
                    for c in range(8):
                        mm(psm[pi][0:64, 0:n], wv[:, c, :], h[:, c, c0:c0 + n], c == 0, c == 7, [wk, "h"], [P(pi)])
                    if func is None:
                        cp(dst[0:64, c0:c0 + n], psm[pi][0:64, 0:n], [P(pi)], [dk], eng="act")
                    else:
                        act(dst[0:64, c0:c0 + n], psm[pi][0:64, 0:n], func, [P(pi)], [dk])
            fm_proj(512, q_, K(0))
            fm_proj(768, f_, K(1), func=AF.Sigmoid)
            fm_proj(1280, g_, K(6))
            ts(f_[0:64, A], f_[0:64, A], dpar[0:64, l, 8 + hd:9 + hd], dpar[0:64, l, 4 + hd:5 + hd], ALU.mult, ALU.add,
               [K(1), "dpar"], [K(1)])
            ts(k_[0:64, A], f_[0:64, A], -1.0, 1.0, ALU.mult, ALU.add, [K(1)], [K(5)])
            act(f_[0:64, A], f_[0:64, A], AF.Ln, [K(1)], [K(1)])
            S.op("dve", lambda E: E.tensor_tensor_scan(out=cum[0:64, A], data0=rmask[0:64, A], data1=f_[0:64, A],
                                                       initial=0.0, op0=ALU.mult, op1=ALU.add),
                 ["rmask", K(1)], [K(2)])
            act(ec[0:64, A], cum[0:64, A], AF.Exp, [K(2)], [K(3)])
            act(en[0:64, A], cum[0:64, A], AF.Exp, [K(2)], [K(4)], scale=-1.0)
            tt(qt[0:64, A], q_[0:64, A], ec[0:64, A], ALU.mult, [K(0), K(3)], ["bs0"])
            tt(k_[0:64, A], k_[0:64, A], en[0:64, A], ALU.mult, [K(5), K(4)], [K(5)])
            cp(kt[0:64, A], k_[0:64, A], [K(5)], ["bs1"], eng="act")
            wv, wvk = wload(win[:, 1024 + hd * 64: 1024 + (hd + 1) * 64], 128, 8, 64)
            chunks = []
            for (kind, c0, T, b, C) in segs:
                for j in range(T // C):
                    chunks.append((kind, c0 + j * C, C, b, j == T // C - 1))
            for gi in range(0, len(chunks), 8):
                grp = chunks[gi:gi + 8]
                pi = pring.get()
                for jj, (kind, cc, C, b, last) in enumerate(grp):
                    for c in range(8):
                        mm(psm[pi][0:C, jj * 64:(jj + 1) * 64], h[:, c, cc:cc + C], wv[:, c, :], c == 0, c == 7,
                           ["h", wvk], [P(pi)])
                Cg = grp[0][2]
                cp(vtm[0:Cg, gi * 64:(gi + len(grp)) * 64], psm[pi][0:Cg, 0:len(grp) * 64], [P(pi)], ["bs3"], eng="act")
            for ci, (kind, cc, C, b, last) in enumerate(chunks):
                ts(khat[0:64, cc:cc + C], k_[0:64, cc:cc + C], ec[0:64, cc + C - 1:cc + C], None, ALU.mult, ALU.bypass,
                   [K(5), K(3)], ["bs2"])
            for gi in range(0, len(chunks), 8):
                grp = chunks[gi:gi + 8]
                ti = ptring.get()
                for jj, (kind, cc, C, b, last) in enumerate(grp):
                    tr(pst[ti][0:C, jj * 64:(jj + 1) * 64], khat[0:64, cc:cc + C], ident_b[0:64, 0:64],
                       ["bs2", "ident_b"], ["pst%d" % ti])
                Cg = grp[0][2]
                cp(khtm[0:Cg, gi * 64:(gi + len(grp)) * 64], pst[ti][0:Cg, 0:len(grp) * 64], ["pst%d" % ti], ["bs4"])
            for gi in range(0, len(chunks), 8):
                grp = chunks[gi:gi + 8]
                Cg = grp[0][2]
                ng = len(grp)
                pds = pring.get()
                for jj in range(ng):
                    ci = gi + jj
                    mm(psm[pds][0:64, jj * 64:(jj + 1) * 64], khtm[0:Cg, ci * 64:(ci + 1) * 64],
                       vtm[0:Cg, ci * 64:(ci + 1) * 64], True, True, ["bs4", "bs3"], [P(pds)])
                pin = pring.get()
                for jj, (kind, cc, C, b, last) in enumerate(grp):
                    mm(psm[pin][0:C, jj * 64: jj * 64 + C], kt[0:64, cc:cc + C], qt[0:64, cc:cc + C], True, True,
                       ["bs1", "bs0"], [P(pin)])
                bi = bring.get()
                AT = b512[bi]
                if Cg == 64:
                    tt(AT[0:64, 0:ng * 64], psm[pin][0:64, 0:ng * 64], triu[0:64, 0:ng * 64], ALU.mult,
                       [P(pin), "triu"], ["b512_%d" % bi])
                else:
                    for jj in range(ng):
                        tt(AT[0:Cg, jj * 64: jj * 64 + Cg], psm[pin][0:Cg, jj * 64: jj * 64 + Cg], triu[0:Cg, 0:Cg],
                           ALU.mult, [P(pin), "triu"], ["b512_%d" % bi])
                for jj, (kind, cc, C, b, last) in enumerate(grp):
                    if kind == "p":
                        Sst = hgS[:, l, hd, :]
                        sk = "hgS"
                    else:
                        Sst = smp_S[:, b, hd, :]
                        sk = "smp_S"
                    cp(hgSb[:, jj, :], Sst, [sk], ["hgSb"], eng="act")
                    stt(Sst, Sst, ec[0:64, cc + C - 1:cc + C], psm[pds][0:64, jj * 64:(jj + 1) * 64], ALU.mult, ALU.add,
                        [sk, K(3), P(pds)], [sk])
                po = pring.get()
                for jj, (kind, cc, C, b, last) in enumerate(grp):
                    ci = gi + jj
                    mm(psm[po][0:64, jj * 64: jj * 64 + C], vtm[0:C, ci * 64:(ci + 1) * 64], AT[0:C, jj * 64: jj * 64 + C],
                       True, False, ["bs3", "b512_%d" % bi], [P(po)])
                    mm(psm[po][0:64, jj * 64: jj * 64 + C], hgSb[:, jj, :], qt[0:64, cc:cc + C], False, True,
                       ["hgSb", "bs0"], [P(po)])
                if Cg == 64:
                    cc0 = grp[0][1]
                    cp(oT[0:64, cc0:cc0 + ng * 64], psm[po][0:64, 0:ng * 64], [P(po)], [K(7)], eng="act")
                else:
                    for jj, (kind, cc, C, b, last) in enumerate(grp):
                        cp(oT[0:64, cc:cc + C], psm[po][0:64, jj * 64: jj * 64 + C], [P(po)], [K(7)], eng="act")
            for (c0, n) in cts:
                bi = bring.get()
                act(b512[bi][0:64, 0:n], oT[0:64, c0:c0 + n], AF.Square, [K(7)], ["b512_%d" % bi])
                pi = pring.get()
                mm(psm[pi][0:64, 0:n], ones_b[0:64, 0:64], b512[bi][0:64, 0:n], True, True, ["ones_b", "b512_%d" % bi],
                   [P(pi)])
                ti = tring.get()
                rs = t512[ti][0:64, 0:n]
                ts(rs, psm[pi][0:64, 0:n], 1.0 / 64.0, EPS, ALU.mult, ALU.add, [P(pi)], ["t512_%d" % ti])
                act(rs, rs, AF.Sqrt, ["t512_%d" % ti], ["t512_%d" % ti])
                S.op("dve", lambda E, rs=rs: E.reciprocal(out=rs, in_=rs), ["t512_%d" % ti], ["t512_%d" % ti])
                stt(oT[0:64, c0:c0 + n], oT[0:64, c0:c0 + n], ptab[0:64, l * PL + 50: l * PL + 51], rs, ALU.mult, ALU.mult,
                    [K(7), "ptab", "t512_%d" % ti], [K(7)])
                tj = tring.get()
                act(t512[tj][0:64, 0:n], g_[0:64, c0:c0 + n], AF.Silu, [K(6)], ["t512_%d" % tj])
                tt(yB[:, hd, c0:c0 + n], oT[0:64, c0:c0 + n], t512[tj][0:64, 0:n], ALU.mult, [K(7), "t512_%d" % tj],
                   ["ybuf"])
        proj_fm(w["w_out"][l][256:512, :], 64, 4, D, 128, lambda c, c0, n: yB[:, c, c0:c0 + n], ["ybuf"], cts,
                resid_add(1.0))
        if p == NPASS - 1:
            finals.append(S.dma("sp", o_hg[l], hgS[:, l, :, :], reads=["hgS"]))

    def attn_tile(qT_ap, qk, n, kbs, yout, accw):
        nk = len(kbs)
        LOOK = 2
        sc = [None] * nk

        def emit_qk(i):
            ka, kk, va, vk, wa = kbs[i]
            pi = pring.get()
            mm(psm[pi][:, 0:n], ka, qT_ap, True, True, [kk, qk], [P(pi)])
            bi = bring.get()
            pt_ = b512[bi][:, 0:n]
            act(pt_, psm[pi][:, 0:n], AF.Exp, [P(pi)], ["b512_%d" % bi], scale=0.125)
            tt(pt_, pt_, wa, ALU.mult, ["b512_%d" % bi, "wtb"], ["b512_%d" % bi])
            sc[i] = (pt_, bi)
        for i in range(min(LOOK, nk)):
            emit_qk(i)
        for i in range(nk):
            if i + LOOK < nk:
                emit_qk(i + LOOK)
            ka, kk, va, vk, wa = kbs[i]
            pt_, bi = sc[i]
            mm(psacc[0][0:64, 0:n], va, pt_, i == 0, i == nk - 1, [vk, "b512_%d" % bi], ["psacc0"])
            mm(psacc[1][0:64, 0:n], ones_b[:, 0:64], pt_, i == 0, i == nk - 1, ["ones_b", "b512_%d" % bi], ["psacc1"])
        ti = tring.get()
        rd = t512[ti][0:64, 0:n]
        S.op("dve", lambda E: E.reciprocal(out=rd, in_=psacc[1][0:64, 0:n]), ["psacc1"], ["t512_%d" % ti])
        tt(yout, psacc[0][0:64, 0:n], rd, ALU.mult, ["psacc0", "t512_%d" % ti], [accw])

    def mixer_c(l, p):
        cts = coltiles(p)
        pcts = [(c0, n) for (c0, n) in cts if c0 < TC]
        win = w["w_in"][l]
        t0 = p * TC
        hist = min(W, t0)
        hb = hist // 128
        yC = FS
        yCb = ybuf[0:64, 0:8 * NTX].rearrange("p (c n) -> p c n", c=8)
        emit_kv = (t0 + TC > SEQ - KEEP)
        if emit_kv or p == 0:
            wvp = [wload(win[:, 2560 + q * 256: 2560 + (q + 1) * 256], 128, 8, 256) for q in range(2)]
            wkp = [wload(win[:, 2048 + q * 256: 2048 + (q + 1) * 256], 128, 8, 256) for q in range(2)]

        def kv_tok(pi, lo, hi, pieces, rows):
            for q, (wv_, wk_) in enumerate(pieces):
                for c in range(8):
                    mm(psm[pi][0:rows, q * 256:(q + 1) * 256], h[:, c, lo:hi], wv_[:, c, :], c == 0, c == 7, ["h", wk_],
                       [P(pi)])
        if emit_kv:
            for b in range(NBLK):
                row = t0 + b * 128 - (SEQ - KEEP)
                xi = xinring.get()
                pi = pring.get()
                kv_tok(pi, b * 128, (b + 1) * 128, wvp, 128)
                cp(xin[xi][:, 0:512], psm[pi][:, :], [P(pi)], ["xin%d" % xi])
                pj = pring.get()
                kv_tok(pj, b * 128, (b + 1) * 128, wkp, 128)
                cp(xin[xi][:, 512:1024], psm[pj][:, :], [P(pj)], ["xin%d" % xi], eng="act")
                finals.append(S.dma("sp", o_v[l][row:row + 128, :], xin[xi][:, 0:512], reads=["xin%d" % xi]))
                finals.append(S.dma("sp", o_k[l][row:row + 128, :], xin[xi][:, 512:1024], reads=["xin%d" % xi]))
        if p == 0:
            skv = FS[9]
            pi = pring.get()
            kv_tok(pi, TC, TC + NST, wkp, NST)
            cp(skv[0:NST, 0:512], psm[pi][0:NST, :], [P(pi)], ["fs9"])
            pj = pring.get()
            kv_tok(pj, TC, TC + NST, wvp, NST)
            cp(skv[0:NST, 512:1024], psm[pj][0:NST, :], [P(pj)], ["fs9"], eng="act")
            for b in range(NSB):
                finals.append(S.dma("sp", s_k[l, b, W - DS:W, :], skv[b * DS:(b + 1) * DS, 0:512], reads=["fs9"]))
                finals.append(S.dma("sp", s_v[l, b, W - DS:W, :], skv[b * DS:(b + 1) * DS, 512:1024], reads=["fs9"]))
                finals.append(S.dma("act", s_k[l, b, 0:W - DS, :], c_k[l, b, DS:W, :]))
                finals.append(S.dma("act", s_v[l, b, 0:W - DS, :], c_v[l, b, DS:W, :]))
        for hd in range(8):
            qT = BS[0]
            S.dma("sp", wtb[:], wt_d[hd], writes=["wtb"])
            wq, wqk = wload(win[:, 1536 + hd * 64: 1536 + (hd + 1) * 64], 128, 8, 64)
            wkh, wkhk = wload(win[:, 2048 + hd * 64: 2048 + (hd + 1) * 64], 128, 8, 64)
            wvh, wvhk = wload(win[:, 2560 + hd * 64: 2560 + (hd + 1) * 64], 128, 8, 64)
            if hb > 0:
                S.dma("sp", kT[:, W - hist:W], kT_scr[l][hd * 64:(hd + 1) * 64, t0 - hist:t0],
                      reads=["kT_scr%d" % l], writes=["kT"])
                S.dma("sp", vh[:, WB - hb:WB, :], v_scr[l][hd, t0 - hist:t0, :].rearrange("(b p) d -> p b d", p=128),
                      reads=["v_scr%d" % l], writes=["vh"])
            for b0 in range(0, NBLK, 8):
                pi = pring.get()
                for bb in range(8):
                    b = b0 + bb
                    for c in range(8):
                        mm(psm[pi][:, bb * 64:(bb + 1) * 64], h[:, c, b * 128:(b + 1) * 128], wvh[:, c, :], c == 0, c == 7,
                           ["h", wvhk], [P(pi)])
                cp(vh[:, WB + b0:WB + b0 + 8, :], psm[pi][:].rearrange("p (b d) -> p b d", b=8), [P(pi)], ["vh"], eng="act")
            S.dma("sp", v_scr[l][hd, t0:t0 + TC, :].rearrange("(b p) d -> p b d", p=128), vh[:, WB:WB + NBLK, :],
                  reads=["vh"], writes=["v_scr%d" % l])
            for (c0, n) in pcts:
                pi = pring.get()
                for c in range(8):
                    mm(psm[pi][0:64, 0:n], wq[:, c, :], h[:, c, c0:c0 + n], c == 0, c == 7, [wqk, "h"], [P(pi)])
                cp(qT[0:64, c0:c0 + n], psm[pi][0:64, 0:n], [P(pi)], ["bs0"], eng="act")
                pj = pring.get()
                for c in range(8):
                    mm(psm[pj][0:64, 0:n], wkh[:, c, :], h[:, c, c0:c0 + n], c == 0, c == 7, [wkhk, "h"], [P(pj)])
                cp(kT[:, W + c0: W + c0 + n], psm[pj][0:64, 0:n], [P(pj)], ["kT"])
            S.dma("sp", kT_scr[l][hd * 64:(hd + 1) * 64, t0:t0 + TC], kT[:, W:W + TC], reads=["kT"],
                  writes=["kT_scr%d" % l])
            for (c0, n) in pcts:
                qb0 = c0 // 128
                kbs = []
                for kb in range(qb0 - WB, qb0 + 4):
                    if kb < -hb:
                        continue
                    col = W + kb * 128
                    dbi = qb0 - kb + 3
                    kbs.append((kT[:, col:col + 128], "kT", vh[:, WB + kb, :], "vh",
                                wtb[:, dbi * 128:(dbi + 4) * 128]))
                attn_tile(qT[0:64, c0:c0 + n], "bs0", n, kbs, yC[hd][0:64, c0:c0 + n], "fs%d" % hd)
        if p == 0:
            for b in range(NSB):
                cs = TC + b * DS
                for hd in range(8):
                    S.dma("sp", wtb[:], wt_d[hd], writes=["wtb"])
                    wq, wqk = wload(win[:, 1536 + hd * 64: 1536 + (hd + 1) * 64], 128, 8, 64)
                    wkh, wkhk = wload(win[:, 2048 + hd * 64: 2048 + (hd + 1) * 64], 128, 8, 64)
                    wvh, wvhk = wload(win[:, 2560 + hd * 64: 2560 + (hd + 1) * 64], 128, 8, 64)
                    S.dma("pool", vh[:, 0:WB, :],
                          c_v[l, b].rearrange("(k p) n -> p k n", p=128)[:, :, hd * 64:(hd + 1) * 64], writes=["vh"])
                    pv = pring.get()
                    for c in range(8):
                        mm(psm[pv][0:DS, 0:64], h[:, c, cs:cs + DS], wvh[:, c, :], c == 0, c == 7, ["h", wvhk], [P(pv)])
                    cp(vh[0:DS, WB, :], psm[pv][0:DS, 0:64], [P(pv)], ["vh"])
                    kch = BS[2]
                    S.dma("pool", kch[:, 0:WB * 64].rearrange("p (k d) -> p k d", k=WB),
                          c_k[l, b].rearrange("(k p) n -> p k n", p=128)[:, :, hd * 64:(hd + 1) * 64], writes=["bs2"])
                    for g in range(WB // 8):
                        ti = ptring.get()
                        for jj in range(8):
                            kb = g * 8 + jj
                            tr(pst[ti][0:64, jj * 128:(jj + 1) * 128], kch[:, kb * 64:(kb + 1) * 64], ident_b[:],
                               ["bs2", "ident_b"], ["pst%d" % ti])
                        cp(kT[:, g * 1024:(g + 1) * 1024], pst[ti][0:64, :], ["pst%d" % ti], ["kT"],
                           eng="act" if g % 2 else "dve")
                    pi = pring.get()
                    for c in range(8):
                        mm(psm[pi][0:64, 0:DS], wkh[:, c, :], h[:, c, cs:cs + DS], c == 0, c == 7, [wkhk, "h"], [P(pi)])
                    S.op("dve", lambda E: E.memset(kT[:, W:W + 128], 0.0), [], ["kT"])
                    cp(kT[:, W:W + DS], psm[pi][0:64, 0:DS], [P(pi)], ["kT"])
                    pj = pring.get()
                    for c in range(8):
                        mm(psm[pj][0:64, 0:DS], wq[:, c, :], h[:, c, cs:cs + DS], c == 0, c == 7, [wqk, "h"], [P(pj)])
                    qs = BS[0]
                    cp(qs[0:64, 0:DS], psm[pj][0:64, 0:DS], [P(pj)], ["bs0"], eng="act")
                    kbs = []
                    for kb in range(WB + 1):
                        dbi = (WB - kb) + 3
                        kbs.append((kT[:, kb * 128:(kb + 1) * 128], "kT", vh[:, kb, :], "vh",
                                    wtb[:, dbi * 128: dbi * 128 + DS]))
                    attn_tile(qs[0:64, 0:DS], "bs0", DS, kbs, yC[hd][0:64, cs:cs + DS], "fs%d" % hd)
        for (c0, n) in cts:
            pi = pring.get()
            for hd in range(8):
                bi = bring.get()
                act(b512[bi][0:64, 0:n], yC[hd][0:64, c0:c0 + n], AF.Square, ["fs%d" % hd], ["b512_%d" % bi])
                mm(psm[pi][0:64, 0:n], ones_b[0:64, 0:64], b512[bi][0:64, 0:n], hd == 0, hd == 7,
                   ["ones_b", "b512_%d" % bi], [P(pi)])
            ti = tring.get()
            rs = t512[ti][0:64, 0:n]
            ts(rs, psm[pi][0:64, 0:n], 1.0 / 512.0, EPS, ALU.mult, ALU.add, [P(pi)], ["t512_%d" % ti])
            act(rs, rs, AF.Sqrt, ["t512_%d" % ti], ["t512_%d" % ti])
            S.op("dve", lambda E, rs=rs: E.reciprocal(out=rs, in_=rs), ["t512_%d" % ti], ["t512_%d" % ti])
            for hd in range(8):
                stt(yCb[:, hd, c0:c0 + n], yC[hd][0:64, c0:c0 + n], ptab[0:64, l * PL + 51 + hd: l * PL + 52 + hd], rs,
                    ALU.mult, ALU.mult, ["fs%d" % hd, "ptab", "t512_%d" % ti], ["ybuf"])
        proj_fm(w["w_out"][l][512:1024, :], 64, 8, D, 128, lambda c, c0, n: yCb[:, c, c0:c0 + n], ["ybuf"], cts,
                resid_add(1.0))

    def load_memT():
        for mb in range(NMEM // 128):
            xi = xinring.get()
            S.dma("sp", xin[xi][:], memp[mb * 128:(mb + 1) * 128, :], writes=["xin%d" % xi])
            for c4 in range(2):
                pi = pring.get()
                for cc in range(4):
                    c = c4 * 4 + cc
                    tr(psm[pi][:, cc * 128:(cc + 1) * 128], xin[xi][:, c * 128:(c + 1) * 128], ident_f[:],
                       ["xin%d" % xi, "ident_f"], [P(pi)])
                cp(memT[:, c4 * 4:(c4 + 1) * 4, mb * 128:(mb + 1) * 128], psm[pi][:].rearrange("p (c n) -> p c n", c=4),
                   [P(pi)], [KMT])

    def prep_mem(l, p):
        if p == 0:
            load_memT()
            for nm, osrc in (("x_wk", o_mk), ("x_wv", o_mv)):
                for q4 in range(4):
                    wv_, wk_ = wload(w[nm][l][:, q4 * 256:(q4 + 1) * 256], 128, 8, 256)
                    for mb in range(NMEM // 128):
                        pi = pring.get()
                        for c in range(8):
                            mm(psm[pi][:, 0:256], memT[:, c, mb * 128:(mb + 1) * 128], wv_[:, c, :], c == 0, c == 7,
                               [KMT, wk_], [P(pi)])
                        xi = xinring.get()
                        cp(xin[xi][:, 0:256], psm[pi][:, 0:256], [P(pi)], ["xin%d" % xi])
                        finals.append(S.dma("sp", osrc[l][mb * 128:(mb + 1) * 128, q4 * 256:(q4 + 1) * 256],
                                            xin[xi][:, 0:256], reads=["xin%d" % xi]))
                        if nm == "x_wv":
                            cp(mv[:, mb, q4 * 256:(q4 + 1) * 256], xin[xi][:, 0:256], ["xin%d" % xi], [KMV], eng="act")
                    if nm == "x_wk":
                        for mi in range(2):
                            pi = pring.get()
                            for c in range(8):
                                mm(psm[pi][:, 0:NMEM], wv_[:, c, mi * 128:(mi + 1) * 128], memT[:, c, :], c == 0, c == 7,
                                   [wk_, KMT], [P(pi)])
                            cp(mkT[:, q4 * 2 + mi, :], psm[pi][:, 0:NMEM], [P(pi)], [KMK], eng="act")
            if NPASS > 1 and "p_scr" not in cfg.get("xskip", ()):
                S.dma("sp", mkT_scr[l], mkT_flat, reads=[KMK], writes=["mkT_scr%d" % l])
                S.dma("sp", mv_scr[l], mv_flat, reads=[KMV], writes=["mv_scr%d" % l])
        elif "p_scr" not in cfg.get("xskip", ()):
            S.dma("sp", mkT_flat, mkT_scr[l], reads=["mkT_scr%d" % l], writes=[KMK])
            S.dma("sp", mv_flat, mv_scr[l], reads=["mv_scr%d" % l], writes=[KMV])

    def cross_core(l, c0, n, qoff):
        oX = ybuf[:, 0:8 * NTX].rearrange("p (c n) -> p c n", c=8)
        for hx in range(4):
            pts = []
            for mb in range(NMEM // 128):
                pi = pring.get()
                for dc in range(2):
                    mm(psm[pi][:, 0:n], mkT[:, hx * 2 + dc, mb * 128:(mb + 1) * 128],
                       BS[1 + (hx * 2 + dc) // 2][:, ((hx * 2 + dc) % 2) * 512 + qoff:((hx * 2 + dc) % 2) * 512 + qoff + n],
                       dc == 0, dc == 1, [KMK, sbqk(hx * 2 + dc)], [P(pi)])
                bi = bring.get()
                act(b512[bi][:, 0:n], psm[pi][:, 0:n], AF.Exp, [P(pi)], ["b512_%d" % bi], scale=1.0 / 16.0)
                pts.append(bi)
            pdn = pring.get()
            for i, bi in enumerate(pts):
                mm(psm[pdn][:, 0:n], ones_b[:], b512[bi][:, 0:n], i == 0, i == len(pts) - 1, ["ones_b", "b512_%d" % bi],
                   [P(pdn)])
            ti = tring.get()
            rd = t512[ti][:, 0:n]
            S.op("dve", lambda E, rd=rd, pdn=pdn: E.reciprocal(out=rd, in_=psm[pdn][:, 0:n]), [P(pdn)], ["t512_%d" % ti])
            for dc in range(2):
                po = pring.get()
                for i, bi in enumerate(pts):
                    mm(psm[po][:, 0:n], mv[:, i, (hx * 2 + dc) * 128:(hx * 2 + dc + 1) * 128], b512[bi][:, 0:n],
                       i == 0, i == len(pts) - 1, [KMV, "b512_%d" % bi], [P(po)])
                tt(oX[:, hx * 2 + dc, c0:c0 + n], psm[po][:, 0:n], rd, ALU.mult, [P(po), "t512_%d" % ti], ["ybuf"])

    def cross(l, p):
        cts = coltiles(p)
        rmsnorm(l * PL + 16, p, hview, "h", cts)
        if "prep" not in cfg.get("xskip", ()):
            prep_mem(l, p)

        def qsink(mi, c0_, n_, pap, pk):
            cp(sbq(mi, n_), pap, [pk], [sbqk(mi)], eng="act" if mi % 2 else "dve")
        for (c0, n) in cts:
            if c0 >= TC:
                continue
            proj_fm(w["x_wq"][l], 128, 8, D, 128, lambda c, c0_, n_: h[:, c, c0_:c0_ + n_], ["h"], [(c0, n)], qsink)
            if "core" not in cfg.get("xskip", ()):
                cross_core(l, c0, n, 0)
        if p == 0 and "sample" not in cfg.get("xskip", ()):
            proj_fm(w["x_wq"][l], 128, 8, D, 128, lambda c, c0_, n_: h[:, c, c0_:c0_ + n_], ["h"], [(TC, NST)], qsink)
            for b in range(NSB):
                S.dma("pool", mv, cm_v[l, b].rearrange("(k p) n -> p k n", p=128), writes=[KMV])
                kst = BS[0]
                for mb in range(NMEM // 128):
                    S.dma("pool", kst[:, 0:D], cm_k[l, b, mb * 128:(mb + 1) * 128, :], writes=["bs0"])
                    ti = ptring.get()
                    for c in range(8):
                        tr(pst[ti][:, c * 128:(c + 1) * 128], kst[:, c * 128:(c + 1) * 128], ident_b[:],
                           ["bs0", "ident_b"], ["pst%d" % ti])
                    cp(mkT[:, :, mb * 128:(mb + 1) * 128], pst[ti][:].rearrange("p (c n) -> p c n", c=8),
                       ["pst%d" % ti], [KMK])
                cs = TC + b * DS
                cross_core(l, cs, DS, b * DS)
        proj_fm(w["x_wo"][l], 128, 8, D, 128,
                lambda c, c0, n: ybuf[:, 0:8 * NTX].rearrange("p (c n) -> p c n", c=8)[:, c, c0:c0 + n], ["ybuf"], cts,
                resid_add(1.0))

    stages = cfg.get("stages", ("ffn1", "a", "b", "c", "x", "ffn2"))
    for p in range(NPASS):
        load_x(p)
        for l in range(DEPTH):
            if p == 0:
                for b in range(NSB):
                    S.dma("sp", smp_h[:, b, :], st_h[l, b], writes=["smp_h"])
                    S.dma("sp", smp_tail[:, b, :, :], st_conv[l, b], writes=["smp_tail"])
                    S.dma("sp", smp_S[:, b, :, :], st_hg[l, b], writes=["smp_S"])
            if "ffn1" in stages:
                ffn(l, 1, p)
            rmsnorm(l * PL + 8, p, hview, "h", coltiles(p))
            if "a" in stages:
                mixer_a(l, p)
            if "b" in stages:
                mixer_b(l, p)
            if "c" in stages:
                mixer_c(l, p)
            if "x" in stages:
                cross(l, p)
            if "ffn2" in stages:
                ffn(l, 2, p)
            if p == 0:
                for b in range(NSB):
                    finals.append(S.dma("sp", s_h[l, b], smp_h[:, b, :], reads=["smp_h"]))
                    finals.append(S.dma("sp", s_conv[l, b], smp_tail[:, b, :, :], reads=["smp_tail"]))
                    finals.append(S.dma("sp", s_hg[l, b], smp_S[:, b, :, :], reads=["smp_S"]))
        store_y(p)
    S.finish(finals)
    st.close()
    print("[kernel] instructions=%d waits=%d sems=%d" % (S.n_ins, S.n_wait, S.nsem), flush=True)
    return nc


def host_consts(cfg):
    TC, NSB = cfg["TC"], cfg["NSB"]
    NTX = TC + NSB * DS
    triu = np.tile(np.triu(np.ones((64, 64), np.float32)), (1, 8))
    rmask = np.ones((64, NTX), np.float32)
    rmask[:, 0:TC:64] = 0.0
    rmask[:, TC::DS] = 0.0
    return {"wt_tab": wt_table().reshape(8, 128, 23 * 128).astype(ml_dtypes.bfloat16), "ident": np.eye(128, dtype=np.float32),
            "triu": triu, "rmask": rmask}


def pack_ptab(inp, DEPTH):
    NPT = DEPTH * PL + 8
    t = np.zeros((128, NPT), np.float32)

    def fm8(v):
        return np.ascontiguousarray(v.reshape(8, 128).T)

    def fm2(v):
        return np.ascontiguousarray(v.reshape(2, 128).T)
    for l in range(DEPTH):
        b = l * PL
        t[:, b + 0:b + 8] = fm8(inp["n_ffn1"][l])
        t[:, b + 8:b + 16] = fm8(inp["n_mix"][l])
        t[:, b + 16:b + 24] = fm8(inp["n_cross"][l])
        t[:, b + 24:b + 32] = fm8(inp["n_ffn2"][l])
        cw = inp["lru_conv_w"][l]
        for pt in range(2):
            for tap in range(4):
                t[:, b + 32 + pt * 4 + tap] = cw[tap, pt * 128:(pt + 1) * 128]
        t[:, b + 40:b + 42] = fm2(inp["lru_conv_b"][l])
        t[:, b + 42:b + 44] = fm2(inp["lru_ba"][l])
        t[:, b + 44:b + 46] = fm2(inp["lru_bx"][l])
        t[:, b + 46:b + 48] = fm2(inp["lru_lambda"][l])
        t[:, b + 48:b + 50] = fm2(inp["gn_a"][l])
        t[0:64, b + 50] = inp["hgrn_norm"][l]
        t[0:64, b + 51:b + 59] = inp["gn_c"][l].reshape(8, 64).T
        t[0:64, b + 59:b + 63] = inp["hgrn_lb"][l].reshape(4, 64).T
    t[:, DEPTH * PL:DEPTH * PL + 8] = fm8(inp["n_final"])
    return t


def block_diag(wb):
    DEPTH = wb.shape[0]
    o = np.zeros((DEPTH, 2, 128, 128), np.float32)
    for pt in range(2):
        for j in range(2):
            o[:, pt, j * 64:(j + 1) * 64, j * 64:(j + 1) * 64] = wb[:, pt * 2 + j]
    return o


_NC_CACHE = {}


def run(inp, cfg, n_cores=8):
    DEPTH, NSB, SEQ = cfg["DEPTH"], cfg["NSB"], cfg["SEQ"]
    BATCH = inp["x_prompt"].shape[0]
    key = tuple(sorted((k, str(v)) for k, v in cfg.items()))
    if key not in _NC_CACHE:
        _NC_CACHE[key] = build(cfg)
    nc = _NC_CACHE[key]
    consts = host_consts(cfg)
    shared = {k: np.ascontiguousarray(inp[k]) for k in
              ["ffn1_wg", "ffn1_wu", "ffn1_wd", "ffn2_wg", "ffn2_wu", "ffn2_wd", "w_in", "w_out", "x_wq", "x_wk",
               "x_wv", "x_wo"]}
    shared["lru_wbd"] = np.ascontiguousarray(
        np.stack([block_diag(inp["lru_wa"]), block_diag(inp["lru_wx"])], axis=1))
    shared["ptab"] = pack_ptab(inp, DEPTH)
    shared.update(consts)
    in_maps = []
    for c in range(n_cores):
        m = dict(shared)
        if c < BATCH:
            m["xp"] = np.ascontiguousarray(inp["x_prompt"][c])
            m["memp"] = np.ascontiguousarray(inp["mem_prompt"][c])
        else:
            m["xp"] = np.zeros((SEQ, D), np.float32)
            m["memp"] = np.zeros((NMEM, D), np.float32)
        bs = slice(c * NSB, (c + 1) * NSB)
        m["xs"] = np.ascontiguousarray(inp["x_sample"][bs].reshape(NSB * DS, D))
        m["st_h"] = np.ascontiguousarray(inp["state_lru_h"][:, bs].reshape(DEPTH, NSB, 2, 128).transpose(0, 1, 3, 2))
        m["st_conv"] = np.ascontiguousarray(
            inp["state_lru_conv"][:, bs].reshape(DEPTH, NSB, 3, 2, 128).transpose(0, 1, 4, 3, 2))
        m["st_hg"] = np.ascontiguousarray(inp["state_hgrn"][:, bs].transpose(0, 1, 3, 2, 4))
        m["c_k"] = np.ascontiguousarray(inp["cache_swa_k"][:, bs].reshape(DEPTH, NSB, W, 512))
        m["c_v"] = np.ascontiguousarray(inp["cache_swa_v"][:, bs].reshape(DEPTH, NSB, W, 512))
        m["cm_k"] = np.ascontiguousarray(inp["cache_mem_k"][:, bs].reshape(DEPTH, NSB, NMEM, D))
        m["cm_v"] = np.ascontiguousarray(inp["cache_mem_v"][:, bs].reshape(DEPTH, NSB, NMEM, D))
        in_maps.append(m)
    res = run_bass_kernel_spmd(nc, in_maps, core_ids=list(range(n_cores)))
    R = res.results
    KEEP = min(W, SEQ)
    y_prompt = np.stack([R[b]["yp"] for b in range(BATCH)], 0)
    y_sample = np.concatenate([R[c]["ys"].reshape(NSB, DS, D) for c in range(n_cores)], 0)
    p_h = np.stack([R[b]["o_h"].transpose(0, 2, 1).reshape(DEPTH, 256) for b in range(BATCH)], 1)
    p_c = np.stack([R[b]["o_conv"].transpose(0, 3, 2, 1).reshape(DEPTH, 3, 256) for b in range(BATCH)], 1)
    p_s = np.stack([R[b]["o_hg"].transpose(0, 2, 1, 3) for b in range(BATCH)], 1)
    p_k = np.stack([R[b]["o_k"].reshape(DEPTH, KEEP, 8, 64) for b in range(BATCH)], 1)
    p_v = np.stack([R[b]["o_v"].reshape(DEPTH, KEEP, 8, 64) for b in range(BATCH)], 1)
    p_mk = np.stack([R[b]["o_mk"].reshape(DEPTH, NMEM, 4, 256) for b in range(BATCH)], 1)
    p_mv = np.stack([R[b]["o_mv"].reshape(DEPTH, NMEM, 4, 256) for b in range(BATCH)], 1)
    s_h = np.concatenate([R[c]["s_h"].transpose(0, 1, 3, 2).reshape(DEPTH, NSB, 256) for c in range(n_cores)], 1)
    s_c = np.concatenate([R[c]["s_conv"].transpose(0, 1, 4, 3, 2).reshape(DEPTH, NSB, 3, 256) for c in range(n_cores)], 1)
    s_s = np.concatenate([R[c]["s_hg"].transpose(0, 1, 3, 2, 4) for c in range(n_cores)], 1)
    s_k = np.concatenate([R[c]["s_k"].reshape(DEPTH, NSB, W, 8, 64) for c in range(n_cores)], 1)
    s_v = np.concatenate([R[c]["s_v"].reshape(DEPTH, NSB, W, 8, 64) for c in range(n_cores)], 1)
    outs = (y_prompt, y_sample, p_h, p_c, p_s, p_k, p_v, p_mk, p_mv, s_h, s_c, s_s, s_k, s_v)
    return tuple(np.ascontiguousarray(o, dtype=np.float32) for o in outs)


def kernel(**inputs):
    inp = {k: np.asarray(v) for k, v in inputs.items()}
    cfg = {"SEQ": int(inp["x_prompt"].shape[1]), "TC": 1024, "DEPTH": int(inp["w_in"].shape[0]),
           "NSB": int(inp["x_sample"].shape[0]) // 8}
    return run(inp, cfg)
```
